# Optimizing a Trainium2 kernel written in Bass

```python
import jax, jax.numpy as jnp
from jax import lax
import numpy as np

D_MODEL = 1024
BATCH = 2
SEQ = 8192
DEPTH = 1

GRID_W = 64
NA_HEAD_DIM = 64
NA_WIDTH = D_MODEL // 2
NA_HEADS = NA_WIDTH // NA_HEAD_DIM
NA_KH = 8
NA_KW = 16
GLA_HEADS = 4
GLA_VAL_WIDTH = D_MODEL - NA_WIDTH
GLA_DV = GLA_VAL_WIDTH // GLA_HEADS
GLA_DK = GLA_DV // 2
GLA_KEY_WIDTH = GLA_HEADS * GLA_DK
GLA_GATE_RANK = 16
GLA_GATE_NORM = 16.0
GLA_CHUNK = 64
D_MIX = NA_WIDTH + GLA_VAL_WIDTH
D_FF = 4 * D_MODEL
IN_SPLITS = [NA_WIDTH, NA_WIDTH, NA_WIDTH,
             GLA_KEY_WIDTH, GLA_KEY_WIDTH, GLA_VAL_WIDTH, GLA_VAL_WIDTH,
             GLA_GATE_RANK, GLA_GATE_RANK]
D_IN = sum(IN_SPLITS)
EPS = 1e-6

kernel_name = "hybrid_natten_gla_encoder_block"


def rmsnorm(x, g):
    xf = x.astype(jnp.float32)
    y = xf * lax.rsqrt(jnp.mean(xf * xf, axis=-1, keepdims=True) + EPS)
    return (y * g.astype(jnp.float32)).astype(x.dtype)


def neighbourhood_attention(q, k, v, rpb):
    B, T, H, dh = q.shape
    R = T // GRID_W
    W = GRID_W
    KH = min(NA_KH, R)
    KW = NA_KW
    grid = lambda t: t.reshape(B, R, W, H, dh).transpose(0, 3, 1, 2, 4)
    qg, kg, vg = grid(q), grid(k), grid(v)
    rows = jnp.arange(R)
    row_start = jnp.clip(rows - KH // 2, 0, R - KH)
    row_idx = row_start[:, None] + jnp.arange(KH)[None, :]
    k_rows = kg[:, :, row_idx]
    v_rows = vg[:, :, row_idx]
    scores = jnp.einsum('bhrqd,bhrikd->bhrqik', qg, k_rows).astype(jnp.float32)
    scores = scores * (dh ** -0.5)
    dr_idx = row_idx - rows[:, None] + (NA_KH - 1)
    cols = jnp.arange(W)
    dc = cols[None, :] - cols[:, None]
    dc_idx = jnp.clip(dc, -(KW - 1), KW - 1) + (KW - 1)
    bias = rpb.astype(jnp.float32)[:, dr_idx[:, None, :, None], dc_idx[None, :, None, :]]
    col_start = jnp.clip(cols - KW // 2, 0, W - KW)
    in_win = (cols[None, :] >= col_start[:, None]) & (cols[None, :] < col_start[:, None] + KW)
    scores = jnp.where(in_win[None, None, None, :, None, :], scores + bias[None], -jnp.inf)
    p = jax.nn.softmax(scores.reshape(B, H, R, W, KH * W), axis=-1).reshape(B, H, R, W, KH, W)
    out = jnp.einsum('bhrqik,bhrikd->bhrqd', p.astype(v.dtype), v_rows)
    return out.transpose(0, 2, 3, 1, 4).reshape(B, T, H * dh)


def gla_chunked(q, k, v, log_a, strict):
    B, H, T, dk = q.shape
    dv = v.shape[-1]
    C = GLA_CHUNK
    N = T // C
    q = q.reshape(B, H, N, C, dk)
    k = k.reshape(B, H, N, C, dk)
    v = v.reshape(B, H, N, C, dv)
    b = jnp.cumsum(log_a.reshape(B, H, N, C, dk), axis=-2)
    q_dec = q * jnp.exp(b)
    k_inv = k * jnp.exp(-b)
    mask = jnp.tril(jnp.ones((C, C), dtype=bool), k=-1 if strict else 0)
    A = jnp.where(mask, jnp.einsum('bhnid,bhnjd->bhnij', q_dec, k_inv), 0.0)
    o_intra = jnp.einsum('bhnij,bhnje->bhnie', A, v)
    b_last = b[..., -1:, :]
    contrib = jnp.einsum('bhncd,bhnce->bhnde', k * jnp.exp(b_last - b), v)
    decay = jnp.exp(b_last[..., 0, :])

    def step(S, inp):
        g, c = inp
        return g[..., None] * S + c, S

    S0 = jnp.zeros((B, H, dk, dv), dtype=contrib.dtype)
    _, S_prev = lax.scan(step, S0, (jnp.moveaxis(decay, 2, 0), jnp.moveaxis(contrib, 2, 0)))
    S_prev = jnp.moveaxis(S_prev, 0, 2)
    o = o_intra + jnp.einsum('bhncd,bhnde->bhnce', q_dec, S_prev)
    return o.reshape(B, H, T, dv)


def bidirectional_gla(q, k, v, r, z_f, z_b, gu_f, gb_f, gu_b, gb_b, norm_g):
    B, T, _ = q.shape
    bhtd = lambda t, d: t.reshape(B, T, GLA_HEADS, d).transpose(0, 2, 1, 3)
    qh = bhtd(q, GLA_DK) * (GLA_DK ** -0.5)
    kh = bhtd(k, GLA_DK)
    vh = bhtd(v, GLA_DV)
    log_a_f = jax.nn.log_sigmoid((z_f @ gu_f + gb_f).astype(jnp.float32)) / GLA_GATE_NORM
    log_a_b = jax.nn.log_sigmoid((z_b @ gu_b + gb_b).astype(jnp.float32)) / GLA_GATE_NORM
    la_f = bhtd(log_a_f, GLA_DK)
    la_b = bhtd(log_a_b, GLA_DK)
    fwd = gla_chunked(qh, kh, vh, la_f, strict=False)
    flip = lambda t: jnp.flip(t, axis=2)
    bwd = flip(gla_chunked(flip(qh), flip(kh), flip(vh), flip(la_b), strict=True))
    o = (fwd + bwd).astype(v.dtype).transpose(0, 2, 1, 3)
    o = rmsnorm(o, norm_g) * jax.nn.silu(r.reshape(B, T, GLA_HEADS, GLA_DV))
    return o.reshape(B, T, GLA_VAL_WIDTH)


def hybrid_mixer(h, ln_g, w_in, rpb, gu_f, gb_f, gu_b, gb_b, norm_g, w_out):
    B, T, _ = h.shape
    n = rmsnorm(h, ln_g)
    proj = n @ w_in
    offsets = [int(o) for o in np.cumsum(IN_SPLITS)[:-1]]
    qa, ka, va, qg, kg, vg, rg, zf, zb = jnp.split(proj, offsets, axis=-1)
    na_heads = lambda t: t.reshape(B, T, NA_HEADS, NA_HEAD_DIM)
    y_na = neighbourhood_attention(na_heads(qa), na_heads(ka), na_heads(va), rpb)
    y_gla = bidirectional_gla(qg, kg, vg, rg, zf, zb, gu_f, gb_f, gu_b, gb_b, norm_g)
    return jnp.concatenate([y_na, y_gla], axis=-1) @ w_out


def sqrelu_mlp(h, ln_g, w1, w2):
    u = rmsnorm(h, ln_g) @ w1
    return jnp.square(jax.nn.relu(u)) @ w2


def setup_inputs(seed: int = 0) -> dict:
    key = jax.random.key(seed)
    ks = jax.random.split(key, 16)
    nrm = lambda k, shape, s: jax.random.normal(k, shape, jnp.float32) * s
    L = DEPTH
    return {
        "x": nrm(ks[0], (BATCH, SEQ, D_MODEL), 1.0),
        "ln_mix_g": 1.0 + nrm(ks[1], (L, D_MODEL), 0.02),
        "w_in": nrm(ks[2], (L, D_MODEL, D_IN), D_MODEL ** -0.5),
        "na_rpb": nrm(ks[3], (L, NA_HEADS, 2 * NA_KH - 1, 2 * NA_KW - 1), 0.02),
        "gla_gate_up_fwd": nrm(ks[4], (L, GLA_GATE_RANK, GLA_KEY_WIDTH), GLA_GATE_RANK ** -0.5),
        "gla_gate_bias_fwd": nrm(ks[5], (L, GLA_KEY_WIDTH), 0.1),
        "gla_gate_up_bwd": nrm(ks[6], (L, GLA_GATE_RANK, GLA_KEY_WIDTH), GLA_GATE_RANK ** -0.5),
        "gla_gate_bias_bwd": nrm(ks[7], (L, GLA_KEY_WIDTH), 0.1),
        "gla_norm_g": 1.0 + nrm(ks[8], (L, GLA_DV), 0.02),
        "w_out": nrm(ks[9], (L, D_MIX, D_MODEL), D_MIX ** -0.5),
        "ln_ff_g": 1.0 + nrm(ks[10], (L, D_MODEL), 0.02),
        "w_ff1": nrm(ks[11], (L, D_MODEL, D_FF), D_MODEL ** -0.5),
        "w_ff2": nrm(ks[12], (L, D_FF, D_MODEL), D_FF ** -0.5),
        "ln_final_g": 1.0 + nrm(ks[13], (D_MODEL,), 0.02),
    }


def reference(x, ln_mix_g, w_in, na_rpb, gla_gate_up_fwd, gla_gate_bias_fwd,
              gla_gate_up_bwd, gla_gate_bias_bwd, gla_norm_g, w_out,
              ln_ff_g, w_ff1, w_ff2, ln_final_g):
    h = x
    for l in range(DEPTH):
        h = h + hybrid_mixer(h, ln_mix_g[l], w_in[l], na_rpb[l],
                             gla_gate_up_fwd[l], gla_gate_bias_fwd[l],
                             gla_gate_up_bwd[l], gla_gate_bias_bwd[l],
                             gla_norm_g[l], w_out[l])
        h = h + sqrelu_mlp(h, ln_ff_g[l], w_ff1[l], w_ff2[l])
    return rmsnorm(h, ln_final_g)
```

```python
import numpy as np
from contextlib import ExitStack
import concourse.bass as bass
import concourse.mybir as mybir
from concourse.bass_utils import run_bass_kernel_spmd

F32 = mybir.dt.float32
BF16 = mybir.dt.bfloat16
AF = mybir.ActivationFunctionType
ALU = mybir.AluOpType
AX = mybir.AxisListType

NCORES = 8
EPS = 1e-6
PAGE = 128
ENGS = ("sp", "act", "pool", "dve", "pe")
NDSEM = 24


class Reg:
    __slots__ = ("ap", "keys")

    def __init__(self, ap, keys):
        self.ap = ap
        self.keys = keys

    def v(self, fn):
        return Reg(fn(self.ap), self.keys)


class Ten:
    def __init__(self, name, h, ncols, esz):
        self.name = name
        self.h = h
        self.ncols = ncols
        self.esz = esz

    def r(self, c0=0, c1=None, p0=0, p1=128):
        if c1 is None:
            c1 = self.ncols
        b0 = (c0 * self.esz) // PAGE
        b1 = (c1 * self.esz - 1) // PAGE
        keys = tuple((self.name, b) for b in range(b0, b1 + 1))
        return Reg(self.h[p0:p1, c0:c1], keys)


def dkey(name, i=0):
    return Reg(None, ((name, i),))


class Op:
    __slots__ = ("eng", "fn", "deps", "dma", "idx", "eidx", "sig", "tick", "sem", "val", "prev", "kept")


class Sched:
    def __init__(self):
        self.ops = []
        self.last_w = {}
        self.readers = {}

    def op(self, eng, fn, reads=(), writes=(), dma=False):
        o = Op()
        o.eng = eng
        o.fn = fn
        o.dma = dma
        o.idx = len(self.ops)
        o.sig = False
        o.tick = 0
        deps = set()
        lw = self.last_w
        rd = self.readers
        for r in reads:
            for k in r.keys:
                w = lw.get(k)
                if w is not None:
                    deps.add(w)
        for wr in writes:
            for k in wr.keys:
                w = lw.get(k)
                if w is not None:
                    deps.add(w)
                d = rd.get(k)
                if d:
                    deps.update(d.values())
        ekey = ("dma", o.idx) if dma else eng
        for wr in writes:
            for k in wr.keys:
                lw[k] = o.idx
                rd[k] = {}
        for r in reads:
            for k in r.keys:
                d = rd.get(k)
                if d is None:
                    d = rd[k] = {}
                d[ekey] = o.idx
        deps.discard(o.idx)
        o.deps = deps
        self.ops.append(o)
        return o

    def emit(self, block, esem, dsem):
        ops = self.ops
        per = {e: [] for e in ENGS}
        for o in ops:
            o.eidx = len(per[o.eng])
            per[o.eng].append(o)
        for o in ops:
            kept = []
            for d in o.deps:
                p = ops[d]
                if p.dma:
                    kept.append(p)
                elif p.eng != o.eng:
                    p.sig = True
                    kept.append(p)
                elif o.eng in ("act", "dve", "pool") and (o.eidx - p.eidx) <= 2:
                    p.sig = True
                    kept.append(p)
            o.kept = kept
        cnt = {e: 0 for e in ENGS}
        dcnt = [0] * NDSEM
        nd = 0
        nsw = 0
        for o in ops:
            if o.dma:
                if o.eng == "pool":
                    o.sem = 16 + nsw % (NDSEM - 16)
                    nsw += 1
                else:
                    o.sem = nd % 16
                    nd += 1
                o.prev = dcnt[o.sem]
                dcnt[o.sem] += 16
                o.val = dcnt[o.sem]
            elif o.sig:
                cnt[o.eng] += 1
                o.tick = cnt[o.eng]

        def run(e, h):
            waited = {}

            def wait(key, sh, val):
                if waited.get(key, 0) >= val:
                    return
                h.wait_ge(sh, val)
                waited[key] = val

            for o in per[e]:
                for p in o.kept:
                    if p.dma:
                        wait(("d", p.sem), dsem[p.sem], p.val)
                    else:
                        wait(("e", p.eng), esem[p.eng], p.tick)
                if o.dma and o.prev > 0:
                    wait(("d", o.sem), dsem[o.sem], o.prev)
                inst = o.fn(h)
                if o.dma:
                    inst.then_inc(dsem[o.sem], 16)
                elif o.sig:
                    inst.then_inc(esem[o.eng], 1)

        @block.sync
        def _(h):
            run("sp", h)

        @block.scalar
        def _(h):
            run("act", h)

        @block.gpsimd
        def _(h):
            run("pool", h)

        @block.vector
        def _(h):
            run("dve", h)

        @block.tensor
        def _(h):
            run("pe", h)


def build(stage=4):
    nc = bass.Bass("TRN2", target_bir_lowering=False)
    S = Sched()

    def din(name, shape, dt=F32):
        return nc.dram_tensor(name, shape, dt, kind="ExternalInput")

    x_d = din("x", [2560, 1024])
    xoth_d = din("x_oth", [3 * 2048, 1024])
    wna_d = din("w_na", [1024, 1536])
    wg1_d = din("w_g1", [1024, 1536])
    wg2_d = din("w_g2", [1024, 544])
    wout_d = din("w_out", [1024, 1024])
    wff1_d = din("w_ff1", [1024, 4096])
    wff2_d = din("w_ff2", [4096, 1024])
    gmix_d = din("g_mix", [1, 1024])
    gff_d = din("g_ff", [1, 1024])
    gfin_d = din("g_fin", [1, 1024])
    gn_d = din("g_n", [1, 512])
    bmain_d = din("b_main", [128, 8 * 640])
    bedge_d = din("b_edge", [4, 128, 8 * 768])
    wgate_d = din("w_gate", [32, 512])
    gb_d = din("gb", [128, 4])
    sel_d = din("sel", [128, 8])
    mall_d = din("mall", [128, 128])
    rmask_d = din("rmask", [128, 512])
    rmask8_d = din("rmask8", [128, 128])
    ident_d = din("ident", [128, 128])
    out_d = nc.dram_tensor("out", [2048, 1024], F32, kind="ExternalOutput")
    yna_d = nc.dram_tensor("yna_scr", [2048, 512], BF16)
    wff1b_d = nc.dram_tensor("wff1_bf", [1024, 4096], BF16)
    wff2b_d = nc.dram_tensor("wff2_bf", [4096, 1024], BF16)
    woutb_d = nc.dram_tensor("wout_bf", [1024, 1024], BF16)
    dbg_d = None
    if stage == 2:
        dbg_d = nc.dram_tensor("dbg", [2048, 512], BF16, kind="ExternalOutput")
    if stage == 3:
        dbg_d = nc.dram_tensor("dbg", [2048, 512], BF16, kind="ExternalOutput")

    with ExitStack() as es:
        def sb(name, cols, dt, parts=128):
            h = es.enter_context(nc.sbuf_tensor(name, [parts, cols], dt))
            return Ten(name, h, cols, 4 if dt == F32 else 2)

        def ps(name):
            h = es.enter_context(nc.psum_tensor(name, [128, 1024], F32))
            return Ten(name, h, 1024, 4)

        UA = sb("UA", 34816, BF16)
        UB = sb("UB", 12288, BF16)
        UC = sb("UC", 16384, BF16)
        Y = sb("Y", 16 * 512 + 1024, BF16)
        XT = sb("XT", 8192, BF16)
        XN = sb("XN", 1024, BF16)
        XNB = sb("XNB", 1024, BF16)
        GF = sb("GF", 1024, F32)
        TF = sb("TF", 6656, F32)
        G = sb("G", 1024, F32)
        GN = sb("GN", 512, F32)
        RM = sb("RM", 512, F32)
        RM8 = sb("RM8", 128, F32)
        SM = sb("SM", 1024, F32)
        MALL = sb("MALL", 128, BF16)
        IDN = sb("IDN", 128, BF16)
        WG = sb("WG", 512, BF16)
        PS = [ps(f"PS{i}") for i in range(4)]
        esem = {e: es.enter_context(nc.semaphore("es_" + e)) for e in ENGS}
        dsem = [es.enter_context(nc.semaphore(f"ds{i}")) for i in range(NDSEM)]
        block = es.enter_context(nc.Block())

        XN2 = [XN.r(), XNB.r()]
        banks = [PS[i // 2].r((i % 2) * 512, (i % 2) * 512 + 512) for i in range(8)]
        bank_ctr = [0]

        def nbank():
            b = banks[bank_ctr[0] % 6]
            bank_ctr[0] += 1
            return b

        TBANK = banks[7]

        def sm(c0, n=1):
            return SM.r(c0, c0 + n)
        C_SS, C_MS, C_RS, C_NH = 0, 24, 48, 72
        C_SS2, C_MS2, C_RS2 = 80, 84, 88
        C_GBN, C_SEL, C_DP = 96, 100, 108
        C_TOT, C_DEC, C_INC, C_GG = 128, 256, 384, 512
        C_DECS = 704
        C_GT, C_DG, C_DC, C_RDEN = 640, 656, 672, 680

        def load_const(dst, src_ap, eng="sp"):
            S.op(eng, lambda h: h.dma_start(out=dst.ap, in_=src_ap), writes=[dst], dma=True)

        load_const(G.r(), gmix_d.ap().to_broadcast([128, 1024]))
        load_const(GN.r(), gn_d.ap().to_broadcast([128, 512]))
        load_const(RM.r(), rmask_d.ap())
        load_const(RM8.r(), rmask8_d.ap())
        load_const(sm(C_SEL, 8), sel_d.ap())
        load_const(sm(C_GBN, 4), gb_d.ap())
        S.op("pool", lambda h: h.dma_start(out=IDN.r().ap, in_=ident_d.ap()), writes=[IDN.r()], dma=True)
        S.op("pool", lambda h: h.dma_start(out=MALL.r().ap, in_=mall_d.ap()), writes=[MALL.r()], dma=True)
        S.op("pool", lambda h: h.dma_start(out=WG.r(p1=32).ap, in_=wgate_d.ap()), writes=[WG.r()], dma=True)
        S.op("pool", lambda h: h.memset(sm(C_NH, 4).ap, -0.5), writes=[sm(C_NH, 4)])
        gbn = sm(C_GBN, 4)
        S.op("pool", lambda h: h.tensor_scalar(out=gbn.ap, in0=gbn.ap, scalar1=-1.0, scalar2=None, op0=ALU.mult),
             reads=[gbn], writes=[gbn])

        def load_w(dst_ten, c0, d_t, rows, cols, nsplit=4):
            kc = rows // 128
            per = kc // nsplit
            for s in range(nsplit):
                dst = dst_ten.r(c0 + s * per * cols, c0 + (s + 1) * per * cols)
                src = d_t.ap()[s * per * 128:(s + 1) * per * 128, :].rearrange("(c p) n -> p c n", p=128)
                S.op("pool", lambda h, dst=dst, src=src: h.dma_start(
                    out=dst.ap.rearrange("p (c n) -> p c n", n=cols), in_=src), writes=[dst], dma=True)

        def norm_stats(src, col, junk=None, k=0, on_act=False):
            if junk is None:
                junk = TF.r(5376, 6400)
            ss = sm(C_SS2 + k)
            if on_act:
                S.op("act", lambda h: h.activation(out=junk.ap, in_=src.ap, func=AF.Square, accum_out=ss.ap),
                     reads=[src], writes=[junk, ss])
            else:
                S.op("dve", lambda h: h.scalar_tensor_tensor(out=junk.ap, in0=src.ap, scalar=1.0, in1=src.ap,
                                                             op0=ALU.mult, op1=ALU.mult, accum_out=ss.ap),
                     reads=[src], writes=[junk, ss])
            ms = sm(C_MS2 + k)
            S.op("pool", lambda h: h.tensor_scalar(out=ms.ap, in0=ss.ap, scalar1=1.0 / 1024, scalar2=EPS,
                                                   op0=ALU.mult, op1=ALU.add), reads=[ss], writes=[ms])
            rs = sm(col)
            nh = sm(C_NH)
            S.op("pool", lambda h: h.tensor_tensor(out=rs.ap, in0=ms.ap, in1=nh.ap, op=ALU.pow),
                 reads=[ms, nh], writes=[rs])
            return rs

        def scale_to_bf16(src, rs, gten, dst):
            S.op("dve", lambda h: h.scalar_tensor_tensor(out=dst.ap, in0=src.ap, scalar=rs.ap, in1=gten.ap,
                                                         op0=ALU.mult, op1=ALU.mult),
                 reads=[src, rs, gten], writes=[dst])

        def transpose8(src_bf, dst_fn, tb=None):
            if tb is None:
                tb = TBANK
            idn = IDN.r()

            def f(h):
                i = None
                for k in range(8):
                    i = h.transpose(out=tb.ap.bitcast(BF16)[:, k * 128:(k + 1) * 128],
                                    in_=src_bf.ap[:, k * 128:(k + 1) * 128], identity=idn.ap)
                return i
            S.op("pe", f, reads=[src_bf, idn], writes=[tb])
            dst = dst_fn
            S.op("act", lambda h: h.activation(out=dst.ap, in_=tb.ap.bitcast(BF16).rearrange("p (k t) -> p k t", k=8),
                                               func=AF.Copy), reads=[tb], writes=[dst])

        def mm_acc(out, pairs, reads):
            def f(h):
                i = None
                n = len(pairs)
                for q, (l, r) in enumerate(pairs):
                    i = h.matmul(out=out.ap, lhsT=l, rhs=r, start=(q == 0), stop=(q == n - 1))
                return i
            S.op("pe", f, reads=reads, writes=[out])

        def xt_slot(j, par=0):
            return XT.r(par * 4096, par * 4096 + 4096).v(lambda a: a.rearrange("p (k t) -> p k t", k=8)[:, :, j * 128:(j + 1) * 128])

        def cast_dram(src_ap, dst_ap, key):
            S.op("pool", lambda h: h.dma_start(out=dst_ap, in_=src_ap), writes=[key], dma=True)
        def run_inproj(tile_ids, load_stats, jobs_of_group, after_group0=None):
            n = len(tile_ids)
            rs = {0: load_stats(tile_ids[0])}

            def tile_work(i):
                t = tile_ids[i]
                xt = TF.r((t % 2) * 1024, (t % 2) * 1024 + 1024)
                xn = XN2[t % 2]
                scale_to_bf16(xt, rs[i], gmix, xn)
                if i + 1 < n:
                    rs[i + 1] = load_stats(tile_ids[i + 1])
                transpose8(xn, xt_slot(i % 4, (i // 4) % 2), tb=banks[6 + t % 2])
            for i in range(4):
                tile_work(i)
            ng = n // 4
            for g in range(ng):
                xtr = XT.r((g % 2) * 4096, (g % 2) * 4096 + 4096)
                jobs = jobs_of_group(g, xtr)
                nj_ = len(jobs)
                for c in range(4):
                    if g + 1 < ng:
                        tile_work(4 * (g + 1) + c)
                    for job in jobs[c * nj_ // 4:(c + 1) * nj_ // 4]:
                        job()
                if g == 0 and after_group0 is not None:
                    after_group0()

        load_w(UB, 0, wna_d, 1024, 1536)
        WNA = UB
        QT0, KT0, VA0 = 0, 8192, 18432
        va_all = UA.r(VA0, VA0 + 20 * 520)
        S.op("pool", lambda h: h.memset(va_all.ap.rearrange("p (t e) -> p t e", e=65)[:, :, 64:65], 1.0),
             writes=[va_all])
        gmix = G.r()
        EM0, EE0 = 4352, 9472
        EX0, PT0 = 0, 1536
        edge_idx = {2: 0, 3: 1, 16: 2, 17: 3}

        EDGE_Q = ["pool"]
        def load_edge(i):
            for q in range(4):
                stg = TF.r(2304 + (q % 2) * 1536, 2304 + (q % 2) * 1536 + 1536)
                S.op(EDGE_Q[0], lambda h, stg=stg, q=q, i=i: h.dma_start(
                    out=stg.ap, in_=bedge_d.ap()[edge_idx[i], :, q * 1536:(q + 1) * 1536]),
                    writes=[stg], dma=True)
                dst = UC.r(EE0 + q * 1536, EE0 + (q + 1) * 1536)
                S.op("act", lambda h, stg=stg, dst=dst: h.activation(out=dst.ap, in_=stg.ap, func=AF.Exp),
                     reads=[stg], writes=[dst])
        def prep_tables():
            for q in range(4):
                stg = TF.r(2304 + (q % 2) * 1536, 2304 + (q % 2) * 1536 + 1280)
                S.op("pool", lambda h, stg=stg, q=q: h.dma_start(out=stg.ap, in_=bmain_d.ap()[:, q * 1280:(q + 1) * 1280]),
                     writes=[stg], dma=True)
                dst = UC.r(EM0 + q * 1280, EM0 + (q + 1) * 1280)
                S.op("act", lambda h, stg=stg, dst=dst: h.activation(out=dst.ap, in_=stg.ap, func=AF.Exp),
                     reads=[stg], writes=[dst])
            load_edge(2)
            EDGE_Q[0] = "sp"
        def load_stats_A(t):
            xt = TF.r((t % 2) * 1024, (t % 2) * 1024 + 1024)
            S.op("sp", lambda h, xt=xt, t=t: h.dma_start(out=xt.ap, in_=x_d.ap()[t * 128:(t + 1) * 128, :]),
                 writes=[xt], dma=True)
            return norm_stats(xt, C_RS + t, k=t % 2)

        def jobs_A(g, xtr):
            tok0 = g * 512
            wr = WNA.r()
            lo, hi = max(tok0, 256), min(tok0 + 512, 2304)
            jobs = []
            for f in range(4):
                def jq(f=f):
                    b = nbank()
                    o = Reg(b.ap[:, 0:hi - lo], b.keys)
                    mm_acc(o, [(wr.ap[:, k * 1536 + f * 128:k * 1536 + f * 128 + 128],
                                xtr.ap[:, k * 512 + lo - tok0:k * 512 + hi - tok0]) for k in range(8)], [wr, xtr])
                    dst = UA.r(QT0 + f * 2048 + lo - 256, QT0 + f * 2048 + hi - 256)
                    S.op("act", lambda h: h.activation(out=dst.ap, in_=o.ap, func=AF.Copy), reads=[o], writes=[dst])

                def jk(f=f):
                    b = nbank()
                    mm_acc(b, [(wr.ap[:, k * 1536 + 512 + f * 128:k * 1536 + 512 + f * 128 + 128],
                                xtr.ap[:, k * 512:(k + 1) * 512]) for k in range(8)], [wr, xtr])
                    dst = UA.r(KT0 + f * 2560 + tok0, KT0 + f * 2560 + tok0 + 512)
                    S.op("act", lambda h: h.activation(out=dst.ap, in_=b.ap, func=AF.Copy), reads=[b], writes=[dst])
                jobs += [jq, jk]
            for jj in range(4):
                def jv(jj=jj):
                    tt = g * 4 + jj
                    b = nbank()
                    mm_acc(b, [(xtr.ap[:, k * 512 + jj * 128:k * 512 + jj * 128 + 128],
                                wr.ap[:, k * 1536 + 1024:k * 1536 + 1536]) for k in range(8)], [wr, xtr])
                    dst = UA.r(VA0 + tt * 520, VA0 + tt * 520 + 520)
                    S.op("dve", lambda h: h.tensor_copy(
                        out=dst.ap.rearrange("p (e c) -> p e c", c=65)[:, :, 0:64],
                        in_=b.ap.rearrange("p (e c) -> p e c", c=64)), reads=[b], writes=[dst])
                jobs.append(jv)
            return jobs
        run_inproj(list(range(20)), load_stats_A, jobs_A, after_group0=prep_tables)

        def tiles_of(i):
            if i == 2:
                return list(range(0, 6))
            if i == 17:
                return list(range(14, 20))
            return list(range(i - 2, i + 3))

        items = [(i, hd) for i in range(2, 18) for hd in range(8)]

        def st_of(idx, i):
            return PS[idx % 2].r(0, len(tiles_of(i)) * 128)

        def emit_qk(idx):
            i, hd = items[idx]
            tiles_j = tiles_of(i)
            f, pb = hd // 2, (hd % 2) * 64
            st = st_of(idx, i)
            kreg = UA.r(KT0 + f * 2560, KT0 + (f + 1) * 2560)
            qreg = UA.r(QT0 + f * 2048 + (i - 2) * 128, QT0 + f * 2048 + (i - 1) * 128)

            def fqk(h, st=st, kreg=kreg, qreg=qreg, pb=pb, tiles_j=tiles_j):
                ins = None
                for jj, j in enumerate(tiles_j):
                    ins = h.matmul(out=st.ap[:, jj * 128:(jj + 1) * 128],
                                   lhsT=kreg.ap[pb:pb + 64, j * 128:(j + 1) * 128],
                                   rhs=qreg.ap[pb:pb + 64, :], start=True, stop=True)
                return ins
            S.op("pe", fqk, reads=[kreg, qreg], writes=[st])

        def emit_rest(idx):
            i, hd = items[idx]
            tiles_j = tiles_of(i)
            nj = len(tiles_j)
            ncol = nj * 128
            if hd == 0 and i in (3, 17):
                load_edge(i)
            if hd == 0 and i == 4:
                load_edge(16)
            if hd == 0 and i == 3:
                load_w(UB, 0, wg1_d, 1024, 1536)
                load_w(UC, 0, wg2_d, 1024, 544)
            if hd == 0 and i == 13:
                cast_dram(wout_d.ap(), woutb_d.ap(), dkey("woutb"))
            if hd == 0 and 5 <= i <= 12:
                q = i - 5
                if q < 4:
                    cast_dram(wff1_d.ap()[:, q * 1024:(q + 1) * 1024], wff1b_d.ap()[:, q * 1024:(q + 1) * 1024], dkey("wff1b", q))
                else:
                    q -= 4
                    cast_dram(wff2_d.ap()[q * 1024:(q + 1) * 1024, :], wff2b_d.ap()[q * 1024:(q + 1) * 1024, :], dkey("wff2b", q))
            st = st_of(idx, i)
            ex = XT.r(EX0 + (idx % 2) * 768, EX0 + (idx % 2) * 768 + ncol)
            S.op("act", lambda h, ex=ex, st=st: h.activation(out=ex.ap, in_=st.ap, func=AF.Exp, scale=0.125),
                 reads=[st], writes=[ex])
            if i in edge_idx:
                ee = UC.r(EE0 + hd * 768, EE0 + hd * 768 + ncol)
            else:
                ee = UC.r(EM0 + hd * 640, EM0 + hd * 640 + ncol)
            pt = XT.r(PT0 + (idx % 2) * 768, PT0 + (idx % 2) * 768 + ncol)
            S.op("dve", lambda h, pt=pt, ex=ex, ee=ee: h.tensor_tensor(out=pt.ap, in0=ex.ap, in1=ee.ap, op=ALU.mult),
                 reads=[ex, ee], writes=[pt])
            pvt = PS[2 + (i % 2)]
            pc0 = (hd // 4) * 512 + (hd % 4) * 65
            pv = pvt.r(pc0, pc0 + 65)
            vreg = UA.r(VA0, VA0 + 20 * 520)

            def fpv(h, pv=pv, pt=pt, vreg=vreg, hd=hd, tiles_j=tiles_j, nj=nj):
                ins = None
                for jj, j in enumerate(tiles_j):
                    ins = h.matmul(out=pv.ap, lhsT=pt.ap[:, jj * 128:(jj + 1) * 128],
                                   rhs=vreg.ap[:, j * 520 + hd * 65:j * 520 + hd * 65 + 65],
                                   start=(jj == 0), stop=(jj == nj - 1))
                return ins
            S.op("pe", fpv, reads=[pt, vreg], writes=[pv])
            if hd != 7:
                return
            pvr = pvt.r()
            rden = sm(C_RDEN + (i % 2) * 8, 8)
            pv4 = pvr.v(lambda a: a.rearrange("p (b c) -> p b c", b=2)[:, :, 0:260].rearrange("p b (e c) -> p b e c", c=65))
            S.op("dve", lambda h, rden=rden, pv4=pv4: h.reciprocal(
                out=rden.ap.rearrange("p (b e) -> p b e", b=2), in_=pv4.ap[:, :, :, 64]),
                reads=[pvr], writes=[rden])
            ystg = Y.r(8192 + (i % 2) * 512, 8192 + (i % 2) * 512 + 512)
            S.op("dve", lambda h, ystg=ystg, pv4=pv4, rden=rden: h.tensor_tensor(
                out=ystg.ap.rearrange("p (b e c) -> p b e c", b=2, e=4), in0=pv4.ap[:, :, :, 0:64],
                in1=rden.ap.rearrange("p (b e) -> p b e", b=2).unsqueeze(3).to_broadcast([128, 2, 4, 64]),
                op=ALU.mult), reads=[pvr, rden], writes=[ystg])
            tt = i - 2
            S.op("sp", lambda h, ystg=ystg, tt=tt: h.dma_start(out=yna_d.ap()[tt * 128:(tt + 1) * 128, :], in_=ystg.ap),
                 reads=[ystg], writes=[dkey("yna", tt)], dma=True)
            if stage == 2:
                S.op("sp", lambda h, ystg=ystg, tt=tt: h.dma_start(out=dbg_d.ap()[tt * 128:(tt + 1) * 128, :], in_=ystg.ap),
                     reads=[ystg], writes=[dkey("dbg", tt)], dma=True)

        emit_qk(0)
        for idx in range(len(items)):
            if idx + 1 < len(items):
                emit_qk(idx + 1)
            emit_rest(idx)

        if stage == 2:
            S.op("sp", lambda h: h.nop(), reads=[dkey("dbg", t) for t in range(16)] + [dkey("yna", t) for t in range(16)])
            S.emit(block, esem, dsem)
            return nc

        GQ0, GK0, GV0, GR0, GZ0 = 0, 8192, 16384, 24576, 32768
        FSf = Y.r(0, 3120).v(lambda a: a.bitcast(F32))
        SINP = Y.r(0, 1024).v(lambda a: a.bitcast(F32))
        def emit_fold():
            SIN = TF.r(2100, 2612)
            S.op("pool", lambda h: h.memset(SIN.ap, 0.0), writes=[SIN])
            for d, order in ((0, (0, 1, 2)), (1, (2, 1, 0))):
                p0, p1 = d * 64, d * 64 + 64
                for u in order:
                    dp = sm(C_DP, 4)
                    selu = sm(C_SEL + u)
                    S.op("dve", lambda h, dp=dp, u=u, selu=selu, p0=p0, p1=p1: h.tensor_scalar(
                        out=dp.ap[p0:p1, :], in0=FSf.ap[p0:p1, u * 520 + 512:u * 520 + 516], scalar1=-1.0, scalar2=selu.ap[p0:p1, :],
                        op0=ALU.add, op1=ALU.mult), reads=[FSf, selu], writes=[dp])
                    S.op("pool", lambda h, dp=dp, p0=p0, p1=p1: h.tensor_scalar(out=dp.ap[p0:p1, :], in0=dp.ap[p0:p1, :], scalar1=1.0, scalar2=None, op0=ALU.add),
                         reads=[dp], writes=[dp])
                    fx = TF.r(1032, 1544)
                    S.op("dve", lambda h, fx=fx, u=u, selu=selu, p0=p0, p1=p1: h.tensor_scalar(
                        out=fx.ap[p0:p1, :], in0=FSf.ap[p0:p1, u * 520:u * 520 + 512], scalar1=selu.ap[p0:p1, :], scalar2=None,
                        op0=ALU.mult), reads=[FSf, selu], writes=[fx])
                    for hh in range(4):
                        S.op("dve", lambda h, fx=fx, dp=dp, hh=hh, p0=p0, p1=p1: h.scalar_tensor_tensor(
                            out=SIN.ap[p0:p1, hh * 128:(hh + 1) * 128], in0=SIN.ap[p0:p1, hh * 128:(hh + 1) * 128], scalar=dp.ap[p0:p1, hh:hh + 1],
                            in1=fx.ap[p0:p1, hh * 128:(hh + 1) * 128], op0=ALU.mult, op1=ALU.add), reads=[SIN, fx, dp], writes=[SIN])
            S.op("act", lambda h: h.activation(out=SINP.ap, in_=SIN.ap, func=AF.Copy), reads=[SIN, FSf], writes=[SINP])

        for sg, own in ((0, False), (1, False), (2, False), (3, True)):
            if own:
                emit_fold()
            def load_stats_D(t, sg=sg, own=own):
                xt = TF.r((t % 2) * 1024, (t % 2) * 1024 + 1024)
                if own:
                    S.op("sp", lambda h, xt=xt, t=t: h.dma_start(out=xt.ap, in_=x_d.ap()[t * 128:(t + 1) * 128, :]),
                         writes=[xt], dma=True)
                    return sm(C_RS + t)
                S.op("sp", lambda h, xt=xt, t=t, sg=sg: h.dma_start(out=xt.ap, in_=xoth_d.ap()[sg * 2048 + (t - 2) * 128:sg * 2048 + (t - 1) * 128, :]),
                     writes=[xt], dma=True)
                return norm_stats(xt, C_RS2 + t % 2, junk=UC.r(8192, 9216), k=t % 2, on_act=True)

            def jobs_D(g, xtr, own=own):
                w1 = UB.r()
                w2 = UC.r(0, 8 * 544)
                jobs = []
                for hh in range(4):
                    for which, base in (((0, GQ0), (1, GK0)) if own else ((1, GK0),)):
                        def jqk(hh=hh, which=which, base=base):
                            b = nbank()
                            c0 = which * 512 + hh * 128
                            mm_acc(b, [(w1.ap[:, k * 1536 + c0:k * 1536 + c0 + 128], xtr.ap[:, k * 512:(k + 1) * 512])
                                       for k in range(8)], [w1, xtr])
                            dst = UA.r(base + hh * 2048 + g * 512, base + hh * 2048 + g * 512 + 512)
                            sc = 0.125 if which == 0 else 1.0
                            S.op("act", lambda h: h.activation(out=dst.ap, in_=b.ap, func=AF.Copy, scale=sc), reads=[b], writes=[dst])
                        jobs.append(jqk)

                def jz():
                    b = nbank()
                    bz = Reg(b.ap[0:32, :], b.keys)
                    mm_acc(bz, [(w2.ap[:, k * 544 + 512:k * 544 + 544], xtr.ap[:, k * 512:(k + 1) * 512]) for k in range(8)], [w2, xtr])
                    dst = UA.r(GZ0 + g * 512, GZ0 + g * 512 + 512, p1=32)
                    S.op("act", lambda h: h.activation(out=dst.ap, in_=bz.ap, func=AF.Copy), reads=[bz], writes=[dst])
                jobs.append(jz)
                for jj in range(4):
                    def jv(jj=jj):
                        tt = g * 4 + jj
                        b = nbank()
                        mm_acc(b, [(xtr.ap[:, k * 512 + jj * 128:k * 512 + jj * 128 + 128], w1.ap[:, k * 1536 + 1024:k * 1536 + 1536])
                                   for k in range(8)], [w1, xtr])
                        dst = UA.r(GV0 + tt * 512, GV0 + tt * 512 + 512)
                        S.op("dve", lambda h: h.tensor_copy(out=dst.ap, in_=b.ap), reads=[b], writes=[dst])
                    jobs.append(jv)
                    if own:
                        def jr(jj=jj):
                            tt = g * 4 + jj
                            b = nbank()
                            mm_acc(b, [(xtr.ap[:, k * 512 + jj * 128:k * 512 + jj * 128 + 128], w2.ap[:, k * 544:k * 544 + 512])
                                       for k in range(8)], [w2, xtr])
                            dst = UA.r(GR0 + tt * 512, GR0 + tt * 512 + 512)
                            S.op("act", lambda h: h.activation(out=dst.ap, in_=b.ap, func=AF.Copy), reads=[b], writes=[dst])
                        jobs.append(jr)
                return jobs
            run_inproj(list(range(2, 18)), load_stats_D, jobs_D)
            AALL0, KDT0 = 0, 8192
            SLOC = UC
            FG0 = 3328
            mall = MALL.r()
            rmask = RM.r()
            wg = WG.r(p1=32)
            ONES = TF.r(5632, 6144)
            if sg == 0:
                S.op("pool", lambda h: h.memset(ONES.ap, 1.0), writes=[ONES])
            if own:
                for zt in (UC.r(), XN2[0], XN2[1], GF.r()):
                    S.op("pool", lambda h, zt=zt: h.memset(zt.ap, 0.0), writes=[zt])
            def temps(st_):
                if st_ == 0:
                    return dict(sp=TF.r(0, 512), cf=TF.r(512, 1024), d=TF.r(1024, 1536), eB=TF.r(1536, 2048),
                                emB=TF.r(2048, 2560), eD=TF.r(2560, 3072), kd=TF.r(3072, 3328).v(lambda a: a.bitcast(BF16)))
                yv = lambda k: Y.r(3120 + 1024 * k, 3120 + 1024 * (k + 1)).v(lambda a: a.bitcast(F32))
                xv = lambda k: XT.r(4096 + 1024 * k, 4096 + 1024 * (k + 1)).v(lambda a: a.bitcast(F32))
                return dict(sp=yv(0), cf=yv(1), d=yv(2), eB=yv(3), emB=xv(0), eD=xv(1), kd=XT.r(6144, 6656))

            def v3(a):
                return a.rearrange("p (c i) -> p c i", i=64)

            def totb(a):
                return a.unsqueeze(2).to_broadcast([a.shape[0], 8, 64])

            def stage_a(g, hh, part, own=own):
                gh = g * 4 + hh
                T = temps(gh % 2)
                t_sp, t_cf, t_d, t_eB, t_emB, t_eD, kd = T["sp"], T["cf"], T["d"], T["eB"], T["emB"], T["eD"], T["kd"]
                if part == 1:
                    zr = UA.r(GZ0 + g * 512, GZ0 + g * 512 + 512, p1=32)
                    xg = banks[4 + gh % 2]
                    mm_acc(xg, [(wg.ap[:, hh * 128:(hh + 1) * 128], zr.ap)], [wg, zr])
                    gbh = sm(C_GBN + hh)
                    S.op("act", lambda h: h.activation(out=t_sp.ap, in_=xg.ap, func=AF.Exp, scale=-1.0, bias=gbh.ap),
                         reads=[xg, gbh], writes=[t_sp])
                    S.op("act", lambda h: h.activation(out=t_sp.ap, in_=t_sp.ap, func=AF.Ln, bias=1.0, scale=1.0),
                         reads=[t_sp], writes=[t_sp])
                kreg = UA.r(GK0 + hh * 2048 + g * 512, GK0 + hh * 2048 + g * 512 + 512)
                if not own:
                    if part == 1:
                        S.op("dve", lambda h: h.tensor_tensor_scan(out=t_cf.ap, data0=ONES.ap, data1=t_sp.ap, initial=0.0,
                                                                   op0=ALU.mult, op1=ALU.add), reads=[ONES, t_sp], writes=[t_cf])
                        gtc = sm(C_GT + hh * 4 + g)
                        S.op("dve", lambda h: h.tensor_copy(out=gtc.ap, in_=t_cf.ap[:, 511:512]), reads=[t_cf], writes=[gtc])
                        S.op("dve", lambda h: h.tensor_scalar(out=t_d.ap[0:64, :], in0=t_cf.ap[0:64, :], scalar1=-1.0,
                                                              scalar2=gtc.ap[0:64, :], op0=ALU.mult, op1=ALU.add),
                             reads=[t_cf, gtc], writes=[t_d])
                        S.op("dve", lambda h: h.tensor_tensor(out=t_d.ap[64:128, :], in0=t_cf.ap[64:128, :],
                                                              in1=t_sp.ap[64:128, :], op=ALU.subtract), reads=[t_cf, t_sp], writes=[t_d])
                    if part == 1:
                        return
                    S.op("act", lambda h: h.activation(out=t_eD.ap, in_=t_d.ap, func=AF.Exp, scale=-1.0 / 16), reads=[t_d], writes=[t_eD])
                    S.op("dve", lambda h: h.tensor_tensor(out=kd.ap, in0=kreg.ap, in1=t_eD.ap, op=ALU.mult),
                         reads=[kreg, t_eD], writes=[kd])
                    return
                tot = sm(C_TOT + hh * 32 + g * 8, 8)
                if part == 1:
                    S.op("dve", lambda h: h.tensor_tensor_scan(out=t_cf.ap, data0=rmask.ap, data1=t_sp.ap, initial=0.0,
                                                               op0=ALU.mult, op1=ALU.add), reads=[rmask, t_sp], writes=[t_cf])
                    S.op("dve", lambda h: h.tensor_copy(out=tot.ap, in_=t_cf.ap[:, 63:512:64]), reads=[t_cf], writes=[tot])
                    S.op("dve", lambda h: h.tensor_tensor(out=v3(t_cf.ap[64:128, :]), in0=totb(tot.ap[64:128, :]),
                                                          in1=v3(t_cf.ap[64:128, :]), op=ALU.subtract), reads=[t_cf, tot], writes=[t_cf])
                    S.op("dve", lambda h: h.tensor_tensor(out=t_cf.ap[64:128, :], in0=t_cf.ap[64:128, :],
                                                          in1=t_sp.ap[64:128, :], op=ALU.add), reads=[t_cf, t_sp], writes=[t_cf])
                    S.op("dve", lambda h: h.tensor_tensor(out=v3(t_d.ap), in0=totb(tot.ap), in1=v3(t_cf.ap), op=ALU.subtract),
                         reads=[t_cf, tot], writes=[t_d])
                    return
                S.op("act", lambda h: h.activation(out=t_eB.ap, in_=t_cf.ap, func=AF.Exp, scale=-1.0 / 16), reads=[t_cf], writes=[t_eB])
                S.op("act", lambda h: h.activation(out=t_emB.ap, in_=t_cf.ap, func=AF.Exp, scale=1.0 / 16), reads=[t_cf], writes=[t_emB])
                S.op("act", lambda h: h.activation(out=t_eD.ap, in_=t_d.ap, func=AF.Exp, scale=-1.0 / 16), reads=[t_d], writes=[t_eD])
                qreg = UA.r(GQ0 + hh * 2048 + g * 512, GQ0 + hh * 2048 + g * 512 + 512)
                S.op("dve", lambda h: h.tensor_tensor(out=kd.ap, in0=kreg.ap, in1=t_eD.ap, op=ALU.mult), reads=[kreg, t_eD], writes=[kd])
                S.op("pool", lambda h: h.tensor_tensor(out=kreg.ap, in0=kreg.ap, in1=t_emB.ap, op=ALU.mult), reads=[kreg, t_emB], writes=[kreg])
                S.op("dve", lambda h: h.tensor_tensor(out=qreg.ap, in0=qreg.ap, in1=t_eB.ap, op=ALU.mult), reads=[qreg, t_eB], writes=[qreg])

            HBS = [TF.r(5376, 6400), G.r()]

            def stage_b(g, hh, own=own):
                gh = g * 4 + hh
                kd = temps(gh % 2)["kd"]
                tb = banks[6 + gh % 2]
                idn = IDN.r()

                def ftr(h):
                    ins = None
                    for ttl in range(4):
                        ins = h.transpose(out=tb.ap.bitcast(BF16)[:, ttl * 128:(ttl + 1) * 128],
                                          in_=kd.ap[:, ttl * 128:(ttl + 1) * 128], identity=idn.ap)
                    return ins
                S.op("pe", ftr, reads=[kd, idn], writes=[tb])
                if own:
                    kdt = XN2[gh % 2]
                    S.op("act", lambda h: h.activation(out=kdt.ap[0:64, 0:512], in_=tb.ap.bitcast(BF16)[0:64, 0:512], func=AF.Copy),
                         reads=[tb], writes=[kdt])
                    S.op("act", lambda h: h.activation(out=kdt.ap[64:128, 512:1024], in_=tb.ap.bitcast(BF16)[64:128, 0:512], func=AF.Copy),
                         reads=[tb], writes=[kdt])
                else:
                    kdt = XN2[gh % 2].v(lambda a: a[:, 0:512])
                    S.op("act", lambda h: h.activation(out=kdt.ap, in_=tb.ap.bitcast(BF16)[:, 0:512], func=AF.Copy), reads=[tb], writes=[kdt])
                vgrp = UA.r(GV0 + g * 4 * 512, GV0 + (g + 1) * 4 * 512)
                fg = TF.r(FG0 + gh * 128, FG0 + gh * 128 + 128)
                if not own:
                    fb_ = banks[gh % 2]
                    fo_ = Reg(fb_.ap[:, 0:128], fb_.keys)
                    mm_acc(fo_, [(kdt.ap[:, ttl * 128:(ttl + 1) * 128],
                                  vgrp.ap[:, ttl * 512 + hh * 128:ttl * 512 + hh * 128 + 128]) for ttl in range(4)], [kdt, vgrp])
                    S.op("act", lambda h: h.activation(out=fg.ap, in_=fo_.ap, func=AF.Copy), reads=[fo_], writes=[fg])
                    return
                qreg = UA.r(GQ0 + hh * 2048 + g * 512, GQ0 + hh * 2048 + g * 512 + 512)
                kreg = UA.r(GK0 + hh * 2048 + g * 512, GK0 + hh * 2048 + g * 512 + 512)
                ab = PS[0].r(0, 256)
                ab2 = PS[0].r(512, 768)

                def fa(h):
                    ins = None
                    for ttl in range(4):
                        for pr in range(2):
                            for d in range(2):
                                c0 = (2 * ttl + pr) * 64
                                dst = ab if d == 0 else ab2
                                ins = h.matmul(out=dst.ap[pr * 64:(pr + 1) * 64, ttl * 64:(ttl + 1) * 64],
                                               lhsT=kreg.ap[d * 64:(d + 1) * 64, c0:c0 + 64],
                                               rhs=qreg.ap[d * 64:(d + 1) * 64, c0:c0 + 64], start=True, stop=True)
                    return ins
                S.op("pe", fa, reads=[kreg, qreg], writes=[ab, ab2])
                adst = UB.r(AALL0 + g * 4 * 512, AALL0 + (g + 1) * 4 * 512)
                for d, src in ((0, ab), (1, ab2)):
                    S.op("dve" if d == 0 else "pool" if False else "dve", lambda h, src=src, d=d: h.tensor_tensor(
                        out=adst.ap.rearrange("p (t x) -> p t x", x=512)[:, :, hh * 128 + d * 64:hh * 128 + d * 64 + 64],
                        in0=src.ap.rearrange("p (t x) -> p t x", x=64),
                        in1=mall.ap[:, d * 64:(d + 1) * 64].unsqueeze(1).to_broadcast([128, 4, 64]), op=ALU.mult),
                        reads=[src, mall], writes=[adst])
                cps = PS[1].r()
                vt4 = UA.r(GV0 + g * 4 * 512, GV0 + (g + 1) * 4 * 512)

                def fc(h):
                    ins = None
                    for c in range(8):
                        ttl, pr = c // 2, c % 2
                        for eh in range(2):
                            for dh in range(2):
                                pos = c if dh == 0 else 7 - c
                                o = cps.ap[dh * 64:(dh + 1) * 64, eh * 512:(eh + 1) * 512].rearrange("p (e c) -> p c e", c=8)[:, pos, :]
                                ins = h.matmul(out=o,
                                               lhsT=kdt.ap[:, pr * 512 + ttl * 128 + dh * 64:pr * 512 + ttl * 128 + dh * 64 + 64],
                                               rhs=vt4.ap[:, ttl * 512 + hh * 128 + eh * 64:ttl * 512 + hh * 128 + eh * 64 + 64],
                                               start=True, stop=True)
                    return ins
                S.op("pe", fc, reads=[kdt, vt4], writes=[cps])
                tot = sm(C_TOT + hh * 32 + g * 8, 8)
                decb = GF.r()
                dbv = decb.ap.rearrange("p (e c) -> p e c", c=8)
                S.op("act", lambda h: h.activation(out=dbv[0:64, :, 1:8], in_=tot.ap[0:64, 1:8].unsqueeze(1).to_broadcast([64, 128, 7]),
                                                   func=AF.Exp, scale=-1.0 / 16), reads=[tot], writes=[decb])
                S.op("act", lambda h: h.activation(out=dbv[64:128, :, 1:8], in_=tot.ap[64:128, 6::-1].unsqueeze(1).to_broadcast([64, 128, 7]),
                                                   func=AF.Exp, scale=-1.0 / 16), reads=[tot], writes=[decb])
                HB = HBS[gh % 2]
                S.op("dve", lambda h: h.tensor_tensor_scan(out=HB.ap, data0=decb.ap, data1=cps.ap, initial=0.0, op0=ALU.mult, op1=ALU.add),
                     reads=[decb, cps], writes=[HB])

            def stage_c(g, hh):
                gh = g * 4 + hh
                HB = HBS[gh % 2]
                fg = TF.r(FG0 + gh * 128, FG0 + gh * 128 + 128)
                sl8 = SLOC.r((g * 8 * 4) * 128, ((g + 1) * 8 * 4) * 128)
                hv = HB.ap.rearrange("p (e c) -> p c e", c=8)
                slv = sl8.ap.rearrange("p (n x) -> p n x", x=512)
                S.op("act", lambda h: h.activation(out=slv[0:64, 1:8, hh * 128:(hh + 1) * 128], in_=hv[0:64, 0:7, :], func=AF.Copy),
                     reads=[HB], writes=[sl8])
                S.op("dve", lambda h: h.tensor_copy(out=slv[64:128, 0:7, hh * 128:(hh + 1) * 128], in_=hv[64:128, 6::-1, :]),
                     reads=[HB], writes=[sl8])
                S.op("act", lambda h: h.activation(out=fg.ap, in_=hv[:, 7, :], func=AF.Copy), reads=[HB], writes=[fg])

            ghs = [(g, hh) for g in range(4) for hh in range(4)]
            stage_a(*ghs[0], 1)
            stage_a(*ghs[1], 1)
            stage_a(*ghs[0], 2)
            for ii in range(16):
                if ii + 2 < 16:
                    stage_a(*ghs[ii + 2], 1)
                if ii + 1 < 16:
                    stage_a(*ghs[ii + 1], 2)
                stage_b(*ghs[ii])
                if own and ii >= 1:
                    stage_c(*ghs[ii - 1])
            if own:
                stage_c(*ghs[15])

            totall = sm(C_TOT, 128)
            gt = sm(C_GT, 16)
            if own:
                S.op("dve", lambda h: h.tensor_reduce(out=gt.ap, in_=totall.ap.rearrange("p (a c) -> p a c", c=8), axis=AX.X, op=ALU.add),
                     reads=[totall], writes=[gt])
            dg = sm(C_DG, 16)
            S.op("act", lambda h: h.activation(out=dg.ap, in_=gt.ap, func=AF.Exp, scale=-1.0 / 16), reads=[gt], writes=[dg])
            if own:
                inc = sm(C_INC, 128)
                rm8 = RM8.r()
                S.op("dve", lambda h: h.tensor_tensor_scan(out=inc.ap, data0=rm8.ap, data1=totall.ap, initial=0.0, op0=ALU.mult, op1=ALU.add),
                     reads=[rm8, totall], writes=[inc])
                S.op("dve", lambda h: h.tensor_tensor(out=inc.ap[0:64, :], in0=inc.ap[0:64, :], in1=totall.ap[0:64, :], op=ALU.subtract),
                     reads=[inc, totall], writes=[inc])
                S.op("dve", lambda h: h.tensor_tensor(out=inc.ap[64:128, :].rearrange("p (a c) -> p a c", c=8),
                                                      in0=gt.ap[64:128, :].unsqueeze(2).to_broadcast([64, 16, 8]),
                                                      in1=inc.ap[64:128, :].rearrange("p (a c) -> p a c", c=8), op=ALU.subtract),
                     reads=[inc, gt], writes=[inc])
                gg = sm(C_GG, 128)
                S.op("act", lambda h: h.activation(out=gg.ap, in_=inc.ap, func=AF.Exp, scale=-1.0 / 16), reads=[inc], writes=[gg])
                for hh in range(4):
                    qreg = UA.r(GQ0 + hh * 2048, GQ0 + (hh + 1) * 2048)
                    kreg = UA.r(GK0 + hh * 2048, GK0 + (hh + 1) * 2048)
                    ggh = sm(C_GG + hh * 32, 32)
                    S.op("dve", lambda h, qreg=qreg, kreg=kreg, ggh=ggh: h.tensor_tensor(
                        out=kreg.ap.rearrange("p (n i) -> p n i", i=64), in0=qreg.ap.rearrange("p (n i) -> p n i", i=64),
                        in1=ggh.ap.unsqueeze(2).to_broadcast([128, 32, 64]), op=ALU.mult), reads=[qreg, ggh], writes=[kreg])

            else:
                PK = TF.r(0, 516)
                for hh in range(4):
                    for d, order in ((0, range(4)), (1, range(3, -1, -1))):
                        p0, p1 = d * 64, d * 64 + 64
                        pk = TF.r(hh * 128, hh * 128 + 128)
                        for q, g in enumerate(order):
                            fg = TF.r(FG0 + (g * 4 + hh) * 128, FG0 + (g * 4 + hh) * 128 + 128)
                            if q == 0:
                                S.op("act", lambda h, pk=pk, fg=fg, p0=p0, p1=p1: h.activation(out=pk.ap[p0:p1, :], in_=fg.ap[p0:p1, :], func=AF.Copy),
                                     reads=[fg], writes=[pk])
                            else:
                                dcol = sm(C_DG + hh * 4 + g)
                                S.op("dve", lambda h, pk=pk, fg=fg, dcol=dcol, p0=p0, p1=p1: h.scalar_tensor_tensor(
                                    out=pk.ap[p0:p1, :], in0=pk.ap[p0:p1, :], scalar=dcol.ap[p0:p1, :], in1=fg.ap[p0:p1, :],
                                    op0=ALU.mult, op1=ALU.add), reads=[pk, fg, dcol], writes=[pk])
                ct = sm(C_DC, 4)
                S.op("dve", lambda h: h.tensor_reduce(out=ct.ap, in_=gt.ap.rearrange("p (a c) -> p a c", c=4), axis=AX.X, op=ALU.add),
                     reads=[gt], writes=[ct])
                pkd = TF.r(512, 516)
                S.op("act", lambda h: h.activation(out=pkd.ap, in_=ct.ap, func=AF.Exp, scale=-1.0 / 16), reads=[ct], writes=[pkd])
                S.op("act", lambda h, sg=sg: h.activation(out=FSf.ap[:, sg * 520:sg * 520 + 516], in_=PK.ap, func=AF.Copy), reads=[PK], writes=[FSf])
        SIGf = XT.r(0, 2048)
        SIG = SIGf
        for hh in range(4):
            for d, order in ((0, range(4)), (1, range(3, -1, -1))):
                p0, p1 = d * 64, d * 64 + 64
                sin = Y.r(hh * 256, hh * 256 + 256).v(lambda a: a.bitcast(F32))
                for q, g in enumerate(order):
                    S.op("act", lambda h, sin=sin, g=g, hh=hh, p0=p0, p1=p1: h.activation(
                        out=SIG.ap[p0:p1, (g * 4 + hh) * 128:(g * 4 + hh + 1) * 128], in_=sin.ap[p0:p1, :], func=AF.Copy),
                        reads=[sin], writes=[SIGf])
                    if q < 3:
                        fg = TF.r(FG0 + (g * 4 + hh) * 128, FG0 + (g * 4 + hh) * 128 + 128)
                        dcol = sm(C_DG + hh * 4 + g)
                        S.op("dve", lambda h, sin=sin, fg=fg, dcol=dcol, p0=p0, p1=p1: h.scalar_tensor_tensor(
                            out=sin.ap[p0:p1, :], in0=sin.ap[p0:p1, :], scalar=dcol.ap[p0:p1, :], in1=fg.ap[p0:p1, :],
                            op0=ALU.mult, op1=ALU.add), reads=[sin, fg, dcol], writes=[sin])

        gnb = GN.r()
        nh4 = sm(C_NH, 4)

        def obs_of(tt):
            P_ = PS[1 + tt % 2]
            return [P_.r(0, 512), P_.r(512, 1024)]

        def emit_fo(tt):
            obs = obs_of(tt)
            vt = UA.r(GV0 + tt * 512, GV0 + tt * 512 + 512)
            at = UB.r(AALL0 + tt * 512, AALL0 + tt * 512 + 512)
            qall = UA.r(GQ0, GQ0 + 8192)
            kall = UA.r(GK0, GK0 + 8192)
            sloc = SLOC.r((2 * tt) * 512, (2 * tt + 2) * 512)

            def fo(h):
                ins = None
                for hh in range(4):
                    for pr in range(2):
                        n = 2 * tt + pr
                        g = n // 8
                        o = obs[pr].ap[pr * 64:(pr + 1) * 64, hh * 128:(hh + 1) * 128]
                        vv = vt.ap[pr * 64:(pr + 1) * 64, hh * 128:(hh + 1) * 128]
                        for d in range(2):
                            ins = h.matmul(out=o, lhsT=at.ap[pr * 64:(pr + 1) * 64, hh * 128 + d * 64:hh * 128 + d * 64 + 64],
                                           rhs=vv, start=(d == 0), stop=False)
                        ins = h.matmul(out=o, lhsT=qall.ap[:, hh * 2048 + n * 64:hh * 2048 + n * 64 + 64],
                                       rhs=sloc.ap[:, (pr * 4 + hh) * 128:(pr * 4 + hh + 1) * 128], start=False, stop=False)
                        ins = h.matmul(out=o, lhsT=kall.ap[:, hh * 2048 + n * 64:hh * 2048 + n * 64 + 64],
                                       rhs=SIG.ap[:, (g * 4 + hh) * 128:(g * 4 + hh + 1) * 128], start=False, stop=True)
                return ins
            S.op("pe", fo, reads=[vt, at, qall, kall, sloc, SIGf], writes=obs)

        def emit_epi(tt):
            obs = obs_of(tt)
            if tt % 2 == 0:
                osb, sq, sr = TF.r(3636, 4148), TF.r(4148, 4660), TF.r(4660, 5172)
                ss4, ms4, r4 = sm(C_SS2, 4), sm(C_MS2, 4), sm(C_RS2, 4)
            else:
                osb, sq, sr = TF.r(0, 512), TF.r(512, 1024), TF.r(1024, 1536)
                ss4, ms4, r4 = sm(0, 4), sm(4, 4), sm(8, 4)
            for pr in range(2):
                S.op("act", lambda h, ob=obs[pr], pr=pr: h.activation(out=osb.ap[pr * 64:(pr + 1) * 64, :], in_=ob.ap[pr * 64:(pr + 1) * 64, :], func=AF.Copy),
                     reads=[obs[pr]], writes=[osb])
                S.op("act", lambda h, ob=obs[pr], pr=pr: h.activation(out=sq.ap[pr * 64:(pr + 1) * 64, :], in_=ob.ap[pr * 64:(pr + 1) * 64, :], func=AF.Square),
                     reads=[obs[pr]], writes=[sq])
            S.op("dve", lambda h: h.tensor_reduce(out=ss4.ap, in_=sq.ap.rearrange("p (a c) -> p a c", c=128), axis=AX.X, op=ALU.add),
                 reads=[sq], writes=[ss4])
            S.op("pool", lambda h: h.tensor_scalar(out=ms4.ap, in0=ss4.ap, scalar1=1.0 / 128, scalar2=EPS, op0=ALU.mult, op1=ALU.add),
                 reads=[ss4], writes=[ms4])
            S.op("pool", lambda h: h.tensor_tensor(out=r4.ap, in0=ms4.ap, in1=nh4.ap, op=ALU.pow), reads=[ms4, nh4], writes=[r4])
            rt = UA.r(GR0 + tt * 512, GR0 + tt * 512 + 512)
            S.op("act", lambda h: h.activation(out=sr.ap, in_=rt.ap, func=AF.Silu), reads=[rt], writes=[sr])
            S.op("pool", lambda h: h.tensor_tensor(out=sr.ap, in0=sr.ap, in1=gnb.ap, op=ALU.mult), reads=[sr, gnb], writes=[sr])
            S.op("dve", lambda h: h.tensor_tensor(out=osb.ap.rearrange("p (a c) -> p a c", c=128),
                                                  in0=osb.ap.rearrange("p (a c) -> p a c", c=128),
                                                  in1=r4.ap.unsqueeze(2).to_broadcast([128, 4, 128]), op=ALU.mult),
                 reads=[osb, r4], writes=[osb])
            yg = Y.r(tt * 512, tt * 512 + 512)
            S.op("dve", lambda h: h.tensor_tensor(out=yg.ap, in0=osb.ap, in1=sr.ap, op=ALU.mult), reads=[osb, sr], writes=[yg])
            if stage == 3:
                S.op("sp", lambda h: h.dma_start(out=dbg_d.ap()[tt * 128:(tt + 1) * 128, :], in_=yg.ap),
                     reads=[yg], writes=[dkey("dbg", tt)], dma=True)

        emit_fo(0)
        for tt in range(16):
            if tt + 1 < 16:
                emit_fo(tt + 1)
            emit_epi(tt)
        if stage == 3:
            S.op("sp", lambda h: h.nop(), reads=[dkey("dbg", t) for t in range(16)] + [dkey("yna", t) for t in range(16)])
            S.emit(block, esem, dsem)
            return nc

        WO0, YT0 = 0, 8192
        WF1_0, UT0 = 0, 8192
        for q in range(2):
            dst = UB.r(WO0 + q * 4096, WO0 + (q + 1) * 4096)
            S.op("sp", lambda h, dst=dst, q=q: h.dma_start(
                out=dst.ap.rearrange("p (c n) -> p c n", n=1024),
                in_=woutb_d.ap()[q * 512:(q + 1) * 512, :].rearrange("(c p) n -> p c n", p=128)),
                reads=[dkey("woutb")], writes=[dst], dma=True)
        for q in range(8):
            dst = UA.r(q * 4096, (q + 1) * 4096)
            S.op("sp", lambda h, dst=dst, q=q: h.dma_start(
                out=dst.ap.rearrange("p (c n) -> p c n", n=1024),
                in_=wff2b_d.ap()[q * 512:(q + 1) * 512, :].rearrange("(c p) n -> p c n", p=128)),
                reads=[dkey("wff2b", q // 2)], writes=[dst], dma=True)
        load_const(G.r(), gff_d.ap().to_broadcast([128, 1024]))
        gfin = GF.r()
        load_const(gfin, gfin_d.ap().to_broadcast([128, 1024]))
        gff = G.r()
        wo = UB.r(WO0, WO0 + 8192)
        wf2 = UA.r(0, 32768)
        gjunk = XT.r(5120, 6144)

        def h1_of(grp, jt):
            sl = (grp % 2) * 2 + jt
            return TF.r(1024 + sl * 1024, 2048 + sl * 1024)

        def n2_of(grp):
            return XT.r((grp % 2) * 2048, (grp % 2) * 2048 + 2048)

        def pro_a(grp, jt):
            tt = grp * 2 + jt
            ynas = Y.r(8192 + jt * 512, 8192 + jt * 512 + 512)
            S.op("sp", lambda h, ynas=ynas, tt=tt: h.dma_start(out=ynas.ap, in_=yna_d.ap()[tt * 128:(tt + 1) * 128, :]),
                 reads=[dkey("yna", tt)], writes=[ynas], dma=True)
            xt = TF.r(0, 1024)
            S.op("sp", lambda h, xt=xt, tt=tt: h.dma_start(out=xt.ap, in_=x_d.ap()[(tt + 2) * 128:(tt + 3) * 128, :]),
                 writes=[xt], dma=True)
            yg = Y.r(tt * 512, tt * 512 + 512)
            tb = TBANK
            idn = IDN.r()

            def fty(h, ynas=ynas, yg=yg, tb=tb, idn=idn):
                ins = None
                for k in range(8):
                    src = ynas.ap[:, k * 128:(k + 1) * 128] if k < 4 else yg.ap[:, (k - 4) * 128:(k - 3) * 128]
                    ins = h.transpose(out=tb.ap.bitcast(BF16)[:, k * 128:(k + 1) * 128], in_=src, identity=idn.ap)
                return ins
            S.op("pe", fty, reads=[ynas, yg, idn], writes=[tb])
            yT = UB.r(YT0, YT0 + 1024)
            S.op("act", lambda h, yT=yT, tb=tb: h.activation(out=yT.ap, in_=tb.ap.bitcast(BF16), func=AF.Copy), reads=[tb], writes=[yT])
            h1 = h1_of(grp, jt)
            for cg in range(2):
                b = nbank()
                mm_acc(b, [(yT.ap[:, k * 128:(k + 1) * 128], wo.ap[:, k * 1024 + cg * 512:k * 1024 + cg * 512 + 512]) for k in range(8)],
                       [yT, wo])
                S.op("dve", lambda h, h1=h1, b=b, xt=xt, cg=cg: h.tensor_tensor(out=h1.ap[:, cg * 512:(cg + 1) * 512], in0=b.ap,
                                                                               in1=xt.ap[:, cg * 512:(cg + 1) * 512], op=ALU.add),
                     reads=[b, xt], writes=[h1])
            rs = norm_stats(h1, C_RS2 + jt, junk=gjunk, k=jt)
            xn = XN.r((jt % 2) * 512, (jt % 2) * 512 + 512) if False else XN2[jt]
            scale_to_bf16(h1, rs, gff, xn)

        def pro_b(grp, jt):
            xn = XN2[jt]
            n2slot = n2_of(grp).v(lambda a, jt=jt: a.rearrange("p (k t) -> p k t", k=8)[:, :, jt * 128:(jt + 1) * 128])
            transpose8(xn, n2slot)

        pro_a(0, 0)
        pro_b(0, 0)
        pro_a(0, 1)
        pro_b(0, 1)
        wf1_ctr = 0

        def load_wf1(grp, fb):
            nonlocal_ctr = wf1_state[0]
            wb = UC.r(WF1_0 + (nonlocal_ctr % 2) * 4096, WF1_0 + (nonlocal_ctr % 2) * 4096 + 4096)
            wf1_state[0] += 1
            S.op("sp", lambda h, wb=wb, fb=fb: h.dma_start(
                out=wb.ap.rearrange("p (c n) -> p c n", n=512),
                in_=wff1b_d.ap()[:, fb * 512:(fb + 1) * 512].rearrange("(c p) n -> p c n", p=128)),
                reads=[dkey("wff1b", fb // 2)], writes=[wb], dma=True)
            return wb
        wf1_state = [0]
        pending = load_wf1(0, 0)
        for grp in range(8):
            n2T = n2_of(grp)
            for fb in range(8):
                wb = pending
                if fb < 7:
                    pending = load_wf1(grp, fb + 1)
                elif grp < 7:
                    pending = load_wf1(grp + 1, 0)
                for fl in range(4):
                    f = fb * 4 + fl
                    b = nbank()
                    o = Reg(b.ap[:, 0:256], b.keys)
                    mm_acc(o, [(wb.ap[:, k * 512 + fl * 128:k * 512 + fl * 128 + 128], n2T.ap[:, k * 256:(k + 1) * 256]) for k in range(8)],
                           [wb, n2T])
                    rl = XT.r(4096 + (f % 2) * 256, 4096 + (f % 2) * 256 + 256)
                    S.op("act", lambda h, rl=rl, o=o: h.activation(out=rl.ap, in_=o.ap, func=AF.Relu), reads=[o], writes=[rl])
                    ut = UC.r(UT0 + f * 256, UT0 + f * 256 + 256)
                    S.op("dve", lambda h, ut=ut, o=o, rl=rl: h.scalar_tensor_tensor(out=ut.ap, in0=o.ap, scalar=0.0, in1=rl.ap,
                                                                                   op0=ALU.max, op1=ALU.mult), reads=[o, rl], writes=[ut])
                if grp < 7:
                    if fb == 0:
                        pro_a(grp + 1, 0)
                    elif fb == 2:
                        pro_b(grp + 1, 0)
                    elif fb == 3:
                        pro_a(grp + 1, 1)
                    elif fb == 5:
                        pro_b(grp + 1, 1)
            utall = UC.r(UT0, UT0 + 8192)
            for jt in range(2):
                tt = grp * 2 + jt
                h1 = h1_of(grp, jt)
                hf = TF.r(5120, 6144)
                for cg in range(2):
                    b = nbank()
                    mm_acc(b, [(utall.ap[:, f * 256 + jt * 128:f * 256 + jt * 128 + 128], wf2.ap[:, f * 1024 + cg * 512:f * 1024 + cg * 512 + 512])
                               for f in range(32)], [utall, wf2])
                    S.op("dve", lambda h, hf=hf, b=b, h1=h1, cg=cg: h.tensor_tensor(out=hf.ap[:, cg * 512:(cg + 1) * 512], in0=b.ap,
                                                                                   in1=h1.ap[:, cg * 512:(cg + 1) * 512], op=ALU.add),
                         reads=[b, h1], writes=[hf])
                rs = norm_stats(hf, C_RS2 + 2 + jt, junk=gjunk, k=2 + jt)
                ho = h1
                S.op("dve", lambda h, hf=hf, ho=ho, rs=rs: h.scalar_tensor_tensor(out=ho.ap, in0=hf.ap, scalar=rs.ap, in1=gfin.ap,
                                                                                  op0=ALU.mult, op1=ALU.mult), reads=[hf, rs, gfin], writes=[ho])
                S.op("sp", lambda h, ho=ho, tt=tt: h.dma_start(out=out_d.ap()[tt * 128:(tt + 1) * 128, :], in_=ho.ap),
                     reads=[ho], writes=[dkey("out", tt)], dma=True)
        S.op("sp", lambda h: h.nop(), reads=[dkey("out", t) for t in range(16)])
        S.emit(block, esem, dsem)
    return nc


NEG = -30000.0


def _bias_tables(rpb, seg):
    H = 8
    kc = np.arange(64)
    qc = np.arange(64)
    cs = np.clip(qc - 8, 0, 48)
    inwin = (kc[:, None] >= cs[None, :]) & (kc[:, None] < cs[None, :] + 16)
    dcidx = np.clip(kc[:, None] - qc[None, :], -15, 15) + 15

    def table(i, tiles_j, width):
        out = np.full((H, 128, width), NEG, np.float32)
        for jj, j in enumerate(tiles_j):
            for kp in range(2):
                kl = 2 * j + kp
                kg = 32 * seg - 4 + kl
                if kg < 0 or kg > 127:
                    continue
                for qp in range(2):
                    ql = 2 * i + qp
                    qg = 32 * seg - 4 + ql
                    rs = min(max(qg - 4, 0), 120)
                    if not (rs <= kg < rs + 8):
                        continue
                    dr = kg - qg + 7
                    vals = np.where(inwin[None], rpb[:, dr][:, dcidx], NEG)
                    out[:, kp * 64:(kp + 1) * 64, jj * 128 + qp * 64:jj * 128 + (qp + 1) * 64] = vals
        return out

    def table_main():
        out = np.full((H, 128, 640), NEG, np.float32)
        for jj in range(5):
            for kp in range(2):
                for qp in range(2):
                    drel = 2 * (jj - 2) + kp - qp
                    if not (-4 <= drel <= 3):
                        continue
                    vals = np.where(inwin[None], rpb[:, drel + 7][:, dcidx], NEG)
                    out[:, kp * 64:(kp + 1) * 64, jj * 128 + qp * 64:jj * 128 + (qp + 1) * 64] = vals
        return out

    main = table_main().transpose(1, 0, 2).reshape(128, 8 * 640)
    edges = []
    for i, tj in ((2, range(0, 6)), (3, range(1, 6)), (16, range(14, 19)), (17, range(14, 20))):
        edges.append(table(i, list(tj), 768).transpose(1, 0, 2).reshape(128, 8 * 768))
    return np.ascontiguousarray(main), np.ascontiguousarray(np.stack(edges))


def _prep(inputs):
    x = np.asarray(inputs["x"], np.float32)
    w_in = np.asarray(inputs["w_in"], np.float32)[0]
    rpb = np.asarray(inputs["na_rpb"], np.float32)[0]
    guf = np.asarray(inputs["gla_gate_up_fwd"], np.float32)[0]
    gub = np.asarray(inputs["gla_gate_up_bwd"], np.float32)[0]
    gbf = np.asarray(inputs["gla_gate_bias_fwd"], np.float32)[0]
    gbb = np.asarray(inputs["gla_gate_bias_bwd"], np.float32)[0]
    w_na = np.ascontiguousarray(w_in[:, 0:1536])
    qg, kg = w_in[:, 1536:1792], w_in[:, 1792:2048]
    qd = np.concatenate([np.concatenate([qg[:, h * 64:(h + 1) * 64]] * 2, axis=1) for h in range(4)], axis=1)
    kd = np.concatenate([np.concatenate([kg[:, h * 64:(h + 1) * 64]] * 2, axis=1) for h in range(4)], axis=1)
    w_g1 = np.ascontiguousarray(np.concatenate([qd, kd, w_in[:, 2048:2560]], axis=1))
    w_g2 = np.ascontiguousarray(np.concatenate([w_in[:, 2560:3072], w_in[:, 3072:3104]], axis=1))
    w_gate = np.zeros((32, 512), np.float32)
    gb = np.zeros((128, 4), np.float32)
    for h in range(4):
        w_gate[0:16, h * 128:h * 128 + 64] = guf[:, h * 64:(h + 1) * 64]
        w_gate[16:32, h * 128 + 64:h * 128 + 128] = gub[:, h * 64:(h + 1) * 64]
        gb[0:64, h] = gbf[h * 64:(h + 1) * 64]
        gb[64:128, h] = gbb[h * 64:(h + 1) * 64]
    j = np.arange(64)[:, None]
    i = np.arange(64)[None, :]
    m1 = np.concatenate([(j <= i), (j > i)], axis=1).astype(np.float32)
    mall = np.concatenate([m1, m1], axis=0)
    rmask = np.ones((128, 512), np.float32)
    rmask[:, 0::64] = 0.0
    rmask8 = np.ones((128, 128), np.float32)
    rmask8[:, 0::8] = 0.0
    ident = np.eye(128, dtype=np.float32)
    gn = np.tile(np.asarray(inputs["gla_norm_g"], np.float32)[0], 4)[None, :]
    common = dict(
        w_na=w_na, w_g1=w_g1, w_g2=w_g2,
        w_out=np.ascontiguousarray(np.asarray(inputs["w_out"], np.float32)[0]),
        w_ff1=np.ascontiguousarray(np.asarray(inputs["w_ff1"], np.float32)[0]),
        w_ff2=np.ascontiguousarray(np.asarray(inputs["w_ff2"], np.float32)[0]),
        g_mix=np.asarray(inputs["ln_mix_g"], np.float32).reshape(1, 1024),
        g_ff=np.asarray(inputs["ln_ff_g"], np.float32).reshape(1, 1024),
        g_fin=np.asarray(inputs["ln_final_g"], np.float32).reshape(1, 1024),
        g_n=np.ascontiguousarray(gn), w_gate=w_gate, gb=gb, mall=mall, rmask=rmask, rmask8=rmask8, ident=ident,
    )
    maps = []
    for c in range(NCORES):
        b, seg = c // 4, c % 4
        xc = np.zeros((2560, 1024), np.float32)
        r0 = 32 * seg - 4
        lo, hi = max(r0, 0), min(r0 + 40, 128)
        xc[(lo - r0) * 64:(hi - r0) * 64] = x[b, lo * 64:hi * 64]
        bm, be = _bias_tables(rpb, seg)
        sel = np.zeros((128, 8), np.float32)
        others = [o for o in range(4) if o != seg]
        for u, o in enumerate(others):
            if o < seg:
                sel[0:64, u] = 1.0
            if o > seg:
                sel[64:128, u] = 1.0
        xo = np.ascontiguousarray(np.concatenate([x[b, o * 2048:(o + 1) * 2048] for o in others], axis=0))
        m = dict(common)
        m.update(x=xc, x_oth=xo, b_main=bm, b_edge=be, sel=sel)
        maps.append(m)
    return maps


_NC_CACHE = {}


def kernel(**inputs):
    maps = _prep(inputs)
    if 4 not in _NC_CACHE:
        _NC_CACHE[4] = build(4)
    res = run_bass_kernel_spmd(_NC_CACHE[4], maps, core_ids=list(range(NCORES)))
    out = np.zeros((2, 8192, 1024), np.float32)
    for c in range(NCORES):
        b, seg = c // 4, c % 4
        out[b, seg * 2048:(seg + 1) * 2048] = np.asarray(res.results[c]["out"], np.float32)
    return out
```

```python
import numpy as np
from contextlib import ExitStack
import concourse.bass as bass
import concourse.mybir as mybir
from concourse.bass_utils import run_bass_kernel_spmd

F32 = mybir.dt.float32
BF16 = mybir.dt.bfloat16
AF = mybir.ActivationFunctionType
ALU = mybir.AluOpType
AX = mybir.AxisListType

NCORES = 8
EPS = 1e-6
PAGE = 128
ENGS = ("sp", "act", "pool", "dve", "pe")
NDSEM = 24


class Reg:
    __slots__ = ("ap", "keys")

    def __init__(self, ap, keys):
        self.ap = ap
        self.keys = keys

    def v(self, fn):
        return Reg(fn(self.ap), self.keys)


class Ten:
    def __init__(self, name, h, ncols, esz):
        self.name = name
        self.h = h
        self.ncols = ncols
        self.esz = esz

    def r(self, c0=0, c1=None, p0=0, p1=128):
        if c1 is None:
            c1 = self.ncols
        b0 = (c0 * self.esz) // PAGE
        b1 = (c1 * self.esz - 1) // PAGE
        keys = tuple((self.name, b) for b in range(b0, b1 + 1))
        return Reg(self.h[p0:p1, c0:c1], keys)


def dkey(name, i=0):
    return Reg(None, ((name, i),))


class Op:
    __slots__ = ("eng", "fn", "deps", "dma", "idx", "eidx", "sig", "tick", "sem", "val", "prev", "kept")


class Sched:
    def __init__(self):
        self.ops = []
        self.last_w = {}
        self.readers = {}

    def op(self, eng, fn, reads=(), writes=(), dma=False):
        o = Op()
        o.eng = eng
        o.fn = fn
        o.dma = dma
        o.idx = len(self.ops)
        o.sig = False
        o.tick = 0
        deps = set()
        lw = self.last_w
        rd = self.readers
        for r in reads:
            for k in r.keys:
                w = lw.get(k)
                if w is not None:
                    deps.add(w)
        for wr in writes:
            for k in wr.keys:
                w = lw.get(k)
                if w is not None:
                    deps.add(w)
                d = rd.get(k)
                if d:
                    deps.update(d.values())
        ekey = ("dma", o.idx) if dma else eng
        for wr in writes:
            for k in wr.keys:
                lw[k] = o.idx
                rd[k] = {}
        for r in reads:
            for k in r.keys:
                d = rd.get(k)
                if d is None:
                    d = rd[k] = {}
                d[ekey] = o.idx
        deps.discard(o.idx)
        o.deps = deps
        self.ops.append(o)
        return o

    def emit(self, block, esem, dsem):
        ops = self.ops
        per = {e: [] for e in ENGS}
        for o in ops:
            o.eidx = len(per[o.eng])
            per[o.eng].append(o)
        for o in ops:
            kept = []
            for d in o.deps:
                p = ops[d]
                if p.dma:
                    kept.append(p)
                elif p.eng != o.eng:
                    p.sig = True
                    kept.append(p)
                elif o.eng in ("act", "dve", "pool") and (o.eidx - p.eidx) <= 2:
                    p.sig = True
                    kept.append(p)
            o.kept = kept
        cnt = {e: 0 for e in ENGS}
        dcnt = [0] * NDSEM
        nd = 0
        nsw = 0
        for o in ops:
            if o.dma:
                if o.eng == "pool":
                    o.sem = 16 + nsw % (NDSEM - 16)
                    nsw += 1
                else:
                    o.sem = nd % 16
                    nd += 1
                o.prev = dcnt[o.sem]
                dcnt[o.sem] += 16
                o.val = dcnt[o.sem]
            elif o.sig:
                cnt[o.eng] += 1
                o.tick = cnt[o.eng]

        def run(e, h):
            waited = {}

            def wait(key, sh, val):
                if waited.get(key, 0) >= val:
                    return
                h.wait_ge(sh, val)
                waited[key] = val

            for o in per[e]:
                for p in o.kept:
                    if p.dma:
                        wait(("d", p.sem), dsem[p.sem], p.val)
                    else:
                        wait(("e", p.eng), esem[p.eng], p.tick)
                if o.dma and o.prev > 0:
                    wait(("d", o.sem), dsem[o.sem], o.prev)
                inst = o.fn(h)
                if o.dma:
                    inst.then_inc(dsem[o.sem], 16)
                elif o.sig:
                    inst.then_inc(esem[o.eng], 1)

        @block.sync
        def _(h):
            run("sp", h)

        @block.scalar
        def _(h):
            run("act", h)

        @block.gpsimd
        def _(h):
            run("pool", h)

        @block.vector
        def _(h):
            run("dve", h)

        @block.tensor
        def _(h):
            run("pe", h)


def build(stage=4):
    nc = bass.Bass("TRN2", target_bir_lowering=False)
    S = Sched()

    def din(name, shape, dt=F32):
        return nc.dram_tensor(name, shape, dt, kind="ExternalInput")

    x_d = din("x", [2560, 1024])
    xoth_d = din("x_oth", [3 * 2048, 1024])
    wna_d = din("w_na", [1024, 1536])
    wg1_d = din("w_g1", [1024, 1536])
    wg2_d = din("w_g2", [1024, 544])
    wout_d = din("w_out", [1024, 1024])
    wff1_d = din("w_ff1", [1024, 4096])
    wff2_d = din("w_ff2", [4096, 1024])
    gmix_d = din("g_mix", [1, 1024])
    gff_d = din("g_ff", [1, 1024])
    gfin_d = din("g_fin", [1, 1024])
    gn_d = din("g_n", [1, 512])
    bmain_d = din("b_main", [128, 8 * 640])
    bedge_d = din("b_edge", [4, 128, 8 * 768])
    wgate_d = din("w_gate", [32, 512])
    gb_d = din("gb", [128, 4])
    sel_d = din("sel", [128, 8])
    mall_d = din("mall", [128, 128])
    rmask_d = din("rmask", [128, 512])
    rmask8_d = din("rmask8", [128, 128])
    ident_d = din("ident", [128, 128])
    out_d = nc.dram_tensor("out", [2048, 1024], F32, kind="ExternalOutput")
    yna_d = nc.dram_tensor("yna_scr", [2048, 512], BF16)
    wff1b_d = nc.dram_tensor("wff1_bf", [1024, 4096], BF16)
    wff2b_d = nc.dram_tensor("wff2_bf", [4096, 1024], BF16)
    woutb_d = nc.dram_tensor("wout_bf", [1024, 1024], BF16)
    dbg_d = None
    if stage == 2:
        dbg_d = nc.dram_tensor("dbg", [2048, 512], BF16, kind="ExternalOutput")
    if stage == 3:
        dbg_d = nc.dram_tensor("dbg", [2048, 512], BF16, kind="ExternalOutput")

    with ExitStack() as es:
        def sb(name, cols, dt, parts=128):
            h = es.enter_context(nc.sbuf_tensor(name, [parts, cols], dt))
            return Ten(name, h, cols, 4 if dt == F32 else 2)

        def ps(name):
            h = es.enter_context(nc.psum_tensor(name, [128, 1024], F32))
            return Ten(name, h, 1024, 4)

        UA = sb("UA", 34816, BF16)
        UB = sb("UB", 12288, BF16)
        UC = sb("UC", 16384, BF16)
        Y = sb("Y", 16 * 512 + 1024, BF16)
        XT = sb("XT", 8192, BF16)
        XN = sb("XN", 1024, BF16)
        XNB = sb("XNB", 1024, BF16)
        GF = sb("GF", 1024, F32)
        TF = sb("TF", 6656, F32)
        G = sb("G", 1024, F32)
        GN = sb("GN", 512, F32)
        RM = sb("RM", 512, F32)
        RM8 = sb("RM8", 128, F32)
        SM = sb("SM", 1024, F32)
        MALL = sb("MALL", 128, BF16)
        IDN = sb("IDN", 128, BF16)
        WG = sb("WG", 512, BF16)
        PS = [ps(f"PS{i}") for i in range(4)]
        esem = {e: es.enter_context(nc.semaphore("es_" + e)) for e in ENGS}
        dsem = [es.enter_context(nc.semaphore(f"ds{i}")) for i in range(NDSEM)]
        block = es.enter_context(nc.Block())

        XN2 = [XN.r(), XNB.r()]
        banks = [PS[i // 2].r((i % 2) * 512, (i % 2) * 512 + 512) for i in range(8)]
        bank_ctr = [0]

        NB = [6]

        def nbank():
            b = banks[bank_ctr[0] % NB[0]]
            bank_ctr[0] += 1
            return b

        TBANK = banks[7]

        def sm(c0, n=1):
            return SM.r(c0, c0 + n)
        C_SS, C_MS, C_RS, C_NH = 0, 24, 48, 72
        C_SS2, C_MS2, C_RS2 = 80, 84, 88
        C_GBN, C_SEL, C_DP = 96, 100, 108
        C_TOT, C_DEC, C_INC, C_GG = 128, 256, 384, 512
        C_DECS = 704
        C_GT, C_DG, C_DC, C_RDEN = 640, 656, 672, 680

        def load_const(dst, src_ap, eng="sp"):
            S.op(eng, lambda h: h.dma_start(out=dst.ap, in_=src_ap), writes=[dst], dma=True)

        load_const(G.r(), gmix_d.ap().to_broadcast([128, 1024]))
        load_const(GN.r(), gn_d.ap().to_broadcast([128, 512]))
        load_const(RM.r(), rmask_d.ap())
        load_const(RM8.r(), rmask8_d.ap())
        load_const(sm(C_SEL, 8), sel_d.ap())
        load_const(sm(C_GBN, 4), gb_d.ap())
        S.op("pool", lambda h: h.dma_start(out=IDN.r().ap, in_=ident_d.ap()), writes=[IDN.r()], dma=True)
        S.op("pool", lambda h: h.dma_start(out=MALL.r().ap, in_=mall_d.ap()), writes=[MALL.r()], dma=True)
        S.op("pool", lambda h: h.dma_start(out=WG.r(p1=32).ap, in_=wgate_d.ap()), writes=[WG.r()], dma=True)
        S.op("pool", lambda h: h.memset(sm(C_NH, 4).ap, -0.5), writes=[sm(C_NH, 4)])
        gbn = sm(C_GBN, 4)
        S.op("pool", lambda h: h.tensor_scalar(out=gbn.ap, in0=gbn.ap, scalar1=-1.0, scalar2=None, op0=ALU.mult),
             reads=[gbn], writes=[gbn])

        def load_w(dst_ten, c0, d_t, rows, cols, nsplit=4):
            kc = rows // 128
            per = kc // nsplit
            for s in range(nsplit):
                dst = dst_ten.r(c0 + s * per * cols, c0 + (s + 1) * per * cols)
                src = d_t.ap()[s * per * 128:(s + 1) * per * 128, :].rearrange("(c p) n -> p c n", p=128)
                S.op("pool", lambda h, dst=dst, src=src: h.dma_start(
                    out=dst.ap.rearrange("p (c n) -> p c n", n=cols), in_=src), writes=[dst], dma=True)

        def norm_stats(src, col, junk=None, k=0):
            if junk is None:
                junk = TF.r(5376, 6400)
            ss = sm(C_SS2 + k)
            S.op("dve", lambda h: h.scalar_tensor_tensor(out=junk.ap, in0=src.ap, scalar=1.0, in1=src.ap,
                                                         op0=ALU.mult, op1=ALU.mult, accum_out=ss.ap),
                 reads=[src], writes=[junk, ss])
            ms = sm(C_MS2 + k)
            S.op("pool", lambda h: h.tensor_scalar(out=ms.ap, in0=ss.ap, scalar1=1.0 / 1024, scalar2=EPS,
                                                   op0=ALU.mult, op1=ALU.add), reads=[ss], writes=[ms])
            rs = sm(col)
            nh = sm(C_NH)
            S.op("pool", lambda h: h.tensor_tensor(out=rs.ap, in0=ms.ap, in1=nh.ap, op=ALU.pow),
                 reads=[ms, nh], writes=[rs])
            return rs

        def scale_to_bf16(src, rs, gten, dst):
            S.op("dve", lambda h: h.scalar_tensor_tensor(out=dst.ap, in0=src.ap, scalar=rs.ap, in1=gten.ap,
                                                         op0=ALU.mult, op1=ALU.mult),
                 reads=[src, rs, gten], writes=[dst])

        def transpose8(src_bf, dst_fn, tb=None):
            if tb is None:
                tb = TBANK
            idn = IDN.r()

            def f(h):
                i = None
                for k in range(8):
                    i = h.transpose(out=tb.ap.bitcast(BF16)[:, k * 128:(k + 1) * 128],
                                    in_=src_bf.ap[:, k * 128:(k + 1) * 128], identity=idn.ap)
                return i
            S.op("pe", f, reads=[src_bf, idn], writes=[tb])
            dst = dst_fn
            S.op("act", lambda h: h.activation(out=dst.ap, in_=tb.ap.bitcast(BF16).rearrange("p (k t) -> p k t", k=8),
                                               func=AF.Copy), reads=[tb], writes=[dst])

        def mm_acc(out, pairs, reads):
            def f(h):
                i = None
                n = len(pairs)
                for q, (l, r) in enumerate(pairs):
                    i = h.matmul(out=out.ap, lhsT=l, rhs=r, start=(q == 0), stop=(q == n - 1))
                return i
            S.op("pe", f, reads=reads, writes=[out])

        def xt_slot(j, par=0):
            return XT.r(par * 4096, par * 4096 + 4096).v(lambda a: a.rearrange("p (k t) -> p k t", k=8)[:, :, j * 128:(j + 1) * 128])

        def cast_dram(src_ap, dst_ap, key):
            S.op("pool", lambda h: h.dma_start(out=dst_ap, in_=src_ap), writes=[key], dma=True)
        def run_inproj(tile_ids, load_stats, jobs_of_group, after_group0=None):
            n = len(tile_ids)
            rs = {0: load_stats(tile_ids[0])}

            def tile_work(i):
                t = tile_ids[i]
                xt = TF.r((t % 2) * 1024, (t % 2) * 1024 + 1024)
                xn = XN2[t % 2]
                scale_to_bf16(xt, rs[i], gmix, xn)
                if i + 1 < n:
                    rs[i + 1] = load_stats(tile_ids[i + 1])
                transpose8(xn, xt_slot(i % 4, (i // 4) % 2), tb=banks[6 + t % 2])
            for i in range(4):
                tile_work(i)
            ng = n // 4
            for g in range(ng):
                xtr = XT.r((g % 2) * 4096, (g % 2) * 4096 + 4096)
                jobs = jobs_of_group(g, xtr)
                nj_ = len(jobs)
                for c in range(4):
                    if g + 1 < ng:
                        tile_work(4 * (g + 1) + c)
                    for job in jobs[c * nj_ // 4:(c + 1) * nj_ // 4]:
                        job()
                if g == 0 and after_group0 is not None:
                    after_group0()

        load_w(UB, 0, wna_d, 1024, 1536)
        WNA = UB
        QT0, KT0, VA0 = 0, 8192, 18432
        va_all = UA.r(VA0, VA0 + 20 * 520)
        S.op("pool", lambda h: h.memset(va_all.ap.rearrange("p (t e) -> p t e", e=65)[:, :, 64:65], 1.0),
             writes=[va_all])
        gmix = G.r()
        EM0, EE0 = 4352, 9472
        EX0, PT0 = 0, 1536
        edge_idx = {2: 0, 3: 1, 16: 2, 17: 3}

        EDGE_Q = ["pool"]
        def load_edge(i):
            for q in range(4):
                stg = TF.r(2304 + (q % 2) * 1536, 2304 + (q % 2) * 1536 + 1536)
                S.op(EDGE_Q[0], lambda h, stg=stg, q=q, i=i: h.dma_start(
                    out=stg.ap, in_=bedge_d.ap()[edge_idx[i], :, q * 1536:(q + 1) * 1536]),
                    writes=[stg], dma=True)
                dst = UC.r(EE0 + q * 1536, EE0 + (q + 1) * 1536)
                S.op("act", lambda h, stg=stg, dst=dst: h.activation(out=dst.ap, in_=stg.ap, func=AF.Exp),
                     reads=[stg], writes=[dst])
        def prep_tables():
            for q in range(4):
                stg = TF.r(2304 + (q % 2) * 1536, 2304 + (q % 2) * 1536 + 1280)
                S.op("pool", lambda h, stg=stg, q=q: h.dma_start(out=stg.ap, in_=bmain_d.ap()[:, q * 1280:(q + 1) * 1280]),
                     writes=[stg], dma=True)
                dst = UC.r(EM0 + q * 1280, EM0 + (q + 1) * 1280)
                S.op("act", lambda h, stg=stg, dst=dst: h.activation(out=dst.ap, in_=stg.ap, func=AF.Exp),
                     reads=[stg], writes=[dst])
            load_edge(2)
            EDGE_Q[0] = "sp"
        def load_stats_A(t):
            xt = TF.r((t % 2) * 1024, (t % 2) * 1024 + 1024)
            S.op("sp", lambda h, xt=xt, t=t: h.dma_start(out=xt.ap, in_=x_d.ap()[t * 128:(t + 1) * 128, :]),
                 writes=[xt], dma=True)
            return norm_stats(xt, C_RS + t, k=t % 2)

        def jobs_A(g, xtr):
            tok0 = g * 512
            wr = WNA.r()
            lo, hi = max(tok0, 256), min(tok0 + 512, 2304)
            jobs = []
            for f in range(4):
                def jq(f=f):
                    b = nbank()
                    o = Reg(b.ap[:, 0:hi - lo], b.keys)
                    mm_acc(o, [(wr.ap[:, k * 1536 + f * 128:k * 1536 + f * 128 + 128],
                                xtr.ap[:, k * 512 + lo - tok0:k * 512 + hi - tok0]) for k in range(8)], [wr, xtr])
                    dst = UA.r(QT0 + f * 2048 + lo - 256, QT0 + f * 2048 + hi - 256)
                    S.op("act", lambda h: h.activation(out=dst.ap, in_=o.ap, func=AF.Copy), reads=[o], writes=[dst])

                def jk(f=f):
                    b = nbank()
                    mm_acc(b, [(wr.ap[:, k * 1536 + 512 + f * 128:k * 1536 + 512 + f * 128 + 128],
                                xtr.ap[:, k * 512:(k + 1) * 512]) for k in range(8)], [wr, xtr])
                    dst = UA.r(KT0 + f * 2560 + tok0, KT0 + f * 2560 + tok0 + 512)
                    S.op("act", lambda h: h.activation(out=dst.ap, in_=b.ap, func=AF.Copy), reads=[b], writes=[dst])
                jobs += [jq, jk]
            for jj in range(4):
                def jv(jj=jj):
                    tt = g * 4 + jj
                    b = nbank()
                    mm_acc(b, [(xtr.ap[:, k * 512 + jj * 128:k * 512 + jj * 128 + 128],
                                wr.ap[:, k * 1536 + 1024:k * 1536 + 1536]) for k in range(8)], [wr, xtr])
                    dst = UA.r(VA0 + tt * 520, VA0 + tt * 520 + 520)
                    S.op("dve", lambda h: h.tensor_copy(
                        out=dst.ap.rearrange("p (e c) -> p e c", c=65)[:, :, 0:64],
                        in_=b.ap.rearrange("p (e c) -> p e c", c=64)), reads=[b], writes=[dst])
                jobs.append(jv)
            return jobs
        run_inproj(list(range(20)), load_stats_A, jobs_A, after_group0=prep_tables)

        def tiles_of(i):
            if i == 2:
                return list(range(0, 6))
            if i == 17:
                return list(range(14, 20))
            return list(range(i - 2, i + 3))

        items = [(i, hd) for i in range(2, 18) for hd in range(8)]

        def st_of(idx, i):
            return PS[idx % 2].r(0, len(tiles_of(i)) * 128)

        def emit_qk(idx):
            i, hd = items[idx]
            tiles_j = tiles_of(i)
            f, pb = hd // 2, (hd % 2) * 64
            st = st_of(idx, i)
            kreg = UA.r(KT0 + f * 2560, KT0 + (f + 1) * 2560)
            qreg = UA.r(QT0 + f * 2048 + (i - 2) * 128, QT0 + f * 2048 + (i - 1) * 128)

            def fqk(h, st=st, kreg=kreg, qreg=qreg, pb=pb, tiles_j=tiles_j):
                ins = None
                for jj, j in enumerate(tiles_j):
                    ins = h.matmul(out=st.ap[:, jj * 128:(jj + 1) * 128],
                                   lhsT=kreg.ap[pb:pb + 64, j * 128:(j + 1) * 128],
                                   rhs=qreg.ap[pb:pb + 64, :], start=True, stop=True)
                return ins
            S.op("pe", fqk, reads=[kreg, qreg], writes=[st])

        def emit_rest(idx):
            i, hd = items[idx]
            tiles_j = tiles_of(i)
            nj = len(tiles_j)
            ncol = nj * 128
            if hd == 0 and i in (3, 17):
                load_edge(i)
            if hd == 0 and i == 4:
                load_edge(16)
            if hd == 0 and i == 3:
                load_w(UB, 0, wg1_d, 1024, 1536)
                load_w(UC, 0, wg2_d, 1024, 544)
            if hd == 0 and i == 13:
                cast_dram(wout_d.ap(), woutb_d.ap(), dkey("woutb"))
            if hd == 0 and 5 <= i <= 12:
                q = i - 5
                if q < 4:
                    cast_dram(wff1_d.ap()[:, q * 1024:(q + 1) * 1024], wff1b_d.ap()[:, q * 1024:(q + 1) * 1024], dkey("wff1b", q))
                else:
                    q -= 4
                    cast_dram(wff2_d.ap()[q * 1024:(q + 1) * 1024, :], wff2b_d.ap()[q * 1024:(q + 1) * 1024, :], dkey("wff2b", q))
            st = st_of(idx, i)
            ex = XT.r(EX0 + (idx % 2) * 768, EX0 + (idx % 2) * 768 + ncol)
            S.op("act", lambda h, ex=ex, st=st: h.activation(out=ex.ap, in_=st.ap, func=AF.Exp, scale=0.125),
                 reads=[st], writes=[ex])
            if i in edge_idx:
                ee = UC.r(EE0 + hd * 768, EE0 + hd * 768 + ncol)
            else:
                ee = UC.r(EM0 + hd * 640, EM0 + hd * 640 + ncol)
            pt = XT.r(PT0 + (idx % 2) * 768, PT0 + (idx % 2) * 768 + ncol)
            S.op("dve", lambda h, pt=pt, ex=ex, ee=ee: h.tensor_tensor(out=pt.ap, in0=ex.ap, in1=ee.ap, op=ALU.mult),
                 reads=[ex, ee], writes=[pt])
            pvt = PS[2 + (i % 2)]
            pc0 = (hd // 4) * 512 + (hd % 4) * 65
            pv = pvt.r(pc0, pc0 + 65)
            vreg = UA.r(VA0, VA0 + 20 * 520)

            def fpv(h, pv=pv, pt=pt, vreg=vreg, hd=hd, tiles_j=tiles_j, nj=nj):
                ins = None
                for jj, j in enumerate(tiles_j):
                    ins = h.matmul(out=pv.ap, lhsT=pt.ap[:, jj * 128:(jj + 1) * 128],
                                   rhs=vreg.ap[:, j * 520 + hd * 65:j * 520 + hd * 65 + 65],
                                   start=(jj == 0), stop=(jj == nj - 1))
                return ins
            S.op("pe", fpv, reads=[pt, vreg], writes=[pv])
            if hd != 7:
                return
            pvr = pvt.r()
            rden = sm(C_RDEN + (i % 2) * 8, 8)
            pv4 = pvr.v(lambda a: a.rearrange("p (b c) -> p b c", b=2)[:, :, 0:260].rearrange("p b (e c) -> p b e c", c=65))
            S.op("dve", lambda h, rden=rden, pv4=pv4: h.reciprocal(
                out=rden.ap.rearrange("p (b e) -> p b e", b=2), in_=pv4.ap[:, :, :, 64]),
                reads=[pvr], writes=[rden])
            ystg = Y.r(8192 + (i % 2) * 512, 8192 + (i % 2) * 512 + 512)
            S.op("dve", lambda h, ystg=ystg, pv4=pv4, rden=rden: h.tensor_tensor(
                out=ystg.ap.rearrange("p (b e c) -> p b e c", b=2, e=4), in0=pv4.ap[:, :, :, 0:64],
                in1=rden.ap.rearrange("p (b e) -> p b e", b=2).unsqueeze(3).to_broadcast([128, 2, 4, 64]),
                op=ALU.mult), reads=[pvr, rden], writes=[ystg])
            tt = i - 2
            S.op("sp", lambda h, ystg=ystg, tt=tt: h.dma_start(out=yna_d.ap()[tt * 128:(tt + 1) * 128, :], in_=ystg.ap),
                 reads=[ystg], writes=[dkey("yna", tt)], dma=True)
            if stage == 2:
                S.op("sp", lambda h, ystg=ystg, tt=tt: h.dma_start(out=dbg_d.ap()[tt * 128:(tt + 1) * 128, :], in_=ystg.ap),
                     reads=[ystg], writes=[dkey("dbg", tt)], dma=True)

        emit_qk(0)
        for idx in range(len(items)):
            if idx + 1 < len(items):
                emit_qk(idx + 1)
            emit_rest(idx)

        if stage == 2:
            S.op("sp", lambda h: h.nop(), reads=[dkey("dbg", t) for t in range(16)] + [dkey("yna", t) for t in range(16)])
            S.emit(block, esem, dsem)
            return nc

        GQ0, GK0, GV0, GR0, GZ0 = 0, 8192, 16384, 24576, 32768
        FSf = Y.r(0, 3120).v(lambda a: a.bitcast(F32))
        SINP = Y.r(0, 1024).v(lambda a: a.bitcast(F32))
        def emit_fold():
            SIN = TF.r(2100, 2612)
            S.op("pool", lambda h: h.memset(SIN.ap, 0.0), writes=[SIN])
            for d, order in ((0, (0, 1, 2)), (1, (2, 1, 0))):
                p0, p1 = d * 64, d * 64 + 64
                for u in order:
                    dp = sm(C_DP, 4)
                    selu = sm(C_SEL + u)
                    S.op("dve", lambda h, dp=dp, u=u, selu=selu, p0=p0, p1=p1: h.tensor_scalar(
                        out=dp.ap[p0:p1, :], in0=FSf.ap[p0:p1, u * 520 + 512:u * 520 + 516], scalar1=-1.0, scalar2=selu.ap[p0:p1, :],
                        op0=ALU.add, op1=ALU.mult), reads=[FSf, selu], writes=[dp])
                    S.op("pool", lambda h, dp=dp, p0=p0, p1=p1: h.tensor_scalar(out=dp.ap[p0:p1, :], in0=dp.ap[p0:p1, :], scalar1=1.0, scalar2=None, op0=ALU.add),
                         reads=[dp], writes=[dp])
                    fx = TF.r(1032, 1544)
                    S.op("dve", lambda h, fx=fx, u=u, selu=selu, p0=p0, p1=p1: h.tensor_scalar(
                        out=fx.ap[p0:p1, :], in0=FSf.ap[p0:p1, u * 520:u * 520 + 512], scalar1=selu.ap[p0:p1, :], scalar2=None,
                        op0=ALU.mult), reads=[FSf, selu], writes=[fx])
                    for hh in range(4):
                        S.op("dve", lambda h, fx=fx, dp=dp, hh=hh, p0=p0, p1=p1: h.scalar_tensor_tensor(
                            out=SIN.ap[p0:p1, hh * 128:(hh + 1) * 128], in0=SIN.ap[p0:p1, hh * 128:(hh + 1) * 128], scalar=dp.ap[p0:p1, hh:hh + 1],
                            in1=fx.ap[p0:p1, hh * 128:(hh + 1) * 128], op0=ALU.mult, op1=ALU.add), reads=[SIN, fx, dp], writes=[SIN])
            S.op("act", lambda h: h.activation(out=SINP.ap, in_=SIN.ap, func=AF.Copy), reads=[SIN, FSf], writes=[SINP])

        for sg, own in ((0, False), (1, False), (2, False), (3, True)):
            if own:
                emit_fold()
            def load_stats_D(t, sg=sg, own=own):
                xt = TF.r((t % 2) * 1024, (t % 2) * 1024 + 1024)
                if own:
                    S.op("sp", lambda h, xt=xt, t=t: h.dma_start(out=xt.ap, in_=x_d.ap()[t * 128:(t + 1) * 128, :]),
                         writes=[xt], dma=True)
                    return sm(C_RS + t)
                S.op("sp", lambda h, xt=xt, t=t, sg=sg: h.dma_start(out=xt.ap, in_=xoth_d.ap()[sg * 2048 + (t - 2) * 128:sg * 2048 + (t - 1) * 128, :]),
                     writes=[xt], dma=True)
                return norm_stats(xt, C_RS2 + t % 2, junk=UC.r(8192, 9216), k=t % 2)

            def jobs_D(g, xtr, own=own):
                w1 = UB.r()
                w2 = UC.r(0, 8 * 544)
                jobs = []
                for hh in range(4):
                    for which, base in (((0, GQ0), (1, GK0)) if own else ((1, GK0),)):
                        def jqk(hh=hh, which=which, base=base):
                            b = nbank()
                            c0 = which * 512 + hh * 128
                            mm_acc(b, [(w1.ap[:, k * 1536 + c0:k * 1536 + c0 + 128], xtr.ap[:, k * 512:(k + 1) * 512])
                                       for k in range(8)], [w1, xtr])
                            dst = UA.r(base + hh * 2048 + g * 512, base + hh * 2048 + g * 512 + 512)
                            sc = 0.125 if which == 0 else 1.0
                            S.op("act", lambda h: h.activation(out=dst.ap, in_=b.ap, func=AF.Copy, scale=sc), reads=[b], writes=[dst])
                        jobs.append(jqk)

                def jz():
                    b = nbank()
                    bz = Reg(b.ap[0:32, :], b.keys)
                    mm_acc(bz, [(w2.ap[:, k * 544 + 512:k * 544 + 544], xtr.ap[:, k * 512:(k + 1) * 512]) for k in range(8)], [w2, xtr])
                    dst = UA.r(GZ0 + g * 512, GZ0 + g * 512 + 512, p1=32)
                    S.op("act", lambda h: h.activation(out=dst.ap, in_=bz.ap, func=AF.Copy), reads=[bz], writes=[dst])
                jobs.append(jz)
                for jj in range(4):
                    def jv(jj=jj):
                        tt = g * 4 + jj
                        b = nbank()
                        mm_acc(b, [(xtr.ap[:, k * 512 + jj * 128:k * 512 + jj * 128 + 128], w1.ap[:, k * 1536 + 1024:k * 1536 + 1536])
                                   for k in range(8)], [w1, xtr])
                        dst = UA.r(GV0 + tt * 512, GV0 + tt * 512 + 512)
                        S.op("dve", lambda h: h.tensor_copy(out=dst.ap, in_=b.ap), reads=[b], writes=[dst])
                    jobs.append(jv)
                    if own:
                        def jr(jj=jj):
                            tt = g * 4 + jj
                            b = nbank()
                            mm_acc(b, [(xtr.ap[:, k * 512 + jj * 128:k * 512 + jj * 128 + 128], w2.ap[:, k * 544:k * 544 + 512])
                                       for k in range(8)], [w2, xtr])
                            dst = UA.r(GR0 + tt * 512, GR0 + tt * 512 + 512)
                            S.op("act", lambda h: h.activation(out=dst.ap, in_=b.ap, func=AF.Copy), reads=[b], writes=[dst])
                        jobs.append(jr)
                return jobs
            run_inproj(list(range(2, 18)), load_stats_D, jobs_D)
            AALL0, KDT0 = 0, 8192
            SLOC = UC
            FG0 = 3328
            mall = MALL.r()
            rmask = RM.r()
            wg = WG.r(p1=32)
            ONES = TF.r(5632, 6144)
            if sg == 0:
                S.op("pool", lambda h: h.memset(ONES.ap, 1.0), writes=[ONES])
            if own:
                for zt in (UC.r(), XN2[0], XN2[1], GF.r()):
                    S.op("pool", lambda h, zt=zt: h.memset(zt.ap, 0.0), writes=[zt])
            def temps(st_):
                if st_ == 0:
                    return dict(sp=TF.r(0, 512), cf=TF.r(512, 1024), d=TF.r(1024, 1536), eB=TF.r(1536, 2048),
                                emB=TF.r(2048, 2560), eD=TF.r(2560, 3072), kd=TF.r(3072, 3328).v(lambda a: a.bitcast(BF16)))
                yv = lambda k: Y.r(3120 + 1024 * k, 3120 + 1024 * (k + 1)).v(lambda a: a.bitcast(F32))
                xv = lambda k: XT.r(4096 + 1024 * k, 4096 + 1024 * (k + 1)).v(lambda a: a.bitcast(F32))
                return dict(sp=yv(0), cf=yv(1), d=yv(2), eB=yv(3), emB=xv(0), eD=xv(1), kd=XT.r(6144, 6656))

            def v3(a):
                return a.rearrange("p (c i) -> p c i", i=64)

            def totb(a):
                return a.unsqueeze(2).to_broadcast([a.shape[0], 8, 64])

            def stage_a(g, hh, part, own=own):
                gh = g * 4 + hh
                T = temps(gh % 2)
                t_sp, t_cf, t_d, t_eB, t_emB, t_eD, kd = T["sp"], T["cf"], T["d"], T["eB"], T["emB"], T["eD"], T["kd"]
                if part == 1:
                    zr = UA.r(GZ0 + g * 512, GZ0 + g * 512 + 512, p1=32)
                    xg = banks[4 + gh % 2]
                    mm_acc(xg, [(wg.ap[:, hh * 128:(hh + 1) * 128], zr.ap)], [wg, zr])
                    gbh = sm(C_GBN + hh)
                    S.op("act", lambda h: h.activation(out=t_sp.ap, in_=xg.ap, func=AF.Exp, scale=-1.0, bias=gbh.ap),
                         reads=[xg, gbh], writes=[t_sp])
                    S.op("act", lambda h: h.activation(out=t_sp.ap, in_=t_sp.ap, func=AF.Ln, bias=1.0, scale=1.0),
                         reads=[t_sp], writes=[t_sp])
                kreg = UA.r(GK0 + hh * 2048 + g * 512, GK0 + hh * 2048 + g * 512 + 512)
                if not own:
                    if part == 1:
                        S.op("dve", lambda h: h.tensor_tensor_scan(out=t_cf.ap, data0=ONES.ap, data1=t_sp.ap, initial=0.0,
                                                                   op0=ALU.mult, op1=ALU.add), reads=[ONES, t_sp], writes=[t_cf])
                        gtc = sm(C_GT + hh * 4 + g)
                        S.op("dve", lambda h: h.tensor_copy(out=gtc.ap, in_=t_cf.ap[:, 511:512]), reads=[t_cf], writes=[gtc])
                        S.op("dve", lambda h: h.tensor_scalar(out=t_d.ap[0:64, :], in0=t_cf.ap[0:64, :], scalar1=-1.0,
                                                              scalar2=gtc.ap[0:64, :], op0=ALU.mult, op1=ALU.add),
                             reads=[t_cf, gtc], writes=[t_d])
                        S.op("dve", lambda h: h.tensor_tensor(out=t_d.ap[64:128, :], in0=t_cf.ap[64:128, :],
                                                              in1=t_sp.ap[64:128, :], op=ALU.subtract), reads=[t_cf, t_sp], writes=[t_d])
                    if part == 1:
                        return
                    S.op("act", lambda h: h.activation(out=t_eD.ap, in_=t_d.ap, func=AF.Exp, scale=-1.0 / 16), reads=[t_d], writes=[t_eD])
                    S.op("dve", lambda h: h.tensor_tensor(out=kd.ap, in0=kreg.ap, in1=t_eD.ap, op=ALU.mult),
                         reads=[kreg, t_eD], writes=[kd])
                    return
                tot = sm(C_TOT + hh * 32 + g * 8, 8)
                if part == 1:
                    S.op("dve", lambda h: h.tensor_tensor_scan(out=t_cf.ap, data0=rmask.ap, data1=t_sp.ap, initial=0.0,
                                                               op0=ALU.mult, op1=ALU.add), reads=[rmask, t_sp], writes=[t_cf])
                    S.op("dve", lambda h: h.tensor_copy(out=tot.ap, in_=t_cf.ap[:, 63:512:64]), reads=[t_cf], writes=[tot])
                    S.op("dve", lambda h: h.tensor_tensor(out=v3(t_cf.ap[64:128, :]), in0=totb(tot.ap[64:128, :]),
                                                          in1=v3(t_cf.ap[64:128, :]), op=ALU.subtract), reads=[t_cf, tot], writes=[t_cf])
                    S.op("dve", lambda h: h.tensor_tensor(out=t_cf.ap[64:128, :], in0=t_cf.ap[64:128, :],
                                                          in1=t_sp.ap[64:128, :], op=ALU.add), reads=[t_cf, t_sp], writes=[t_cf])
                    S.op("dve", lambda h: h.tensor_tensor(out=v3(t_d.ap), in0=totb(tot.ap), in1=v3(t_cf.ap), op=ALU.subtract),
                         reads=[t_cf, tot], writes=[t_d])
                    return
                S.op("act", lambda h: h.activation(out=t_eB.ap, in_=t_cf.ap, func=AF.Exp, scale=-1.0 / 16), reads=[t_cf], writes=[t_eB])
                S.op("act", lambda h: h.activation(out=t_emB.ap, in_=t_cf.ap, func=AF.Exp, scale=1.0 / 16), reads=[t_cf], writes=[t_emB])
                S.op("act", lambda h: h.activation(out=t_eD.ap, in_=t_d.ap, func=AF.Exp, scale=-1.0 / 16), reads=[t_d], writes=[t_eD])
                qreg = UA.r(GQ0 + hh * 2048 + g * 512, GQ0 + hh * 2048 + g * 512 + 512)
                S.op("dve", lambda h: h.tensor_tensor(out=kd.ap, in0=kreg.ap, in1=t_eD.ap, op=ALU.mult), reads=[kreg, t_eD], writes=[kd])
                S.op("pool", lambda h: h.tensor_tensor(out=kreg.ap, in0=kreg.ap, in1=t_emB.ap, op=ALU.mult), reads=[kreg, t_emB], writes=[kreg])
                S.op("dve", lambda h: h.tensor_tensor(out=qreg.ap, in0=qreg.ap, in1=t_eB.ap, op=ALU.mult), reads=[qreg, t_eB], writes=[qreg])

            HBS = [TF.r(5376, 6400), G.r()]

            def stage_b(g, hh, own=own):
                gh = g * 4 + hh
                kd = temps(gh % 2)["kd"]
                tb = banks[6 + gh % 2]
                idn = IDN.r()

                def ftr(h):
                    ins = None
                    for ttl in range(4):
                        ins = h.transpose(out=tb.ap.bitcast(BF16)[:, ttl * 128:(ttl + 1) * 128],
                                          in_=kd.ap[:, ttl * 128:(ttl + 1) * 128], identity=idn.ap)
                    return ins
                S.op("pe", ftr, reads=[kd, idn], writes=[tb])
                if own:
                    kdt = XN2[gh % 2]
                    S.op("act", lambda h: h.activation(out=kdt.ap[0:64, 0:512], in_=tb.ap.bitcast(BF16)[0:64, 0:512], func=AF.Copy),
                         reads=[tb], writes=[kdt])
                    S.op("act", lambda h: h.activation(out=kdt.ap[64:128, 512:1024], in_=tb.ap.bitcast(BF16)[64:128, 0:512], func=AF.Copy),
                         reads=[tb], writes=[kdt])
                else:
                    kdt = XN2[gh % 2].v(lambda a: a[:, 0:512])
                    S.op("act", lambda h: h.activation(out=kdt.ap, in_=tb.ap.bitcast(BF16)[:, 0:512], func=AF.Copy), reads=[tb], writes=[kdt])
                vgrp = UA.r(GV0 + g * 4 * 512, GV0 + (g + 1) * 4 * 512)
                fg = TF.r(FG0 + gh * 128, FG0 + gh * 128 + 128)
                if not own:
                    fb_ = banks[gh % 2]
                    fo_ = Reg(fb_.ap[:, 0:128], fb_.keys)
                    mm_acc(fo_, [(kdt.ap[:, ttl * 128:(ttl + 1) * 128],
                                  vgrp.ap[:, ttl * 512 + hh * 128:ttl * 512 + hh * 128 + 128]) for ttl in range(4)], [kdt, vgrp])
                    S.op("act", lambda h: h.activation(out=fg.ap, in_=fo_.ap, func=AF.Copy), reads=[fo_], writes=[fg])
                    return
                qreg = UA.r(GQ0 + hh * 2048 + g * 512, GQ0 + hh * 2048 + g * 512 + 512)
                kreg = UA.r(GK0 + hh * 2048 + g * 512, GK0 + hh * 2048 + g * 512 + 512)
                ab = PS[0].r(0, 256)
                ab2 = PS[0].r(512, 768)

                def fa(h):
                    ins = None
                    for ttl in range(4):
                        for pr in range(2):
                            for d in range(2):
                                c0 = (2 * ttl + pr) * 64
                                dst = ab if d == 0 else ab2
                                ins = h.matmul(out=dst.ap[pr * 64:(pr + 1) * 64, ttl * 64:(ttl + 1) * 64],
                                               lhsT=kreg.ap[d * 64:(d + 1) * 64, c0:c0 + 64],
                                               rhs=qreg.ap[d * 64:(d + 1) * 64, c0:c0 + 64], start=True, stop=True)
                    return ins
                S.op("pe", fa, reads=[kreg, qreg], writes=[ab, ab2])
                adst = UB.r(AALL0 + g * 4 * 512, AALL0 + (g + 1) * 4 * 512)
                for d, src in ((0, ab), (1, ab2)):
                    S.op("dve" if d == 0 else "pool" if False else "dve", lambda h, src=src, d=d: h.tensor_tensor(
                        out=adst.ap.rearrange("p (t x) -> p t x", x=512)[:, :, hh * 128 + d * 64:hh * 128 + d * 64 + 64],
                        in0=src.ap.rearrange("p (t x) -> p t x", x=64),
                        in1=mall.ap[:, d * 64:(d + 1) * 64].unsqueeze(1).to_broadcast([128, 4, 64]), op=ALU.mult),
                        reads=[src, mall], writes=[adst])
                cps = PS[1].r()
                vt4 = UA.r(GV0 + g * 4 * 512, GV0 + (g + 1) * 4 * 512)

                def fc(h):
                    ins = None
                    for c in range(8):
                        ttl, pr = c // 2, c % 2
                        for eh in range(2):
                            for dh in range(2):
                                pos = c if dh == 0 else 7 - c
                                o = cps.ap[dh * 64:(dh + 1) * 64, eh * 512:(eh + 1) * 512].rearrange("p (e c) -> p c e", c=8)[:, pos, :]
                                ins = h.matmul(out=o,
                                               lhsT=kdt.ap[:, pr * 512 + ttl * 128 + dh * 64:pr * 512 + ttl * 128 + dh * 64 + 64],
                                               rhs=vt4.ap[:, ttl * 512 + hh * 128 + eh * 64:ttl * 512 + hh * 128 + eh * 64 + 64],
                                               start=True, stop=True)
                    return ins
                S.op("pe", fc, reads=[kdt, vt4], writes=[cps])
                tot = sm(C_TOT + hh * 32 + g * 8, 8)
                decb = GF.r()
                dbv = decb.ap.rearrange("p (e c) -> p e c", c=8)
                S.op("act", lambda h: h.activation(out=dbv[0:64, :, 1:8], in_=tot.ap[0:64, 1:8].unsqueeze(1).to_broadcast([64, 128, 7]),
                                                   func=AF.Exp, scale=-1.0 / 16), reads=[tot], writes=[decb])
                S.op("act", lambda h: h.activation(out=dbv[64:128, :, 1:8], in_=tot.ap[64:128, 6::-1].unsqueeze(1).to_broadcast([64, 128, 7]),
                                                   func=AF.Exp, scale=-1.0 / 16), reads=[tot], writes=[decb])
                HB = HBS[gh % 2]
                S.op("dve", lambda h: h.tensor_tensor_scan(out=HB.ap, data0=decb.ap, data1=cps.ap, initial=0.0, op0=ALU.mult, op1=ALU.add),
                     reads=[decb, cps], writes=[HB])

            def stage_c(g, hh):
                gh = g * 4 + hh
                HB = HBS[gh % 2]
                fg = TF.r(FG0 + gh * 128, FG0 + gh * 128 + 128)
                sl8 = SLOC.r((g * 8 * 4) * 128, ((g + 1) * 8 * 4) * 128)
                hv = HB.ap.rearrange("p (e c) -> p c e", c=8)
                slv = sl8.ap.rearrange("p (n x) -> p n x", x=512)
                S.op("act", lambda h: h.activation(out=slv[0:64, 1:8, hh * 128:(hh + 1) * 128], in_=hv[0:64, 0:7, :], func=AF.Copy),
                     reads=[HB], writes=[sl8])
                S.op("dve", lambda h: h.tensor_copy(out=slv[64:128, 0:7, hh * 128:(hh + 1) * 128], in_=hv[64:128, 6::-1, :]),
                     reads=[HB], writes=[sl8])
                S.op("act", lambda h: h.activation(out=fg.ap, in_=hv[:, 7, :], func=AF.Copy), reads=[HB], writes=[fg])

            ghs = [(g, hh) for g in range(4) for hh in range(4)]
            stage_a(*ghs[0], 1)
            stage_a(*ghs[1], 1)
            stage_a(*ghs[0], 2)
            for ii in range(16):
                if ii + 2 < 16:
                    stage_a(*ghs[ii + 2], 1)
                if ii + 1 < 16:
                    stage_a(*ghs[ii + 1], 2)
                stage_b(*ghs[ii])
                if own and ii >= 1:
                    stage_c(*ghs[ii - 1])
            if own:
                stage_c(*ghs[15])

            totall = sm(C_TOT, 128)
            gt = sm(C_GT, 16)
            if own:
                S.op("dve", lambda h: h.tensor_reduce(out=gt.ap, in_=totall.ap.rearrange("p (a c) -> p a c", c=8), axis=AX.X, op=ALU.add),
                     reads=[totall], writes=[gt])
            dg = sm(C_DG, 16)
            S.op("act", lambda h: h.activation(out=dg.ap, in_=gt.ap, func=AF.Exp, scale=-1.0 / 16), reads=[gt], writes=[dg])
            if own:
                inc = sm(C_INC, 128)
                rm8 = RM8.r()
                S.op("dve", lambda h: h.tensor_tensor_scan(out=inc.ap, data0=rm8.ap, data1=totall.ap, initial=0.0, op0=ALU.mult, op1=ALU.add),
                     reads=[rm8, totall], writes=[inc])
                S.op("dve", lambda h: h.tensor_tensor(out=inc.ap[0:64, :], in0=inc.ap[0:64, :], in1=totall.ap[0:64, :], op=ALU.subtract),
                     reads=[inc, totall], writes=[inc])
                S.op("dve", lambda h: h.tensor_tensor(out=inc.ap[64:128, :].rearrange("p (a c) -> p a c", c=8),
                                                      in0=gt.ap[64:128, :].unsqueeze(2).to_broadcast([64, 16, 8]),
                                                      in1=inc.ap[64:128, :].rearrange("p (a c) -> p a c", c=8), op=ALU.subtract),
                     reads=[inc, gt], writes=[inc])
                gg = sm(C_GG, 128)
                S.op("act", lambda h: h.activation(out=gg.ap, in_=inc.ap, func=AF.Exp, scale=-1.0 / 16), reads=[inc], writes=[gg])
                for hh in range(4):
                    qreg = UA.r(GQ0 + hh * 2048, GQ0 + (hh + 1) * 2048)
                    kreg = UA.r(GK0 + hh * 2048, GK0 + (hh + 1) * 2048)
                    ggh = sm(C_GG + hh * 32, 32)
                    S.op("dve", lambda h, qreg=qreg, kreg=kreg, ggh=ggh: h.tensor_tensor(
                        out=kreg.ap.rearrange("p (n i) -> p n i", i=64), in0=qreg.ap.rearrange("p (n i) -> p n i", i=64),
                        in1=ggh.ap.unsqueeze(2).to_broadcast([128, 32, 64]), op=ALU.mult), reads=[qreg, ggh], writes=[kreg])

            else:
                PK = TF.r(0, 516)
                for hh in range(4):
                    for d, order in ((0, range(4)), (1, range(3, -1, -1))):
                        p0, p1 = d * 64, d * 64 + 64
                        pk = TF.r(hh * 128, hh * 128 + 128)
                        for q, g in enumerate(order):
                            fg = TF.r(FG0 + (g * 4 + hh) * 128, FG0 + (g * 4 + hh) * 128 + 128)
                            if q == 0:
                                S.op("act", lambda h, pk=pk, fg=fg, p0=p0, p1=p1: h.activation(out=pk.ap[p0:p1, :], in_=fg.ap[p0:p1, :], func=AF.Copy),
                                     reads=[fg], writes=[pk])
                            else:
                                dcol = sm(C_DG + hh * 4 + g)
                                S.op("dve", lambda h, pk=pk, fg=fg, dcol=dcol, p0=p0, p1=p1: h.scalar_tensor_tensor(
                                    out=pk.ap[p0:p1, :], in0=pk.ap[p0:p1, :], scalar=dcol.ap[p0:p1, :], in1=fg.ap[p0:p1, :],
                                    op0=ALU.mult, op1=ALU.add), reads=[pk, fg, dcol], writes=[pk])
                ct = sm(C_DC, 4)
                S.op("dve", lambda h: h.tensor_reduce(out=ct.ap, in_=gt.ap.rearrange("p (a c) -> p a c", c=4), axis=AX.X, op=ALU.add),
                     reads=[gt], writes=[ct])
                pkd = TF.r(512, 516)
                S.op("act", lambda h: h.activation(out=pkd.ap, in_=ct.ap, func=AF.Exp, scale=-1.0 / 16), reads=[ct], writes=[pkd])
                S.op("act", lambda h, sg=sg: h.activation(out=FSf.ap[:, sg * 520:sg * 520 + 516], in_=PK.ap, func=AF.Copy), reads=[PK], writes=[FSf])
        SIGf = XT.r(0, 2048)
        SIG = SIGf
        for hh in range(4):
            for d, order in ((0, range(4)), (1, range(3, -1, -1))):
                p0, p1 = d * 64, d * 64 + 64
                sin = Y.r(hh * 256, hh * 256 + 256).v(lambda a: a.bitcast(F32))
                for q, g in enumerate(order):
                    S.op("act", lambda h, sin=sin, g=g, hh=hh, p0=p0, p1=p1: h.activation(
                        out=SIG.ap[p0:p1, (g * 4 + hh) * 128:(g * 4 + hh + 1) * 128], in_=sin.ap[p0:p1, :], func=AF.Copy),
                        reads=[sin], writes=[SIGf])
                    if q < 3:
                        fg = TF.r(FG0 + (g * 4 + hh) * 128, FG0 + (g * 4 + hh) * 128 + 128)
                        dcol = sm(C_DG + hh * 4 + g)
                        S.op("dve", lambda h, sin=sin, fg=fg, dcol=dcol, p0=p0, p1=p1: h.scalar_tensor_tensor(
                            out=sin.ap[p0:p1, :], in0=sin.ap[p0:p1, :], scalar=dcol.ap[p0:p1, :], in1=fg.ap[p0:p1, :],
                            op0=ALU.mult, op1=ALU.add), reads=[sin, fg, dcol], writes=[sin])

        gnb = GN.r()
        nh4 = sm(C_NH, 4)

        def obs_of(tt):
            P_ = PS[1 + tt % 2]
            return [P_.r(0, 512), P_.r(512, 1024)]

        def emit_fo(tt):
            obs = obs_of(tt)
            vt = UA.r(GV0 + tt * 512, GV0 + tt * 512 + 512)
            at = UB.r(AALL0 + tt * 512, AALL0 + tt * 512 + 512)
            qall = UA.r(GQ0, GQ0 + 8192)
            kall = UA.r(GK0, GK0 + 8192)
            sloc = SLOC.r((2 * tt) * 512, (2 * tt + 2) * 512)

            def fo(h):
                ins = None
                for hh in range(4):
                    for pr in range(2):
                        n = 2 * tt + pr
                        g = n // 8
                        o = obs[pr].ap[pr * 64:(pr + 1) * 64, hh * 128:(hh + 1) * 128]
                        vv = vt.ap[pr * 64:(pr + 1) * 64, hh * 128:(hh + 1) * 128]
                        for d in range(2):
                            ins = h.matmul(out=o, lhsT=at.ap[pr * 64:(pr + 1) * 64, hh * 128 + d * 64:hh * 128 + d * 64 + 64],
                                           rhs=vv, start=(d == 0), stop=False)
                        ins = h.matmul(out=o, lhsT=qall.ap[:, hh * 2048 + n * 64:hh * 2048 + n * 64 + 64],
                                       rhs=sloc.ap[:, (pr * 4 + hh) * 128:(pr * 4 + hh + 1) * 128], start=False, stop=False)
                        ins = h.matmul(out=o, lhsT=kall.ap[:, hh * 2048 + n * 64:hh * 2048 + n * 64 + 64],
                                       rhs=SIG.ap[:, (g * 4 + hh) * 128:(g * 4 + hh + 1) * 128], start=False, stop=True)
                return ins
            S.op("pe", fo, reads=[vt, at, qall, kall, sloc, SIGf], writes=obs)

        def emit_epi(tt):
            obs = obs_of(tt)
            if tt % 2 == 0:
                osb, sq, sr = TF.r(3636, 4148), TF.r(4148, 4660), TF.r(4660, 5172)
                ss4, ms4, r4 = sm(C_SS2, 4), sm(C_MS2, 4), sm(C_RS2, 4)
            else:
                osb, sq, sr = TF.r(0, 512), TF.r(512, 1024), TF.r(1024, 1536)
                ss4, ms4, r4 = sm(0, 4), sm(4, 4), sm(8, 4)
            for pr in range(2):
                S.op("act", lambda h, ob=obs[pr], pr=pr: h.activation(out=osb.ap[pr * 64:(pr + 1) * 64, :], in_=ob.ap[pr * 64:(pr + 1) * 64, :], func=AF.Copy),
                     reads=[obs[pr]], writes=[osb])
                S.op("act", lambda h, ob=obs[pr], pr=pr: h.activation(out=sq.ap[pr * 64:(pr + 1) * 64, :], in_=ob.ap[pr * 64:(pr + 1) * 64, :], func=AF.Square),
                     reads=[obs[pr]], writes=[sq])
            S.op("dve", lambda h: h.tensor_reduce(out=ss4.ap, in_=sq.ap.rearrange("p (a c) -> p a c", c=128), axis=AX.X, op=ALU.add),
                 reads=[sq], writes=[ss4])
            S.op("pool", lambda h: h.tensor_scalar(out=ms4.ap, in0=ss4.ap, scalar1=1.0 / 128, scalar2=EPS, op0=ALU.mult, op1=ALU.add),
                 reads=[ss4], writes=[ms4])
            S.op("pool", lambda h: h.tensor_tensor(out=r4.ap, in0=ms4.ap, in1=nh4.ap, op=ALU.pow), reads=[ms4, nh4], writes=[r4])
            rt = UA.r(GR0 + tt * 512, GR0 + tt * 512 + 512)
            S.op("act", lambda h: h.activation(out=sr.ap, in_=rt.ap, func=AF.Silu), reads=[rt], writes=[sr])
            S.op("pool", lambda h: h.tensor_tensor(out=sr.ap, in0=sr.ap, in1=gnb.ap, op=ALU.mult), reads=[sr, gnb], writes=[sr])
            S.op("dve", lambda h: h.tensor_tensor(out=osb.ap.rearrange("p (a c) -> p a c", c=128),
                                                  in0=osb.ap.rearrange("p (a c) -> p a c", c=128),
                                                  in1=r4.ap.unsqueeze(2).to_broadcast([128, 4, 128]), op=ALU.mult),
                 reads=[osb, r4], writes=[osb])
            yg = Y.r(tt * 512, tt * 512 + 512)
            S.op("dve", lambda h: h.tensor_tensor(out=yg.ap, in0=osb.ap, in1=sr.ap, op=ALU.mult), reads=[osb, sr], writes=[yg])
            if stage == 3:
                S.op("sp", lambda h: h.dma_start(out=dbg_d.ap()[tt * 128:(tt + 1) * 128, :], in_=yg.ap),
                     reads=[yg], writes=[dkey("dbg", tt)], dma=True)

        emit_fo(0)
        for tt in range(16):
            if tt + 1 < 16:
                emit_fo(tt + 1)
            emit_epi(tt)
        if stage == 3:
            S.op("sp", lambda h: h.nop(), reads=[dkey("dbg", t) for t in range(16)] + [dkey("yna", t) for t in range(16)])
            S.emit(block, esem, dsem)
            return nc

        NB[0] = 7
        WO0, YT0 = 0, 8192
        WF1_0, UT0 = 0, 8192
        for q in range(2):
            dst = UB.r(WO0 + q * 4096, WO0 + (q + 1) * 4096)
            S.op("sp", lambda h, dst=dst, q=q: h.dma_start(
                out=dst.ap.rearrange("p (c n) -> p c n", n=1024),
                in_=woutb_d.ap()[q * 512:(q + 1) * 512, :].rearrange("(c p) n -> p c n", p=128)),
                reads=[dkey("woutb")], writes=[dst], dma=True)
        for q in range(8):
            dst = UA.r(q * 4096, (q + 1) * 4096)
            S.op("sp", lambda h, dst=dst, q=q: h.dma_start(
                out=dst.ap.rearrange("p (c n) -> p c n", n=1024),
                in_=wff2b_d.ap()[q * 512:(q + 1) * 512, :].rearrange("(c p) n -> p c n", p=128)),
                reads=[dkey("wff2b", q // 2)], writes=[dst], dma=True)
        load_const(G.r(), gff_d.ap().to_broadcast([128, 1024]))
        gfin = GF.r()
        load_const(gfin, gfin_d.ap().to_broadcast([128, 1024]))
        gff = G.r()
        wo = UB.r(WO0, WO0 + 8192)
        wf2 = UA.r(0, 32768)
        gjunk = XT.r(5120, 6144)

        def h1_of(grp, jt):
            sl = (grp % 2) * 2 + jt
            return TF.r(1024 + sl * 1024, 2048 + sl * 1024)

        def n2_of(grp):
            return XT.r((grp % 2) * 2048, (grp % 2) * 2048 + 2048)

        def pro_a(grp, jt):
            tt = grp * 2 + jt
            ynas = Y.r(8192 + jt * 512, 8192 + jt * 512 + 512)
            S.op("sp", lambda h, ynas=ynas, tt=tt: h.dma_start(out=ynas.ap, in_=yna_d.ap()[tt * 128:(tt + 1) * 128, :]),
                 reads=[dkey("yna", tt)], writes=[ynas], dma=True)
            xt = TF.r(0, 1024)
            S.op("sp", lambda h, xt=xt, tt=tt: h.dma_start(out=xt.ap, in_=x_d.ap()[(tt + 2) * 128:(tt + 3) * 128, :]),
                 writes=[xt], dma=True)
            yg = Y.r(tt * 512, tt * 512 + 512)
            tb = TBANK
            idn = IDN.r()

            def fty(h, ynas=ynas, yg=yg, tb=tb, idn=idn):
                ins = None
                for k in range(8):
                    src = ynas.ap[:, k * 128:(k + 1) * 128] if k < 4 else yg.ap[:, (k - 4) * 128:(k - 3) * 128]
                    ins = h.transpose(out=tb.ap.bitcast(BF16)[:, k * 128:(k + 1) * 128], in_=src, identity=idn.ap)
                return ins
            S.op("pe", fty, reads=[ynas, yg, idn], writes=[tb])
            yT = UB.r(YT0, YT0 + 1024)
            S.op("act", lambda h, yT=yT, tb=tb: h.activation(out=yT.ap, in_=tb.ap.bitcast(BF16), func=AF.Copy), reads=[tb], writes=[yT])
            h1 = h1_of(grp, jt)
            for cg in range(2):
                b = nbank()
                mm_acc(b, [(yT.ap[:, k * 128:(k + 1) * 128], wo.ap[:, k * 1024 + cg * 512:k * 1024 + cg * 512 + 512]) for k in range(8)],
                       [yT, wo])
                S.op("dve", lambda h, h1=h1, b=b, xt=xt, cg=cg: h.tensor_tensor(out=h1.ap[:, cg * 512:(cg + 1) * 512], in0=b.ap,
                                                                               in1=xt.ap[:, cg * 512:(cg + 1) * 512], op=ALU.add),
                     reads=[b, xt], writes=[h1])
            rs = norm_stats(h1, C_RS2 + jt, junk=gjunk, k=jt)
            xn = XN.r((jt % 2) * 512, (jt % 2) * 512 + 512) if False else XN2[jt]
            scale_to_bf16(h1, rs, gff, xn)

        def pro_b(grp, jt):
            xn = XN2[jt]
            n2slot = n2_of(grp).v(lambda a, jt=jt: a.rearrange("p (k t) -> p k t", k=8)[:, :, jt * 128:(jt + 1) * 128])
            transpose8(xn, n2slot)

        pro_a(0, 0)
        pro_b(0, 0)
        pro_a(0, 1)
        pro_b(0, 1)
        wf1_ctr = 0

        def load_wf1(grp, fb):
            nonlocal_ctr = wf1_state[0]
            wb = UC.r(WF1_0 + (nonlocal_ctr % 2) * 4096, WF1_0 + (nonlocal_ctr % 2) * 4096 + 4096)
            wf1_state[0] += 1
            S.op("sp", lambda h, wb=wb, fb=fb: h.dma_start(
                out=wb.ap.rearrange("p (c n) -> p c n", n=512),
                in_=wff1b_d.ap()[:, fb * 512:(fb + 1) * 512].rearrange("(c p) n -> p c n", p=128)),
                reads=[dkey("wff1b", fb // 2)], writes=[wb], dma=True)
            return wb
        wf1_state = [0]
        pending = load_wf1(0, 0)
        for grp in range(8):
            n2T = n2_of(grp)
            for fb in range(8):
                wb = pending
                if fb < 7:
                    pending = load_wf1(grp, fb + 1)
                elif grp < 7:
                    pending = load_wf1(grp + 1, 0)
                for fl in range(4):
                    f = fb * 4 + fl
                    b = nbank()
                    o = Reg(b.ap[:, 0:256], b.keys)
                    mm_acc(o, [(wb.ap[:, k * 512 + fl * 128:k * 512 + fl * 128 + 128], n2T.ap[:, k * 256:(k + 1) * 256]) for k in range(8)],
                           [wb, n2T])
                    rl = XT.r(4096 + (f % 2) * 256, 4096 + (f % 2) * 256 + 256)
                    S.op("act", lambda h, rl=rl, o=o: h.activation(out=rl.ap, in_=o.ap, func=AF.Relu), reads=[o], writes=[rl])
                    ut = UC.r(UT0 + f * 256, UT0 + f * 256 + 256)
                    S.op("dve", lambda h, ut=ut, o=o, rl=rl: h.scalar_tensor_tensor(out=ut.ap, in0=o.ap, scalar=0.0, in1=rl.ap,
                                                                                   op0=ALU.max, op1=ALU.mult), reads=[o, rl], writes=[ut])
                if grp < 7:
                    if fb == 0:
                        pro_a(grp + 1, 0)
                    elif fb == 2:
                        pro_b(grp + 1, 0)
                    elif fb == 3:
                        pro_a(grp + 1, 1)
                    elif fb == 5:
                        pro_b(grp + 1, 1)
            utall = UC.r(UT0, UT0 + 8192)
            for jt in range(2):
                tt = grp * 2 + jt
                h1 = h1_of(grp, jt)
                hf = TF.r(5120, 6144)
                for cg in range(2):
                    b = nbank()
                    mm_acc(b, [(utall.ap[:, f * 256 + jt * 128:f * 256 + jt * 128 + 128], wf2.ap[:, f * 1024 + cg * 512:f * 1024 + cg * 512 + 512])
                               for f in range(32)], [utall, wf2])
                    S.op("dve", lambda h, hf=hf, b=b, h1=h1, cg=cg: h.tensor_tensor(out=hf.ap[:, cg * 512:(cg + 1) * 512], in0=b.ap,
                                                                                   in1=h1.ap[:, cg * 512:(cg + 1) * 512], op=ALU.add),
                         reads=[b, h1], writes=[hf])
                rs = norm_stats(hf, C_RS2 + 2 + jt, junk=gjunk, k=2 + jt)
                ho = h1
                S.op("dve", lambda h, hf=hf, ho=ho, rs=rs: h.scalar_tensor_tensor(out=ho.ap, in0=hf.ap, scalar=rs.ap, in1=gfin.ap,
                                                                                  op0=ALU.mult, op1=ALU.mult), reads=[hf, rs, gfin], writes=[ho])
                S.op("sp", lambda h, ho=ho, tt=tt: h.dma_start(out=out_d.ap()[tt * 128:(tt + 1) * 128, :], in_=ho.ap),
                     reads=[ho], writes=[dkey("out", tt)], dma=True)
        S.op("sp", lambda h: h.nop(), reads=[dkey("out", t) for t in range(16)])
        S.emit(block, esem, dsem)
    return nc


NEG = -30000.0


def _bias_tables(rpb, seg):
    H = 8
    kc = np.arange(64)
    qc = np.arange(64)
    cs = np.clip(qc - 8, 0, 48)
    inwin = (kc[:, None] >= cs[None, :]) & (kc[:, None] < cs[None, :] + 16)
    dcidx = np.clip(kc[:, None] - qc[None, :], -15, 15) + 15

    def table(i, tiles_j, width):
        out = np.full((H, 128, width), NEG, np.float32)
        for jj, j in enumerate(tiles_j):
            for kp in range(2):
                kl = 2 * j + kp
                kg = 32 * seg - 4 + kl
                if kg < 0 or kg > 127:
                    continue
                for qp in range(2):
                    ql = 2 * i + qp
                    qg = 32 * seg - 4 + ql
                    rs = min(max(qg - 4, 0), 120)
                    if not (rs <= kg < rs + 8):
                        continue
                    dr = kg - qg + 7
                    vals = np.where(inwin[None], rpb[:, dr][:, dcidx], NEG)
                    out[:, kp * 64:(kp + 1) * 64, jj * 128 + qp * 64:jj * 128 + (qp + 1) * 64] = vals
        return out

    def table_main():
        out = np.full((H, 128, 640), NEG, np.float32)
        for jj in range(5):
            for kp in range(2):
                for qp in range(2):
                    drel = 2 * (jj - 2) + kp - qp
                    if not (-4 <= drel <= 3):
                        continue
                    vals = np.where(inwin[None], rpb[:, drel + 7][:, dcidx], NEG)
                    out[:, kp * 64:(kp + 1) * 64, jj * 128 + qp * 64:jj * 128 + (qp + 1) * 64] = vals
        return out

    main = table_main().transpose(1, 0, 2).reshape(128, 8 * 640)
    edges = []
    for i, tj in ((2, range(0, 6)), (3, range(1, 6)), (16, range(14, 19)), (17, range(14, 20))):
        edges.append(table(i, list(tj), 768).transpose(1, 0, 2).reshape(128, 8 * 768))
    return np.ascontiguousarray(main), np.ascontiguousarray(np.stack(edges))


def _prep(inputs):
    x = np.asarray(inputs["x"], np.float32)
    w_in = np.asarray(inputs["w_in"], np.float32)[0]
    rpb = np.asarray(inputs["na_rpb"], np.float32)[0]
    guf = np.asarray(inputs["gla_gate_up_fwd"], np.float32)[0]
    gub = np.asarray(inputs["gla_gate_up_bwd"], np.float32)[0]
    gbf = np.asarray(inputs["gla_gate_bias_fwd"], np.float32)[0]
    gbb = np.asarray(inputs["gla_gate_bias_bwd"], np.float32)[0]
    w_na = np.ascontiguousarray(w_in[:, 0:1536])
    qg, kg = w_in[:, 1536:1792], w_in[:, 1792:2048]
    qd = np.concatenate([np.concatenate([qg[:, h * 64:(h + 1) * 64]] * 2, axis=1) for h in range(4)], axis=1)
    kd = np.concatenate([np.concatenate([kg[:, h * 64:(h + 1) * 64]] * 2, axis=1) for h in range(4)], axis=1)
    w_g1 = np.ascontiguousarray(np.concatenate([qd, kd, w_in[:, 2048:2560]], axis=1))
    w_g2 = np.ascontiguousarray(np.concatenate([w_in[:, 2560:3072], w_in[:, 3072:3104]], axis=1))
    w_gate = np.zeros((32, 512), np.float32)
    gb = np.zeros((128, 4), np.float32)
    for h in range(4):
        w_gate[0:16, h * 128:h * 128 + 64] = guf[:, h * 64:(h + 1) * 64]
        w_gate[16:32, h * 128 + 64:h * 128 + 128] = gub[:, h * 64:(h + 1) * 64]
        gb[0:64, h] = gbf[h * 64:(h + 1) * 64]
        gb[64:128, h] = gbb[h * 64:(h + 1) * 64]
    j = np.arange(64)[:, None]
    i = np.arange(64)[None, :]
    m1 = np.concatenate([(j <= i), (j > i)], axis=1).astype(np.float32)
    mall = np.concatenate([m1, m1], axis=0)
    rmask = np.ones((128, 512), np.float32)
    rmask[:, 0::64] = 0.0
    rmask8 = np.ones((128, 128), np.float32)
    rmask8[:, 0::8] = 0.0
    ident = np.eye(128, dtype=np.float32)
    gn = np.tile(np.asarray(inputs["gla_norm_g"], np.float32)[0], 4)[None, :]
    common = dict(
        w_na=w_na, w_g1=w_g1, w_g2=w_g2,
        w_out=np.ascontiguousarray(np.asarray(inputs["w_out"], np.float32)[0]),
        w_ff1=np.ascontiguousarray(np.asarray(inputs["w_ff1"], np.float32)[0]),
        w_ff2=np.ascontiguousarray(np.asarray(inputs["w_ff2"], np.float32)[0]),
        g_mix=np.asarray(inputs["ln_mix_g"], np.float32).reshape(1, 1024),
        g_ff=np.asarray(inputs["ln_ff_g"], np.float32).reshape(1, 1024),
        g_fin=np.asarray(inputs["ln_final_g"], np.float32).reshape(1, 1024),
        g_n=np.ascontiguousarray(gn), w_gate=w_gate, gb=gb, mall=mall, rmask=rmask, rmask8=rmask8, ident=ident,
    )
    maps = []
    for c in range(NCORES):
        b, seg = c // 4, c % 4
        xc = np.zeros((2560, 1024), np.float32)
        r0 = 32 * seg - 4
        lo, hi = max(r0, 0), min(r0 + 40, 128)
        xc[(lo - r0) * 64:(hi - r0) * 64] = x[b, lo * 64:hi * 64]
        bm, be = _bias_tables(rpb, seg)
        sel = np.zeros((128, 8), np.float32)
        others = [o for o in range(4) if o != seg]
        for u, o in enumerate(others):
            if o < seg:
                sel[0:64, u] = 1.0
            if o > seg:
                sel[64:128, u] = 1.0
        xo = np.ascontiguousarray(np.concatenate([x[b, o * 2048:(o + 1) * 2048] for o in others], axis=0))
        m = dict(common)
        m.update(x=xc, x_oth=xo, b_main=bm, b_edge=be, sel=sel)
        maps.append(m)
    return maps


_NC_CACHE = {}


def kernel(**inputs):
    maps = _prep(inputs)
    if 4 not in _NC_CACHE:
        _NC_CACHE[4] = build(4)
    res = run_bass_kernel_spmd(_NC_CACHE[4], maps, core_ids=list(range(NCORES)))
    out = np.zeros((2, 8192, 1024), np.float32)
    for c in range(NCORES):
        b, seg = c // 4, c % 4
        out[b, seg * 2048:(seg + 1) * 2048] = np.asarray(res.results[c]["out"], np.float32)
    return out
```

```python
import numpy as np
from contextlib import ExitStack
import concourse.bass as bass
import concourse.mybir as mybir
from concourse.bass_utils import run_bass_kernel_spmd

F32 = mybir.dt.float32
BF16 = mybir.dt.bfloat16
AF = mybir.ActivationFunctionType
ALU = mybir.AluOpType
AX = mybir.AxisListType

NCORES = 8
EPS = 1e-6
PAGE = 128
ENGS = ("sp", "act", "pool", "dve", "pe")
NDSEM = 24


class Reg:
    __slots__ = ("ap", "keys")

    def __init__(self, ap, keys):
        self.ap = ap
        self.keys = keys

    def v(self, fn):
        return Reg(fn(self.ap), self.keys)


class Ten:
    def __init__(self, name, h, ncols, esz):
        self.name = name
        self.h = h
        self.ncols = ncols
        self.esz = esz

    def r(self, c0=0, c1=None, p0=0, p1=128):
        if c1 is None:
            c1 = self.ncols
        b0 = (c0 * self.esz) // PAGE
        b1 = (c1 * self.esz - 1) // PAGE
        keys = tuple((self.name, b) for b in range(b0, b1 + 1))
        return Reg(self.h[p0:p1, c0:c1], keys)


def dkey(name, i=0):
    return Reg(None, ((name, i),))


class Op:
    __slots__ = ("eng", "fn", "deps", "dma", "idx", "eidx", "sig", "tick", "sem", "val", "prev", "kept")


class Sched:
    def __init__(self):
        self.ops = []
        self.last_w = {}
        self.readers = {}

    def op(self, eng, fn, reads=(), writes=(), dma=False):
        o = Op()
        o.eng = eng
        o.fn = fn
        o.dma = dma
        o.idx = len(self.ops)
        o.sig = False
        o.tick = 0
        deps = set()
        lw = self.last_w
        rd = self.readers
        for r in reads:
            for k in r.keys:
                w = lw.get(k)
                if w is not None:
                    deps.add(w)
        for wr in writes:
            for k in wr.keys:
                w = lw.get(k)
                if w is not None:
                    deps.add(w)
                d = rd.get(k)
                if d:
                    deps.update(d.values())
        ekey = ("dma", o.idx) if dma else eng
        for wr in writes:
            for k in wr.keys:
                lw[k] = o.idx
                rd[k] = {}
        for r in reads:
            for k in r.keys:
                d = rd.get(k)
                if d is None:
                    d = rd[k] = {}
                d[ekey] = o.idx
        deps.discard(o.idx)
        o.deps = deps
        self.ops.append(o)
        return o

    def emit(self, block, esem, dsem):
        ops = self.ops
        per = {e: [] for e in ENGS}
        for o in ops:
            o.eidx = len(per[o.eng])
            per[o.eng].append(o)
        for o in ops:
            kept = []
            for d in o.deps:
                p = ops[d]
                if p.dma:
                    kept.append(p)
                elif p.eng != o.eng:
                    p.sig = True
                    kept.append(p)
                elif o.eng in ("act", "dve", "pool") and (o.eidx - p.eidx) <= 2:
                    p.sig = True
                    kept.append(p)
            o.kept = kept
        cnt = {e: 0 for e in ENGS}
        dcnt = [0] * NDSEM
        nd = 0
        nsw = 0
        for o in ops:
            if o.dma:
                if o.eng == "pool":
                    o.sem = 16 + nsw % (NDSEM - 16)
                    nsw += 1
                else:
                    o.sem = nd % 16
                    nd += 1
                o.prev = dcnt[o.sem]
                dcnt[o.sem] += 16
                o.val = dcnt[o.sem]
            elif o.sig:
                cnt[o.eng] += 1
                o.tick = cnt[o.eng]

        def run(e, h):
            waited = {}

            def wait(key, sh, val):
                if waited.get(key, 0) >= val:
                    return
                h.wait_ge(sh, val)
                waited[key] = val

            for o in per[e]:
                for p in o.kept:
                    if p.dma:
                        wait(("d", p.sem), dsem[p.sem], p.val)
                    else:
                        wait(("e", p.eng), esem[p.eng], p.tick)
                if o.dma and o.prev > 0:
                    wait(("d", o.sem), dsem[o.sem], o.prev)
                inst = o.fn(h)
                if o.dma:
                    inst.then_inc(dsem[o.sem], 16)
                elif o.sig:
                    inst.then_inc(esem[o.eng], 1)

        @block.sync
        def _(h):
            run("sp", h)

        @block.scalar
        def _(h):
            run("act", h)

        @block.gpsimd
        def _(h):
            run("pool", h)

        @block.vector
        def _(h):
            run("dve", h)

        @block.tensor
        def _(h):
            run("pe", h)


def build(stage=4):
    nc = bass.Bass("TRN2", target_bir_lowering=False)
    S = Sched()

    def din(name, shape, dt=F32):
        return nc.dram_tensor(name, shape, dt, kind="ExternalInput")

    x_d = din("x", [2560, 1024])
    xoth_d = din("x_oth", [3 * 2048, 1024])
    wna_d = din("w_na", [1024, 1536])
    wg1_d = din("w_g1", [1024, 1536])
    wg2_d = din("w_g2", [1024, 544])
    wout_d = din("w_out", [1024, 1024])
    wff1_d = din("w_ff1", [1024, 4096])
    wff2_d = din("w_ff2", [4096, 1024])
    gmix_d = din("g_mix", [1, 1024])
    gff_d = din("g_ff", [1, 1024])
    gfin_d = din("g_fin", [1, 1024])
    gn_d = din("g_n", [1, 512])
    bmain_d = din("b_main", [128, 8 * 640])
    bedge_d = din("b_edge", [4, 128, 8 * 768])
    wgate_d = din("w_gate", [32, 512])
    gb_d = din("gb", [128, 4])
    sel_d = din("sel", [128, 8])
    mall_d = din("mall", [128, 128])
    rmask_d = din("rmask", [128, 512])
    rmask8_d = din("rmask8", [128, 128])
    ident_d = din("ident", [128, 128])
    out_d = nc.dram_tensor("out", [2048, 1024], F32, kind="ExternalOutput")
    yna_d = nc.dram_tensor("yna_scr", [2048, 512], BF16)
    wff1b_d = nc.dram_tensor("wff1_bf", [1024, 4096], BF16)
    wff2b_d = nc.dram_tensor("wff2_bf", [4096, 1024], BF16)
    woutb_d = nc.dram_tensor("wout_bf", [1024, 1024], BF16)
    dbg_d = None
    if stage == 2:
        dbg_d = nc.dram_tensor("dbg", [2048, 512], BF16, kind="ExternalOutput")
    if stage == 3:
        dbg_d = nc.dram_tensor("dbg", [2048, 512], BF16, kind="ExternalOutput")

    with ExitStack() as es:
        def sb(name, cols, dt, parts=128):
            h = es.enter_context(nc.sbuf_tensor(name, [parts, cols], dt))
            return Ten(name, h, cols, 4 if dt == F32 else 2)

        def ps(name):
            h = es.enter_context(nc.psum_tensor(name, [128, 1024], F32))
            return Ten(name, h, 1024, 4)

        UA = sb("UA", 34816, BF16)
        UB = sb("UB", 12288, BF16)
        UC = sb("UC", 16384, BF16)
        Y = sb("Y", 16 * 512 + 1024, BF16)
        XT = sb("XT", 8192, BF16)
        XN = sb("XN", 1024, BF16)
        XNB = sb("XNB", 1024, BF16)
        GF = sb("GF", 1024, F32)
        TF = sb("TF", 6656, F32)
        G = sb("G", 1024, F32)
        GN = sb("GN", 512, F32)
        RM = sb("RM", 512, F32)
        RM8 = sb("RM8", 128, F32)
        SM = sb("SM", 1024, F32)
        MALL = sb("MALL", 128, BF16)
        IDN = sb("IDN", 128, BF16)
        WG = sb("WG", 512, BF16)
        PS = [ps(f"PS{i}") for i in range(4)]
        esem = {e: es.enter_context(nc.semaphore("es_" + e)) for e in ENGS}
        dsem = [es.enter_context(nc.semaphore(f"ds{i}")) for i in range(NDSEM)]
        block = es.enter_context(nc.Block())

        XN2 = [XN.r(), XNB.r()]
        banks = [PS[i // 2].r((i % 2) * 512, (i % 2) * 512 + 512) for i in range(8)]
        bank_ctr = [0]

        def nbank():
            b = banks[bank_ctr[0] % 6]
            bank_ctr[0] += 1
            return b

        TBANK = banks[7]

        def sm(c0, n=1):
            return SM.r(c0, c0 + n)
        C_SS, C_MS, C_RS, C_NH = 0, 24, 48, 72
        C_SS2, C_MS2, C_RS2 = 80, 84, 88
        C_GBN, C_SEL, C_DP = 96, 100, 108
        C_TOT, C_DEC, C_INC, C_GG = 128, 256, 384, 512
        C_DECS = 704
        C_GT, C_DG, C_DC, C_RDEN = 640, 656, 672, 680

        def load_const(dst, src_ap, eng="sp"):
            S.op(eng, lambda h: h.dma_start(out=dst.ap, in_=src_ap), writes=[dst], dma=True)

        load_const(G.r(), gmix_d.ap().to_broadcast([128, 1024]))
        load_const(GN.r(), gn_d.ap().to_broadcast([128, 512]))
        load_const(RM.r(), rmask_d.ap())
        load_const(RM8.r(), rmask8_d.ap())
        load_const(sm(C_SEL, 8), sel_d.ap())
        load_const(sm(C_GBN, 4), gb_d.ap())
        S.op("pool", lambda h: h.dma_start(out=IDN.r().ap, in_=ident_d.ap()), writes=[IDN.r()], dma=True)
        S.op("pool", lambda h: h.dma_start(out=MALL.r().ap, in_=mall_d.ap()), writes=[MALL.r()], dma=True)
        S.op("pool", lambda h: h.dma_start(out=WG.r(p1=32).ap, in_=wgate_d.ap()), writes=[WG.r()], dma=True)
        S.op("pool", lambda h: h.memset(sm(C_NH, 4).ap, -0.5), writes=[sm(C_NH, 4)])
        gbn = sm(C_GBN, 4)
        S.op("pool", lambda h: h.tensor_scalar(out=gbn.ap, in0=gbn.ap, scalar1=-1.0, scalar2=None, op0=ALU.mult),
             reads=[gbn], writes=[gbn])

        def load_w(dst_ten, c0, d_t, rows, cols, nsplit=4):
            kc = rows // 128
            per = kc // nsplit
            for s in range(nsplit):
                dst = dst_ten.r(c0 + s * per * cols, c0 + (s + 1) * per * cols)
                src = d_t.ap()[s * per * 128:(s + 1) * per * 128, :].rearrange("(c p) n -> p c n", p=128)
                S.op("pool", lambda h, dst=dst, src=src: h.dma_start(
                    out=dst.ap.rearrange("p (c n) -> p c n", n=cols), in_=src), writes=[dst], dma=True)

        def norm_stats(src, col, junk=None, k=0):
            if junk is None:
                junk = TF.r(5376, 6400)
            ss = sm(C_SS2 + k)
            S.op("dve", lambda h: h.scalar_tensor_tensor(out=junk.ap, in0=src.ap, scalar=1.0, in1=src.ap,
                                                         op0=ALU.mult, op1=ALU.mult, accum_out=ss.ap),
                 reads=[src], writes=[junk, ss])
            ms = sm(C_MS2 + k)
            S.op("pool", lambda h: h.tensor_scalar(out=ms.ap, in0=ss.ap, scalar1=1.0 / 1024, scalar2=EPS,
                                                   op0=ALU.mult, op1=ALU.add), reads=[ss], writes=[ms])
            rs = sm(col)
            nh = sm(C_NH)
            S.op("pool", lambda h: h.tensor_tensor(out=rs.ap, in0=ms.ap, in1=nh.ap, op=ALU.pow),
                 reads=[ms, nh], writes=[rs])
            return rs

        def scale_to_bf16(src, rs, gten, dst):
            S.op("dve", lambda h: h.scalar_tensor_tensor(out=dst.ap, in0=src.ap, scalar=rs.ap, in1=gten.ap,
                                                         op0=ALU.mult, op1=ALU.mult),
                 reads=[src, rs, gten], writes=[dst])

        def transpose8(src_bf, dst_fn, tb=None):
            if tb is None:
                tb = TBANK
            idn = IDN.r()

            def f(h):
                i = None
                for k in range(8):
                    i = h.transpose(out=tb.ap.bitcast(BF16)[:, k * 128:(k + 1) * 128],
                                    in_=src_bf.ap[:, k * 128:(k + 1) * 128], identity=idn.ap)
                return i
            S.op("pe", f, reads=[src_bf, idn], writes=[tb])
            dst = dst_fn
            S.op("act", lambda h: h.activation(out=dst.ap, in_=tb.ap.bitcast(BF16).rearrange("p (k t) -> p k t", k=8),
                                               func=AF.Copy), reads=[tb], writes=[dst])

        def mm_acc(out, pairs, reads):
            def f(h):
                i = None
                n = len(pairs)
                for q, (l, r) in enumerate(pairs):
                    i = h.matmul(out=out.ap, lhsT=l, rhs=r, start=(q == 0), stop=(q == n - 1))
                return i
            S.op("pe", f, reads=reads, writes=[out])

        def xt_slot(j, par=0):
            return XT.r(par * 4096, par * 4096 + 4096).v(lambda a: a.rearrange("p (k t) -> p k t", k=8)[:, :, j * 128:(j + 1) * 128])

        def cast_dram(src_ap, dst_ap, key):
            S.op("pool", lambda h: h.dma_start(out=dst_ap, in_=src_ap), writes=[key], dma=True)
        def run_inproj(tile_ids, load_stats, jobs_of_group, after_group0=None):
            n = len(tile_ids)
            rs = {0: load_stats(tile_ids[0])}

            def tile_work(i):
                t = tile_ids[i]
                xt = TF.r((t % 2) * 1024, (t % 2) * 1024 + 1024)
                xn = XN2[t % 2]
                scale_to_bf16(xt, rs[i], gmix, xn)
                if i + 1 < n:
                    rs[i + 1] = load_stats(tile_ids[i + 1])
                transpose8(xn, xt_slot(i % 4, (i // 4) % 2), tb=banks[6 + t % 2])
            for i in range(4):
                tile_work(i)
            ng = n // 4
            for g in range(ng):
                xtr = XT.r((g % 2) * 4096, (g % 2) * 4096 + 4096)
                jobs = jobs_of_group(g, xtr)
                nj_ = len(jobs)
                for c in range(4):
                    if g + 1 < ng:
                        tile_work(4 * (g + 1) + c)
                    for job in jobs[c * nj_ // 4:(c + 1) * nj_ // 4]:
                        job()
                if g == 0 and after_group0 is not None:
                    after_group0()

        load_w(UB, 0, wna_d, 1024, 1536)
        WNA = UB
        QT0, KT0, VA0 = 0, 8192, 18432
        va_all = UA.r(VA0, VA0 + 20 * 520)
        S.op("pool", lambda h: h.memset(va_all.ap.rearrange("p (t e) -> p t e", e=65)[:, :, 64:65], 1.0),
             writes=[va_all])
        gmix = G.r()
        EM0, EE0 = 4352, 9472
        EX0, PT0 = 0, 1536
        edge_idx = {2: 0, 3: 1, 16: 2, 17: 3}

        EDGE_Q = ["pool"]
        def load_edge(i):
            for q in range(4):
                stg = TF.r(2304 + (q % 2) * 1536, 2304 + (q % 2) * 1536 + 1536)
                S.op(EDGE_Q[0], lambda h, stg=stg, q=q, i=i: h.dma_start(
                    out=stg.ap, in_=bedge_d.ap()[edge_idx[i], :, q * 1536:(q + 1) * 1536]),
                    writes=[stg], dma=True)
                dst = UC.r(EE0 + q * 1536, EE0 + (q + 1) * 1536)
                S.op("act", lambda h, stg=stg, dst=dst: h.activation(out=dst.ap, in_=stg.ap, func=AF.Exp),
                     reads=[stg], writes=[dst])
        def prep_tables():
            for q in range(4):
                stg = TF.r(2304 + (q % 2) * 1536, 2304 + (q % 2) * 1536 + 1280)
                S.op("pool", lambda h, stg=stg, q=q: h.dma_start(out=stg.ap, in_=bmain_d.ap()[:, q * 1280:(q + 1) * 1280]),
                     writes=[stg], dma=True)
                dst = UC.r(EM0 + q * 1280, EM0 + (q + 1) * 1280)
                S.op("act", lambda h, stg=stg, dst=dst: h.activation(out=dst.ap, in_=stg.ap, func=AF.Exp),
                     reads=[stg], writes=[dst])
            load_edge(2)
            EDGE_Q[0] = "sp"
        def load_stats_A(t):
            xt = TF.r((t % 2) * 1024, (t % 2) * 1024 + 1024)
            S.op("sp", lambda h, xt=xt, t=t: h.dma_start(out=xt.ap, in_=x_d.ap()[t * 128:(t + 1) * 128, :]),
                 writes=[xt], dma=True)
            return norm_stats(xt, C_RS + t, k=t % 2)

        def jobs_A(g, xtr):
            tok0 = g * 512
            wr = WNA.r()
            lo, hi = max(tok0, 256), min(tok0 + 512, 2304)
            jobs = []
            for f in range(4):
                def jq(f=f):
                    b = nbank()
                    o = Reg(b.ap[:, 0:hi - lo], b.keys)
                    mm_acc(o, [(wr.ap[:, k * 1536 + f * 128:k * 1536 + f * 128 + 128],
                                xtr.ap[:, k * 512 + lo - tok0:k * 512 + hi - tok0]) for k in range(8)], [wr, xtr])
                    dst = UA.r(QT0 + f * 2048 + lo - 256, QT0 + f * 2048 + hi - 256)
                    S.op("act", lambda h: h.activation(out=dst.ap, in_=o.ap, func=AF.Copy), reads=[o], writes=[dst])

                def jk(f=f):
                    b = nbank()
                    mm_acc(b, [(wr.ap[:, k * 1536 + 512 + f * 128:k * 1536 + 512 + f * 128 + 128],
                                xtr.ap[:, k * 512:(k + 1) * 512]) for k in range(8)], [wr, xtr])
                    dst = UA.r(KT0 + f * 2560 + tok0, KT0 + f * 2560 + tok0 + 512)
                    S.op("act", lambda h: h.activation(out=dst.ap, in_=b.ap, func=AF.Copy), reads=[b], writes=[dst])
                jobs += [jq, jk]
            for jj in range(4):
                def jv(jj=jj):
                    tt = g * 4 + jj
                    b = nbank()
                    mm_acc(b, [(xtr.ap[:, k * 512 + jj * 128:k * 512 + jj * 128 + 128],
                                wr.ap[:, k * 1536 + 1024:k * 1536 + 1536]) for k in range(8)], [wr, xtr])
                    dst = UA.r(VA0 + tt * 520, VA0 + tt * 520 + 520)
                    S.op("dve", lambda h: h.tensor_copy(
                        out=dst.ap.rearrange("p (e c) -> p e c", c=65)[:, :, 0:64],
                        in_=b.ap.rearrange("p (e c) -> p e c", c=64)), reads=[b], writes=[dst])
                jobs.append(jv)
            return jobs
        run_inproj(list(range(20)), load_stats_A, jobs_A, after_group0=prep_tables)

        def tiles_of(i):
            if i == 2:
                return list(range(0, 6))
            if i == 17:
                return list(range(14, 20))
            return list(range(i - 2, i + 3))

        items = [(i, hd) for i in range(2, 18) for hd in range(8)]

        def st_of(idx, i):
            return PS[idx % 2].r(0, len(tiles_of(i)) * 128)

        def emit_qk(idx):
            i, hd = items[idx]
            tiles_j = tiles_of(i)
            f, pb = hd // 2, (hd % 2) * 64
            st = st_of(idx, i)
            kreg = UA.r(KT0 + f * 2560, KT0 + (f + 1) * 2560)
            qreg = UA.r(QT0 + f * 2048 + (i - 2) * 128, QT0 + f * 2048 + (i - 1) * 128)

            def fqk(h, st=st, kreg=kreg, qreg=qreg, pb=pb, tiles_j=tiles_j):
                ins = None
                for jj, j in enumerate(tiles_j):
                    ins = h.matmul(out=st.ap[:, jj * 128:(jj + 1) * 128],
                                   lhsT=kreg.ap[pb:pb + 64, j * 128:(j + 1) * 128],
                                   rhs=qreg.ap[pb:pb + 64, :], start=True, stop=True)
                return ins
            S.op("pe", fqk, reads=[kreg, qreg], writes=[st])

        def emit_rest(idx):
            i, hd = items[idx]
            tiles_j = tiles_of(i)
            nj = len(tiles_j)
            ncol = nj * 128
            if hd == 0 and i in (3, 17):
                load_edge(i)
            if hd == 0 and i == 4:
                load_edge(16)
            if hd == 0 and i == 3:
                load_w(UB, 0, wg1_d, 1024, 1536)
                load_w(UC, 0, wg2_d, 1024, 544)
            if hd == 0 and i == 13:
                cast_dram(wout_d.ap(), woutb_d.ap(), dkey("woutb"))
            if hd == 0 and 5 <= i <= 12:
                q = i - 5
                if q < 4:
                    cast_dram(wff1_d.ap()[:, q * 1024:(q + 1) * 1024], wff1b_d.ap()[:, q * 1024:(q + 1) * 1024], dkey("wff1b", q))
                else:
                    q -= 4
                    cast_dram(wff2_d.ap()[q * 1024:(q + 1) * 1024, :], wff2b_d.ap()[q * 1024:(q + 1) * 1024, :], dkey("wff2b", q))
            st = st_of(idx, i)
            ex = XT.r(EX0 + (idx % 2) * 768, EX0 + (idx % 2) * 768 + ncol)
            S.op("act", lambda h, ex=ex, st=st: h.activation(out=ex.ap, in_=st.ap, func=AF.Exp, scale=0.125),
                 reads=[st], writes=[ex])
            if i in edge_idx:
                ee = UC.r(EE0 + hd * 768, EE0 + hd * 768 + ncol)
            else:
                ee = UC.r(EM0 + hd * 640, EM0 + hd * 640 + ncol)
            pt = XT.r(PT0 + (idx % 2) * 768, PT0 + (idx % 2) * 768 + ncol)
            S.op("dve", lambda h, pt=pt, ex=ex, ee=ee: h.tensor_tensor(out=pt.ap, in0=ex.ap, in1=ee.ap, op=ALU.mult),
                 reads=[ex, ee], writes=[pt])
            pvt = PS[2 + (i % 2)]
            pc0 = (hd // 4) * 512 + (hd % 4) * 65
            pv = pvt.r(pc0, pc0 + 65)
            vreg = UA.r(VA0, VA0 + 20 * 520)

            def fpv(h, pv=pv, pt=pt, vreg=vreg, hd=hd, tiles_j=tiles_j, nj=nj):
                ins = None
                for jj, j in enumerate(tiles_j):
                    ins = h.matmul(out=pv.ap, lhsT=pt.ap[:, jj * 128:(jj + 1) * 128],
                                   rhs=vreg.ap[:, j * 520 + hd * 65:j * 520 + hd * 65 + 65],
                                   start=(jj == 0), stop=(jj == nj - 1))
                return ins
            S.op("pe", fpv, reads=[pt, vreg], writes=[pv])
            if hd != 7:
                return
            pvr = pvt.r()
            rden = sm(C_RDEN + (i % 2) * 8, 8)
            pv4 = pvr.v(lambda a: a.rearrange("p (b c) -> p b c", b=2)[:, :, 0:260].rearrange("p b (e c) -> p b e c", c=65))
            S.op("dve", lambda h, rden=rden, pv4=pv4: h.reciprocal(
                out=rden.ap.rearrange("p (b e) -> p b e", b=2), in_=pv4.ap[:, :, :, 64]),
                reads=[pvr], writes=[rden])
            ystg = Y.r(8192 + (i % 2) * 512, 8192 + (i % 2) * 512 + 512)
            S.op("dve", lambda h, ystg=ystg, pv4=pv4, rden=rden: h.tensor_tensor(
                out=ystg.ap.rearrange("p (b e c) -> p b e c", b=2, e=4), in0=pv4.ap[:, :, :, 0:64],
                in1=rden.ap.rearrange("p (b e) -> p b e", b=2).unsqueeze(3).to_broadcast([128, 2, 4, 64]),
                op=ALU.mult), reads=[pvr, rden], writes=[ystg])
            tt = i - 2
            S.op("sp", lambda h, ystg=ystg, tt=tt: h.dma_start(out=yna_d.ap()[tt * 128:(tt + 1) * 128, :], in_=ystg.ap),
                 reads=[ystg], writes=[dkey("yna", tt)], dma=True)
            if stage == 2:
                S.op("sp", lambda h, ystg=ystg, tt=tt: h.dma_start(out=dbg_d.ap()[tt * 128:(tt + 1) * 128, :], in_=ystg.ap),
                     reads=[ystg], writes=[dkey("dbg", tt)], dma=True)

        emit_qk(0)
        for idx in range(len(items)):
            if idx + 1 < len(items):
                emit_qk(idx + 1)
            emit_rest(idx)

        if stage == 2:
            S.op("sp", lambda h: h.nop(), reads=[dkey("dbg", t) for t in range(16)] + [dkey("yna", t) for t in range(16)])
            S.emit(block, esem, dsem)
            return nc

        GQ0, GK0, GV0, GR0, GZ0 = 0, 8192, 16384, 24576, 32768
        FSf = Y.r(0, 3120).v(lambda a: a.bitcast(F32))
        SINP = Y.r(0, 1024).v(lambda a: a.bitcast(F32))
        def emit_fold():
            SIN = TF.r(2100, 2612)
            S.op("pool", lambda h: h.memset(SIN.ap, 0.0), writes=[SIN])
            for d, order in ((0, (0, 1, 2)), (1, (2, 1, 0))):
                p0, p1 = d * 64, d * 64 + 64
                for u in order:
                    dp = sm(C_DP, 4)
                    selu = sm(C_SEL + u)
                    S.op("dve", lambda h, dp=dp, u=u, selu=selu, p0=p0, p1=p1: h.tensor_scalar(
                        out=dp.ap[p0:p1, :], in0=FSf.ap[p0:p1, u * 520 + 512:u * 520 + 516], scalar1=-1.0, scalar2=selu.ap[p0:p1, :],
                        op0=ALU.add, op1=ALU.mult), reads=[FSf, selu], writes=[dp])
                    S.op("pool", lambda h, dp=dp, p0=p0, p1=p1: h.tensor_scalar(out=dp.ap[p0:p1, :], in0=dp.ap[p0:p1, :], scalar1=1.0, scalar2=None, op0=ALU.add),
                         reads=[dp], writes=[dp])
                    fx = TF.r(1032, 1544)
                    S.op("dve", lambda h, fx=fx, u=u, selu=selu, p0=p0, p1=p1: h.tensor_scalar(
                        out=fx.ap[p0:p1, :], in0=FSf.ap[p0:p1, u * 520:u * 520 + 512], scalar1=selu.ap[p0:p1, :], scalar2=None,
                        op0=ALU.mult), reads=[FSf, selu], writes=[fx])
                    for hh in range(4):
                        S.op("dve", lambda h, fx=fx, dp=dp, hh=hh, p0=p0, p1=p1: h.scalar_tensor_tensor(
                            out=SIN.ap[p0:p1, hh * 128:(hh + 1) * 128], in0=SIN.ap[p0:p1, hh * 128:(hh + 1) * 128], scalar=dp.ap[p0:p1, hh:hh + 1],
                            in1=fx.ap[p0:p1, hh * 128:(hh + 1) * 128], op0=ALU.mult, op1=ALU.add), reads=[SIN, fx, dp], writes=[SIN])
            S.op("act", lambda h: h.activation(out=SINP.ap, in_=SIN.ap, func=AF.Copy), reads=[SIN, FSf], writes=[SINP])

        for sg, own in ((0, False), (1, False), (2, False), (3, True)):
            if own:
                emit_fold()
            def load_stats_D(t, sg=sg, own=own):
                xt = TF.r((t % 2) * 1024, (t % 2) * 1024 + 1024)
                if own:
                    S.op("sp", lambda h, xt=xt, t=t: h.dma_start(out=xt.ap, in_=x_d.ap()[t * 128:(t + 1) * 128, :]),
                         writes=[xt], dma=True)
                    return sm(C_RS + t)
                S.op("sp", lambda h, xt=xt, t=t, sg=sg: h.dma_start(out=xt.ap, in_=xoth_d.ap()[sg * 2048 + (t - 2) * 128:sg * 2048 + (t - 1) * 128, :]),
                     writes=[xt], dma=True)
                return norm_stats(xt, C_RS2 + t % 2, junk=UC.r(8192, 9216), k=t % 2)

            def jobs_D(g, xtr, own=own):
                w1 = UB.r()
                w2 = UC.r(0, 8 * 544)
                jobs = []
                for hh in range(4):
                    for which, base in (((0, GQ0), (1, GK0)) if own else ((1, GK0),)):
                        def jqk(hh=hh, which=which, base=base):
                            b = nbank()
                            c0 = which * 512 + hh * 128
                            mm_acc(b, [(w1.ap[:, k * 1536 + c0:k * 1536 + c0 + 128], xtr.ap[:, k * 512:(k + 1) * 512])
                                       for k in range(8)], [w1, xtr])
                            dst = UA.r(base + hh * 2048 + g * 512, base + hh * 2048 + g * 512 + 512)
                            sc = 0.125 if which == 0 else 1.0
                            S.op("act", lambda h: h.activation(out=dst.ap, in_=b.ap, func=AF.Copy, scale=sc), reads=[b], writes=[dst])
                        jobs.append(jqk)

                def jz():
                    b = nbank()
                    bz = Reg(b.ap[0:32, :], b.keys)
                    mm_acc(bz, [(w2.ap[:, k * 544 + 512:k * 544 + 544], xtr.ap[:, k * 512:(k + 1) * 512]) for k in range(8)], [w2, xtr])
                    dst = UA.r(GZ0 + g * 512, GZ0 + g * 512 + 512, p1=32)
                    S.op("act", lambda h: h.activation(out=dst.ap, in_=bz.ap, func=AF.Copy), reads=[bz], writes=[dst])
                jobs.append(jz)
                for jj in range(4):
                    def jv(jj=jj):
                        tt = g * 4 + jj
                        b = nbank()
                        mm_acc(b, [(xtr.ap[:, k * 512 + jj * 128:k * 512 + jj * 128 + 128], w1.ap[:, k * 1536 + 1024:k * 1536 + 1536])
                                   for k in range(8)], [w1, xtr])
                        dst = UA.r(GV0 + tt * 512, GV0 + tt * 512 + 512)
                        S.op("dve", lambda h: h.tensor_copy(out=dst.ap, in_=b.ap), reads=[b], writes=[dst])
                    jobs.append(jv)
                    if own:
                        def jr(jj=jj):
                            tt = g * 4 + jj
                            b = nbank()
                            mm_acc(b, [(xtr.ap[:, k * 512 + jj * 128:k * 512 + jj * 128 + 128], w2.ap[:, k * 544:k * 544 + 512])
                                       for k in range(8)], [w2, xtr])
                            dst = UA.r(GR0 + tt * 512, GR0 + tt * 512 + 512)
                            S.op("act", lambda h: h.activation(out=dst.ap, in_=b.ap, func=AF.Copy), reads=[b], writes=[dst])
                        jobs.append(jr)
                return jobs
            run_inproj(list(range(2, 18)), load_stats_D, jobs_D)
            AALL0, KDT0 = 0, 8192
            SLOC = UC
            FG0 = 3328
            mall = MALL.r()
            rmask = RM.r()
            wg = WG.r(p1=32)
            ONES = TF.r(5632, 6144)
            if sg == 0:
                S.op("pool", lambda h: h.memset(ONES.ap, 1.0), writes=[ONES])
            if own:
                for zt in (UC.r(), XN2[0], XN2[1], GF.r()):
                    S.op("pool", lambda h, zt=zt: h.memset(zt.ap, 0.0), writes=[zt])
            def temps(st_):
                if st_ == 0:
                    return dict(sp=TF.r(0, 512), cf=TF.r(512, 1024), d=TF.r(1024, 1536), eB=TF.r(1536, 2048),
                                emB=TF.r(2048, 2560), eD=TF.r(2560, 3072), kd=TF.r(3072, 3328).v(lambda a: a.bitcast(BF16)))
                yv = lambda k: Y.r(3120 + 1024 * k, 3120 + 1024 * (k + 1)).v(lambda a: a.bitcast(F32))
                xv = lambda k: XT.r(4096 + 1024 * k, 4096 + 1024 * (k + 1)).v(lambda a: a.bitcast(F32))
                return dict(sp=yv(0), cf=yv(1), d=yv(2), eB=yv(3), emB=xv(0), eD=xv(1), kd=XT.r(6144, 6656))

            def v3(a):
                return a.rearrange("p (c i) -> p c i", i=64)

            def totb(a):
                return a.unsqueeze(2).to_broadcast([a.shape[0], 8, 64])

            def stage_a(g, hh, part, own=own):
                gh = g * 4 + hh
                T = temps(gh % 2)
                t_sp, t_cf, t_d, t_eB, t_emB, t_eD, kd = T["sp"], T["cf"], T["d"], T["eB"], T["emB"], T["eD"], T["kd"]
                if part == 1:
                    zr = UA.r(GZ0 + g * 512, GZ0 + g * 512 + 512, p1=32)
                    xg = banks[4 + gh % 2]
                    mm_acc(xg, [(wg.ap[:, hh * 128:(hh + 1) * 128], zr.ap)], [wg, zr])
                    gbh = sm(C_GBN + hh)
                    S.op("act", lambda h: h.activation(out=t_sp.ap, in_=xg.ap, func=AF.Exp, scale=-1.0, bias=gbh.ap),
                         reads=[xg, gbh], writes=[t_sp])
                    S.op("act", lambda h: h.activation(out=t_sp.ap, in_=t_sp.ap, func=AF.Ln, bias=1.0, scale=1.0),
                         reads=[t_sp], writes=[t_sp])
                kreg = UA.r(GK0 + hh * 2048 + g * 512, GK0 + hh * 2048 + g * 512 + 512)
                if not own:
                    if part == 1:
                        S.op("dve", lambda h: h.tensor_tensor_scan(out=t_cf.ap, data0=ONES.ap, data1=t_sp.ap, initial=0.0,
                                                                   op0=ALU.mult, op1=ALU.add), reads=[ONES, t_sp], writes=[t_cf])
                        gtc = sm(C_GT + hh * 4 + g)
                        S.op("dve", lambda h: h.tensor_copy(out=gtc.ap, in_=t_cf.ap[:, 511:512]), reads=[t_cf], writes=[gtc])
                        S.op("dve", lambda h: h.tensor_scalar(out=t_d.ap[0:64, :], in0=t_cf.ap[0:64, :], scalar1=-1.0,
                                                              scalar2=gtc.ap[0:64, :], op0=ALU.mult, op1=ALU.add),
                             reads=[t_cf, gtc], writes=[t_d])
                        S.op("dve", lambda h: h.tensor_tensor(out=t_d.ap[64:128, :], in0=t_cf.ap[64:128, :],
                                                              in1=t_sp.ap[64:128, :], op=ALU.subtract), reads=[t_cf, t_sp], writes=[t_d])
                    if part == 1:
                        return
                    S.op("act", lambda h: h.activation(out=t_eD.ap, in_=t_d.ap, func=AF.Exp, scale=-1.0 / 16), reads=[t_d], writes=[t_eD])
                    S.op("dve", lambda h: h.tensor_tensor(out=kd.ap, in0=kreg.ap, in1=t_eD.ap, op=ALU.mult),
                         reads=[kreg, t_eD], writes=[kd])
                    return
                tot = sm(C_TOT + hh * 32 + g * 8, 8)
                if part == 1:
                    S.op("dve", lambda h: h.tensor_tensor_scan(out=t_cf.ap, data0=rmask.ap, data1=t_sp.ap, initial=0.0,
                                                               op0=ALU.mult, op1=ALU.add), reads=[rmask, t_sp], writes=[t_cf])
                    S.op("dve", lambda h: h.tensor_copy(out=tot.ap, in_=t_cf.ap[:, 63:512:64]), reads=[t_cf], writes=[tot])
                    S.op("dve", lambda h: h.tensor_tensor(out=v3(t_cf.ap[64:128, :]), in0=totb(tot.ap[64:128, :]),
                                                          in1=v3(t_cf.ap[64:128, :]), op=ALU.subtract), reads=[t_cf, tot], writes=[t_cf])
                    S.op("dve", lambda h: h.tensor_tensor(out=t_cf.ap[64:128, :], in0=t_cf.ap[64:128, :],
                                                          in1=t_sp.ap[64:128, :], op=ALU.add), reads=[t_cf, t_sp], writes=[t_cf])
                    S.op("dve", lambda h: h.tensor_tensor(out=v3(t_d.ap), in0=totb(tot.ap), in1=v3(t_cf.ap), op=ALU.subtract),
                         reads=[t_cf, tot], writes=[t_d])
                    return
                S.op("act", lambda h: h.activation(out=t_eB.ap, in_=t_cf.ap, func=AF.Exp, scale=-1.0 / 16), reads=[t_cf], writes=[t_eB])
                S.op("act", lambda h: h.activation(out=t_emB.ap, in_=t_cf.ap, func=AF.Exp, scale=1.0 / 16), reads=[t_cf], writes=[t_emB])
                S.op("act", lambda h: h.activation(out=t_eD.ap, in_=t_d.ap, func=AF.Exp, scale=-1.0 / 16), reads=[t_d], writes=[t_eD])
                qreg = UA.r(GQ0 + hh * 2048 + g * 512, GQ0 + hh * 2048 + g * 512 + 512)
                S.op("dve", lambda h: h.tensor_tensor(out=kd.ap, in0=kreg.ap, in1=t_eD.ap, op=ALU.mult), reads=[kreg, t_eD], writes=[kd])
                S.op("pool", lambda h: h.tensor_tensor(out=kreg.ap, in0=kreg.ap, in1=t_emB.ap, op=ALU.mult), reads=[kreg, t_emB], writes=[kreg])
                S.op("dve", lambda h: h.tensor_tensor(out=qreg.ap, in0=qreg.ap, in1=t_eB.ap, op=ALU.mult), reads=[qreg, t_eB], writes=[qreg])

            HBS = [TF.r(5376, 6400), G.r()]

            def stage_b(g, hh, own=own):
                gh = g * 4 + hh
                kd = temps(gh % 2)["kd"]
                tb = banks[6 + gh % 2]
                idn = IDN.r()

                def ftr(h):
                    ins = None
                    for ttl in range(4):
                        ins = h.transpose(out=tb.ap.bitcast(BF16)[:, ttl * 128:(ttl + 1) * 128],
                                          in_=kd.ap[:, ttl * 128:(ttl + 1) * 128], identity=idn.ap)
                    return ins
                S.op("pe", ftr, reads=[kd, idn], writes=[tb])
                if own:
                    kdt = XN2[gh % 2]
                    S.op("act", lambda h: h.activation(out=kdt.ap[0:64, 0:512], in_=tb.ap.bitcast(BF16)[0:64, 0:512], func=AF.Copy),
                         reads=[tb], writes=[kdt])
                    S.op("act", lambda h: h.activation(out=kdt.ap[64:128, 512:1024], in_=tb.ap.bitcast(BF16)[64:128, 0:512], func=AF.Copy),
                         reads=[tb], writes=[kdt])
                else:
                    kdt = XN2[gh % 2].v(lambda a: a[:, 0:512])
                    S.op("act", lambda h: h.activation(out=kdt.ap, in_=tb.ap.bitcast(BF16)[:, 0:512], func=AF.Copy), reads=[tb], writes=[kdt])
                vgrp = UA.r(GV0 + g * 4 * 512, GV0 + (g + 1) * 4 * 512)
                fg = TF.r(FG0 + gh * 128, FG0 + gh * 128 + 128)
                if not own:
                    fb_ = banks[gh % 2]
                    fo_ = Reg(fb_.ap[:, 0:128], fb_.keys)
                    mm_acc(fo_, [(kdt.ap[:, ttl * 128:(ttl + 1) * 128],
                                  vgrp.ap[:, ttl * 512 + hh * 128:ttl * 512 + hh * 128 + 128]) for ttl in range(4)], [kdt, vgrp])
                    S.op("act", lambda h: h.activation(out=fg.ap, in_=fo_.ap, func=AF.Copy), reads=[fo_], writes=[fg])
                    return
                qreg = UA.r(GQ0 + hh * 2048 + g * 512, GQ0 + hh * 2048 + g * 512 + 512)
                kreg = UA.r(GK0 + hh * 2048 + g * 512, GK0 + hh * 2048 + g * 512 + 512)
                ab = PS[0].r(0, 256)
                ab2 = PS[0].r(512, 768)

                def fa(h):
                    ins = None
                    for ttl in range(4):
                        for pr in range(2):
                            for d in range(2):
                                c0 = (2 * ttl + pr) * 64
                                dst = ab if d == 0 else ab2
                                ins = h.matmul(out=dst.ap[pr * 64:(pr + 1) * 64, ttl * 64:(ttl + 1) * 64],
                                               lhsT=kreg.ap[d * 64:(d + 1) * 64, c0:c0 + 64],
                                               rhs=qreg.ap[d * 64:(d + 1) * 64, c0:c0 + 64], start=True, stop=True)
                    return ins
                S.op("pe", fa, reads=[kreg, qreg], writes=[ab, ab2])
                adst = UB.r(AALL0 + g * 4 * 512, AALL0 + (g + 1) * 4 * 512)
                for d, src in ((0, ab), (1, ab2)):
                    S.op("dve" if d == 0 else "pool" if False else "dve", lambda h, src=src, d=d: h.tensor_tensor(
                        out=adst.ap.rearrange("p (t x) -> p t x", x=512)[:, :, hh * 128 + d * 64:hh * 128 + d * 64 + 64],
                        in0=src.ap.rearrange("p (t x) -> p t x", x=64),
                        in1=mall.ap[:, d * 64:(d + 1) * 64].unsqueeze(1).to_broadcast([128, 4, 64]), op=ALU.mult),
                        reads=[src, mall], writes=[adst])
                cps = PS[1].r()
                vt4 = UA.r(GV0 + g * 4 * 512, GV0 + (g + 1) * 4 * 512)

                def fc(h):
                    ins = None
                    for c in range(8):
                        ttl, pr = c // 2, c % 2
                        for eh in range(2):
                            for dh in range(2):
                                pos = c if dh == 0 else 7 - c
                                o = cps.ap[dh * 64:(dh + 1) * 64, eh * 512:(eh + 1) * 512].rearrange("p (e c) -> p c e", c=8)[:, pos, :]
                                ins = h.matmul(out=o,
                                               lhsT=kdt.ap[:, pr * 512 + ttl * 128 + dh * 64:pr * 512 + ttl * 128 + dh * 64 + 64],
                                               rhs=vt4.ap[:, ttl * 512 + hh * 128 + eh * 64:ttl * 512 + hh * 128 + eh * 64 + 64],
                                               start=True, stop=True)
                    return ins
                S.op("pe", fc, reads=[kdt, vt4], writes=[cps])
                tot = sm(C_TOT + hh * 32 + g * 8, 8)
                decb = GF.r()
                dbv = decb.ap.rearrange("p (e c) -> p e c", c=8)
                S.op("act", lambda h: h.activation(out=dbv[0:64, :, 1:8], in_=tot.ap[0:64, 1:8].unsqueeze(1).to_broadcast([64, 128, 7]),
                                                   func=AF.Exp, scale=-1.0 / 16), reads=[tot], writes=[decb])
                S.op("act", lambda h: h.activation(out=dbv[64:128, :, 1:8], in_=tot.ap[64:128, 6::-1].unsqueeze(1).to_broadcast([64, 128, 7]),
                                                   func=AF.Exp, scale=-1.0 / 16), reads=[tot], writes=[decb])
                HB = HBS[gh % 2]
                S.op("dve", lambda h: h.tensor_tensor_scan(out=HB.ap, data0=decb.ap, data1=cps.ap, initial=0.0, op0=ALU.mult, op1=ALU.add),
                     reads=[decb, cps], writes=[HB])

            def stage_c(g, hh):
                gh = g * 4 + hh
                HB = HBS[gh % 2]
                fg = TF.r(FG0 + gh * 128, FG0 + gh * 128 + 128)
                sl8 = SLOC.r((g * 8 * 4) * 128, ((g + 1) * 8 * 4) * 128)
                hv = HB.ap.rearrange("p (e c) -> p c e", c=8)
                slv = sl8.ap.rearrange("p (n x) -> p n x", x=512)
                S.op("act", lambda h: h.activation(out=slv[0:64, 1:8, hh * 128:(hh + 1) * 128], in_=hv[0:64, 0:7, :], func=AF.Copy),
                     reads=[HB], writes=[sl8])
                S.op("dve", lambda h: h.tensor_copy(out=slv[64:128, 0:7, hh * 128:(hh + 1) * 128], in_=hv[64:128, 6::-1, :]),
                     reads=[HB], writes=[sl8])
                S.op("act", lambda h: h.activation(out=fg.ap, in_=hv[:, 7, :], func=AF.Copy), reads=[HB], writes=[fg])

            ghs = [(g, hh) for g in range(4) for hh in range(4)]
            stage_a(*ghs[0], 1)
            stage_a(*ghs[1], 1)
            stage_a(*ghs[0], 2)
            for ii in range(16):
                if ii + 2 < 16:
                    stage_a(*ghs[ii + 2], 1)
                if ii + 1 < 16:
                    stage_a(*ghs[ii + 1], 2)
                stage_b(*ghs[ii])
                if own and ii >= 1:
                    stage_c(*ghs[ii - 1])
            if own:
                stage_c(*ghs[15])

            totall = sm(C_TOT, 128)
            gt = sm(C_GT, 16)
            if own:
                S.op("dve", lambda h: h.tensor_reduce(out=gt.ap, in_=totall.ap.rearrange("p (a c) -> p a c", c=8), axis=AX.X, op=ALU.add),
                     reads=[totall], writes=[gt])
            dg = sm(C_DG, 16)
            S.op("act", lambda h: h.activation(out=dg.ap, in_=gt.ap, func=AF.Exp, scale=-1.0 / 16), reads=[gt], writes=[dg])
            if own:
                inc = sm(C_INC, 128)
                rm8 = RM8.r()
                S.op("dve", lambda h: h.tensor_tensor_scan(out=inc.ap, data0=rm8.ap, data1=totall.ap, initial=0.0, op0=ALU.mult, op1=ALU.add),
                     reads=[rm8, totall], writes=[inc])
                S.op("dve", lambda h: h.tensor_tensor(out=inc.ap[0:64, :], in0=inc.ap[0:64, :], in1=totall.ap[0:64, :], op=ALU.subtract),
                     reads=[inc, totall], writes=[inc])
                S.op("dve", lambda h: h.tensor_tensor(out=inc.ap[64:128, :].rearrange("p (a c) -> p a c", c=8),
                                                      in0=gt.ap[64:128, :].unsqueeze(2).to_broadcast([64, 16, 8]),
                                                      in1=inc.ap[64:128, :].rearrange("p (a c) -> p a c", c=8), op=ALU.subtract),
                     reads=[inc, gt], writes=[inc])
                gg = sm(C_GG, 128)
                S.op("act", lambda h: h.activation(out=gg.ap, in_=inc.ap, func=AF.Exp, scale=-1.0 / 16), reads=[inc], writes=[gg])
                for hh in range(4):
                    qreg = UA.r(GQ0 + hh * 2048, GQ0 + (hh + 1) * 2048)
                    kreg = UA.r(GK0 + hh * 2048, GK0 + (hh + 1) * 2048)
                    ggh = sm(C_GG + hh * 32, 32)
                    S.op("dve", lambda h, qreg=qreg, kreg=kreg, ggh=ggh: h.tensor_tensor(
                        out=kreg.ap.rearrange("p (n i) -> p n i", i=64), in0=qreg.ap.rearrange("p (n i) -> p n i", i=64),
                        in1=ggh.ap.unsqueeze(2).to_broadcast([128, 32, 64]), op=ALU.mult), reads=[qreg, ggh], writes=[kreg])

            else:
                PK = TF.r(0, 516)
                for hh in range(4):
                    for d, order in ((0, range(4)), (1, range(3, -1, -1))):
                        p0, p1 = d * 64, d * 64 + 64
                        pk = TF.r(hh * 128, hh * 128 + 128)
                        for q, g in enumerate(order):
                            fg = TF.r(FG0 + (g * 4 + hh) * 128, FG0 + (g * 4 + hh) * 128 + 128)
                            if q == 0:
                                S.op("act", lambda h, pk=pk, fg=fg, p0=p0, p1=p1: h.activation(out=pk.ap[p0:p1, :], in_=fg.ap[p0:p1, :], func=AF.Copy),
                                     reads=[fg], writes=[pk])
                            else:
                                dcol = sm(C_DG + hh * 4 + g)
                                S.op("dve", lambda h, pk=pk, fg=fg, dcol=dcol, p0=p0, p1=p1: h.scalar_tensor_tensor(
                                    out=pk.ap[p0:p1, :], in0=pk.ap[p0:p1, :], scalar=dcol.ap[p0:p1, :], in1=fg.ap[p0:p1, :],
                                    op0=ALU.mult, op1=ALU.add), reads=[pk, fg, dcol], writes=[pk])
                ct = sm(C_DC, 4)
                S.op("dve", lambda h: h.tensor_reduce(out=ct.ap, in_=gt.ap.rearrange("p (a c) -> p a c", c=4), axis=AX.X, op=ALU.add),
                     reads=[gt], writes=[ct])
                pkd = TF.r(512, 516)
                S.op("act", lambda h: h.activation(out=pkd.ap, in_=ct.ap, func=AF.Exp, scale=-1.0 / 16), reads=[ct], writes=[pkd])
                S.op("act", lambda h, sg=sg: h.activation(out=FSf.ap[:, sg * 520:sg * 520 + 516], in_=PK.ap, func=AF.Copy), reads=[PK], writes=[FSf])
        SIGf = XT.r(0, 2048)
        SIG = SIGf
        for hh in range(4):
            for d, order in ((0, range(4)), (1, range(3, -1, -1))):
                p0, p1 = d * 64, d * 64 + 64
                sin = Y.r(hh * 256, hh * 256 + 256).v(lambda a: a.bitcast(F32))
                for q, g in enumerate(order):
                    S.op("act", lambda h, sin=sin, g=g, hh=hh, p0=p0, p1=p1: h.activation(
                        out=SIG.ap[p0:p1, (g * 4 + hh) * 128:(g * 4 + hh + 1) * 128], in_=sin.ap[p0:p1, :], func=AF.Copy),
                        reads=[sin], writes=[SIGf])
                    if q < 3:
                        fg = TF.r(FG0 + (g * 4 + hh) * 128, FG0 + (g * 4 + hh) * 128 + 128)
                        dcol = sm(C_DG + hh * 4 + g)
                        S.op("dve", lambda h, sin=sin, fg=fg, dcol=dcol, p0=p0, p1=p1: h.scalar_tensor_tensor(
                            out=sin.ap[p0:p1, :], in0=sin.ap[p0:p1, :], scalar=dcol.ap[p0:p1, :], in1=fg.ap[p0:p1, :],
                            op0=ALU.mult, op1=ALU.add), reads=[sin, fg, dcol], writes=[sin])

        gnb = GN.r()
        nh4 = sm(C_NH, 4)

        def obs_of(tt):
            P_ = PS[1 + tt % 2]
            return [P_.r(0, 512), P_.r(512, 1024)]

        def emit_fo(tt):
            obs = obs_of(tt)
            vt = UA.r(GV0 + tt * 512, GV0 + tt * 512 + 512)
            at = UB.r(AALL0 + tt * 512, AALL0 + tt * 512 + 512)
            qall = UA.r(GQ0, GQ0 + 8192)
            kall = UA.r(GK0, GK0 + 8192)
            sloc = SLOC.r((2 * tt) * 512, (2 * tt + 2) * 512)

            def fo(h):
                ins = None
                for hh in range(4):
                    for pr in range(2):
                        n = 2 * tt + pr
                        g = n // 8
                        o = obs[pr].ap[pr * 64:(pr + 1) * 64, hh * 128:(hh + 1) * 128]
                        vv = vt.ap[pr * 64:(pr + 1) * 64, hh * 128:(hh + 1) * 128]
                        for d in range(2):
                            ins = h.matmul(out=o, lhsT=at.ap[pr * 64:(pr + 1) * 64, hh * 128 + d * 64:hh * 128 + d * 64 + 64],
                                           rhs=vv, start=(d == 0), stop=False)
                        ins = h.matmul(out=o, lhsT=qall.ap[:, hh * 2048 + n * 64:hh * 2048 + n * 64 + 64],
                                       rhs=sloc.ap[:, (pr * 4 + hh) * 128:(pr * 4 + hh + 1) * 128], start=False, stop=False)
                        ins = h.matmul(out=o, lhsT=kall.ap[:, hh * 2048 + n * 64:hh * 2048 + n * 64 + 64],
                                       rhs=SIG.ap[:, (g * 4 + hh) * 128:(g * 4 + hh + 1) * 128], start=False, stop=True)
                return ins
            S.op("pe", fo, reads=[vt, at, qall, kall, sloc, SIGf], writes=obs)

        def emit_epi(tt):
            obs = obs_of(tt)
            if tt % 2 == 0:
                osb, sq, sr = TF.r(3636, 4148), TF.r(4148, 4660), TF.r(4660, 5172)
                ss4, ms4, r4 = sm(C_SS2, 4), sm(C_MS2, 4), sm(C_RS2, 4)
            else:
                osb, sq, sr = TF.r(0, 512), TF.r(512, 1024), TF.r(1024, 1536)
                ss4, ms4, r4 = sm(0, 4), sm(4, 4), sm(8, 4)
            for pr in range(2):
                S.op("act", lambda h, ob=obs[pr], pr=pr: h.activation(out=osb.ap[pr * 64:(pr + 1) * 64, :], in_=ob.ap[pr * 64:(pr + 1) * 64, :], func=AF.Copy),
                     reads=[obs[pr]], writes=[osb])
                S.op("act", lambda h, ob=obs[pr], pr=pr: h.activation(out=sq.ap[pr * 64:(pr + 1) * 64, :], in_=ob.ap[pr * 64:(pr + 1) * 64, :], func=AF.Square),
                     reads=[obs[pr]], writes=[sq])
            S.op("dve", lambda h: h.tensor_reduce(out=ss4.ap, in_=sq.ap.rearrange("p (a c) -> p a c", c=128), axis=AX.X, op=ALU.add),
                 reads=[sq], writes=[ss4])
            S.op("pool", lambda h: h.tensor_scalar(out=ms4.ap, in0=ss4.ap, scalar1=1.0 / 128, scalar2=EPS, op0=ALU.mult, op1=ALU.add),
                 reads=[ss4], writes=[ms4])
            S.op("pool", lambda h: h.tensor_tensor(out=r4.ap, in0=ms4.ap, in1=nh4.ap, op=ALU.pow), reads=[ms4, nh4], writes=[r4])
            rt = UA.r(GR0 + tt * 512, GR0 + tt * 512 + 512)
            S.op("act", lambda h: h.activation(out=sr.ap, in_=rt.ap, func=AF.Silu), reads=[rt], writes=[sr])
            S.op("pool", lambda h: h.tensor_tensor(out=sr.ap, in0=sr.ap, in1=gnb.ap, op=ALU.mult), reads=[sr, gnb], writes=[sr])
            S.op("dve", lambda h: h.tensor_tensor(out=osb.ap.rearrange("p (a c) -> p a c", c=128),
                                                  in0=osb.ap.rearrange("p (a c) -> p a c", c=128),
                                                  in1=r4.ap.unsqueeze(2).to_broadcast([128, 4, 128]), op=ALU.mult),
                 reads=[osb, r4], writes=[osb])
            yg = Y.r(tt * 512, tt * 512 + 512)
            S.op("dve", lambda h: h.tensor_tensor(out=yg.ap, in0=osb.ap, in1=sr.ap, op=ALU.mult), reads=[osb, sr], writes=[yg])
            if stage == 3:
                S.op("sp", lambda h: h.dma_start(out=dbg_d.ap()[tt * 128:(tt + 1) * 128, :], in_=yg.ap),
                     reads=[yg], writes=[dkey("dbg", tt)], dma=True)

        emit_fo(0)
        for tt in range(16):
            if tt + 1 < 16:
                emit_fo(tt + 1)
            emit_epi(tt)
        if stage == 3:
            S.op("sp", lambda h: h.nop(), reads=[dkey("dbg", t) for t in range(16)] + [dkey("yna", t) for t in range(16)])
            S.emit(block, esem, dsem)
            return nc

        WO0, YT0 = 0, 8192
        WF1_0, UT0 = 0, 8192
        for q in range(2):
            dst = UB.r(WO0 + q * 4096, WO0 + (q + 1) * 4096)
            S.op("sp", lambda h, dst=dst, q=q: h.dma_start(
                out=dst.ap.rearrange("p (c n) -> p c n", n=1024),
                in_=woutb_d.ap()[q * 512:(q + 1) * 512, :].rearrange("(c p) n -> p c n", p=128)),
                reads=[dkey("woutb")], writes=[dst], dma=True)
        for q in range(8):
            dst = UA.r(q * 4096, (q + 1) * 4096)
            S.op("sp", lambda h, dst=dst, q=q: h.dma_start(
                out=dst.ap.rearrange("p (c n) -> p c n", n=1024),
                in_=wff2b_d.ap()[q * 512:(q + 1) * 512, :].rearrange("(c p) n -> p c n", p=128)),
                reads=[dkey("wff2b", q // 2)], writes=[dst], dma=True)
        load_const(G.r(), gff_d.ap().to_broadcast([128, 1024]))
        gfin = GF.r()
        load_const(gfin, gfin_d.ap().to_broadcast([128, 1024]))
        gff = G.r()
        wo = UB.r(WO0, WO0 + 8192)
        wf2 = UA.r(0, 32768)
        gjunk = XT.r(5120, 6144)

        def h1_of(grp, jt):
            sl = (grp % 2) * 2 + jt
            return TF.r(1024 + sl * 1024, 2048 + sl * 1024)

        def n2_of(grp):
            return XT.r((grp % 2) * 2048, (grp % 2) * 2048 + 2048)

        def pro_a(grp, jt):
            tt = grp * 2 + jt
            ynas = Y.r(8192 + jt * 512, 8192 + jt * 512 + 512)
            S.op("sp", lambda h, ynas=ynas, tt=tt: h.dma_start(out=ynas.ap, in_=yna_d.ap()[tt * 128:(tt + 1) * 128, :]),
                 reads=[dkey("yna", tt)], writes=[ynas], dma=True)
            xt = TF.r(0, 1024)
            S.op("sp", lambda h, xt=xt, tt=tt: h.dma_start(out=xt.ap, in_=x_d.ap()[(tt + 2) * 128:(tt + 3) * 128, :]),
                 writes=[xt], dma=True)
            yg = Y.r(tt * 512, tt * 512 + 512)
            tb = TBANK
            idn = IDN.r()

            def fty(h, ynas=ynas, yg=yg, tb=tb, idn=idn):
                ins = None
                for k in range(8):
                    src = ynas.ap[:, k * 128:(k + 1) * 128] if k < 4 else yg.ap[:, (k - 4) * 128:(k - 3) * 128]
                    ins = h.transpose(out=tb.ap.bitcast(BF16)[:, k * 128:(k + 1) * 128], in_=src, identity=idn.ap)
                return ins
            S.op("pe", fty, reads=[ynas, yg, idn], writes=[tb])
            yT = UB.r(YT0, YT0 + 1024)
            S.op("act", lambda h, yT=yT, tb=tb: h.activation(out=yT.ap, in_=tb.ap.bitcast(BF16), func=AF.Copy), reads=[tb], writes=[yT])
            h1 = h1_of(grp, jt)
            for cg in range(2):
                b = nbank()
                mm_acc(b, [(yT.ap[:, k * 128:(k + 1) * 128], wo.ap[:, k * 1024 + cg * 512:k * 1024 + cg * 512 + 512]) for k in range(8)],
                       [yT, wo])
                S.op("dve", lambda h, h1=h1, b=b, xt=xt, cg=cg: h.tensor_tensor(out=h1.ap[:, cg * 512:(cg + 1) * 512], in0=b.ap,
                                                                               in1=xt.ap[:, cg * 512:(cg + 1) * 512], op=ALU.add),
                     reads=[b, xt], writes=[h1])
            rs = norm_stats(h1, C_RS2 + jt, junk=gjunk, k=jt)
            xn = XN.r((jt % 2) * 512, (jt % 2) * 512 + 512) if False else XN2[jt]
            scale_to_bf16(h1, rs, gff, xn)

        def pro_b(grp, jt):
            xn = XN2[jt]
            n2slot = n2_of(grp).v(lambda a, jt=jt: a.rearrange("p (k t) -> p k t", k=8)[:, :, jt * 128:(jt + 1) * 128])
            transpose8(xn, n2slot)

        pro_a(0, 0)
        pro_b(0, 0)
        pro_a(0, 1)
        pro_b(0, 1)
        wf1_ctr = 0

        def load_wf1(grp, fb):
            nonlocal_ctr = wf1_state[0]
            wb = UC.r(WF1_0 + (nonlocal_ctr % 2) * 4096, WF1_0 + (nonlocal_ctr % 2) * 4096 + 4096)
            wf1_state[0] += 1
            S.op("sp", lambda h, wb=wb, fb=fb: h.dma_start(
                out=wb.ap.rearrange("p (c n) -> p c n", n=512),
                in_=wff1b_d.ap()[:, fb * 512:(fb + 1) * 512].rearrange("(c p) n -> p c n", p=128)),
                reads=[dkey("wff1b", fb // 2)], writes=[wb], dma=True)
            return wb
        wf1_state = [0]
        pending = load_wf1(0, 0)
        for grp in range(8):
            n2T = n2_of(grp)
            for fb in range(8):
                wb = pending
                if fb < 7:
                    pending = load_wf1(grp, fb + 1)
                elif grp < 7:
                    pending = load_wf1(grp + 1, 0)
                for fl in range(4):
                    f = fb * 4 + fl
                    b = nbank()
                    o = Reg(b.ap[:, 0:256], b.keys)
                    mm_acc(o, [(wb.ap[:, k * 512 + fl * 128:k * 512 + fl * 128 + 128], n2T.ap[:, k * 256:(k + 1) * 256]) for k in range(8)],
                           [wb, n2T])
                    rl = XT.r(4096 + (f % 2) * 256, 4096 + (f % 2) * 256 + 256)
                    S.op("act", lambda h, rl=rl, o=o: h.activation(out=rl.ap, in_=o.ap, func=AF.Relu), reads=[o], writes=[rl])
                    ut = UC.r(UT0 + f * 256, UT0 + f * 256 + 256)
                    S.op("dve", lambda h, ut=ut, o=o, rl=rl: h.scalar_tensor_tensor(out=ut.ap, in0=o.ap, scalar=0.0, in1=rl.ap,
                                                                                   op0=ALU.max, op1=ALU.mult), reads=[o, rl], writes=[ut])
                if grp < 7:
                    if fb == 0:
                        pro_a(grp + 1, 0)
                    elif fb == 2:
                        pro_b(grp + 1, 0)
                    elif fb == 3:
                        pro_a(grp + 1, 1)
                    elif fb == 5:
                        pro_b(grp + 1, 1)
            utall = UC.r(UT0, UT0 + 8192)
            for jt in range(2):
                tt = grp * 2 + jt
                h1 = h1_of(grp, jt)
                hf = TF.r(5120, 6144)
                for cg in range(2):
                    b = nbank()
                    mm_acc(b, [(utall.ap[:, f * 256 + jt * 128:f * 256 + jt * 128 + 128], wf2.ap[:, f * 1024 + cg * 512:f * 1024 + cg * 512 + 512])
                               for f in range(32)], [utall, wf2])
                    S.op("dve", lambda h, hf=hf, b=b, h1=h1, cg=cg: h.tensor_tensor(out=hf.ap[:, cg * 512:(cg + 1) * 512], in0=b.ap,
                                                                                   in1=h1.ap[:, cg * 512:(cg + 1) * 512], op=ALU.add),
                         reads=[b, h1], writes=[hf])
                rs = norm_stats(hf, C_RS2 + 2 + jt, junk=gjunk, k=2 + jt)
                ho = h1
                S.op("dve", lambda h, hf=hf, ho=ho, rs=rs: h.scalar_tensor_tensor(out=ho.ap, in0=hf.ap, scalar=rs.ap, in1=gfin.ap,
                                                                                  op0=ALU.mult, op1=ALU.mult), reads=[hf, rs, gfin], writes=[ho])
                S.op("pool", lambda h, ho=ho, tt=tt: h.dma_start(out=out_d.ap()[tt * 128:(tt + 1) * 128, :], in_=ho.ap),
                     reads=[ho], writes=[dkey("out", tt)], dma=True)
        S.op("sp", lambda h: h.nop(), reads=[dkey("out", t) for t in range(16)])
        S.emit(block, esem, dsem)
    return nc


NEG = -30000.0


def _bias_tables(rpb, seg):
    H = 8
    kc = np.arange(64)
    qc = np.arange(64)
    cs = np.clip(qc - 8, 0, 48)
    inwin = (kc[:, None] >= cs[None, :]) & (kc[:, None] < cs[None, :] + 16)
    dcidx = np.clip(kc[:, None] - qc[None, :], -15, 15) + 15

    def table(i, tiles_j, width):
        out = np.full((H, 128, width), NEG, np.float32)
        for jj, j in enumerate(tiles_j):
            for kp in range(2):
                kl = 2 * j + kp
                kg = 32 * seg - 4 + kl
                if kg < 0 or kg > 127:
                    continue
                for qp in range(2):
                    ql = 2 * i + qp
                    qg = 32 * seg - 4 + ql
                    rs = min(max(qg - 4, 0), 120)
                    if not (rs <= kg < rs + 8):
                        continue
                    dr = kg - qg + 7
                    vals = np.where(inwin[None], rpb[:, dr][:, dcidx], NEG)
                    out[:, kp * 64:(kp + 1) * 64, jj * 128 + qp * 64:jj * 128 + (qp + 1) * 64] = vals
        return out

    def table_main():
        out = np.full((H, 128, 640), NEG, np.float32)
        for jj in range(5):
            for kp in range(2):
                for qp in range(2):
                    drel = 2 * (jj - 2) + kp - qp
                    if not (-4 <= drel <= 3):
                        continue
                    vals = np.where(inwin[None], rpb[:, drel + 7][:, dcidx], NEG)
                    out[:, kp * 64:(kp + 1) * 64, jj * 128 + qp * 64:jj * 128 + (qp + 1) * 64] = vals
        return out

    main = table_main().transpose(1, 0, 2).reshape(128, 8 * 640)
    edges = []
    for i, tj in ((2, range(0, 6)), (3, range(1, 6)), (16, range(14, 19)), (17, range(14, 20))):
        edges.append(table(i, list(tj), 768).transpose(1, 0, 2).reshape(128, 8 * 768))
    return np.ascontiguousarray(main), np.ascontiguousarray(np.stack(edges))


def _prep(inputs):
    x = np.asarray(inputs["x"], np.float32)
    w_in = np.asarray(inputs["w_in"], np.float32)[0]
    rpb = np.asarray(inputs["na_rpb"], np.float32)[0]
    guf = np.asarray(inputs["gla_gate_up_fwd"], np.float32)[0]
    gub = np.asarray(inputs["gla_gate_up_bwd"], np.float32)[0]
    gbf = np.asarray(inputs["gla_gate_bias_fwd"], np.float32)[0]
    gbb = np.asarray(inputs["gla_gate_bias_bwd"], np.float32)[0]
    w_na = np.ascontiguousarray(w_in[:, 0:1536])
    qg, kg = w_in[:, 1536:1792], w_in[:, 1792:2048]
    qd = np.concatenate([np.concatenate([qg[:, h * 64:(h + 1) * 64]] * 2, axis=1) for h in range(4)], axis=1)
    kd = np.concatenate([np.concatenate([kg[:, h * 64:(h + 1) * 64]] * 2, axis=1) for h in range(4)], axis=1)
    w_g1 = np.ascontiguousarray(np.concatenate([qd, kd, w_in[:, 2048:2560]], axis=1))
    w_g2 = np.ascontiguousarray(np.concatenate([w_in[:, 2560:3072], w_in[:, 3072:3104]], axis=1))
    w_gate = np.zeros((32, 512), np.float32)
    gb = np.zeros((128, 4), np.float32)
    for h in range(4):
        w_gate[0:16, h * 128:h * 128 + 64] = guf[:, h * 64:(h + 1) * 64]
        w_gate[16:32, h * 128 + 64:h * 128 + 128] = gub[:, h * 64:(h + 1) * 64]
        gb[0:64, h] = gbf[h * 64:(h + 1) * 64]
        gb[64:128, h] = gbb[h * 64:(h + 1) * 64]
    j = np.arange(64)[:, None]
    i = np.arange(64)[None, :]
    m1 = np.concatenate([(j <= i), (j > i)], axis=1).astype(np.float32)
    mall = np.concatenate([m1, m1], axis=0)
    rmask = np.ones((128, 512), np.float32)
    rmask[:, 0::64] = 0.0
    rmask8 = np.ones((128, 128), np.float32)
    rmask8[:, 0::8] = 0.0
    ident = np.eye(128, dtype=np.float32)
    gn = np.tile(np.asarray(inputs["gla_norm_g"], np.float32)[0], 4)[None, :]
    common = dict(
        w_na=w_na, w_g1=w_g1, w_g2=w_g2,
        w_out=np.ascontiguousarray(np.asarray(inputs["w_out"], np.float32)[0]),
        w_ff1=np.ascontiguousarray(np.asarray(inputs["w_ff1"], np.float32)[0]),
        w_ff2=np.ascontiguousarray(np.asarray(inputs["w_ff2"], np.float32)[0]),
        g_mix=np.asarray(inputs["ln_mix_g"], np.float32).reshape(1, 1024),
        g_ff=np.asarray(inputs["ln_ff_g"], np.float32).reshape(1, 1024),
        g_fin=np.asarray(inputs["ln_final_g"], np.float32).reshape(1, 1024),
        g_n=np.ascontiguousarray(gn), w_gate=w_gate, gb=gb, mall=mall, rmask=rmask, rmask8=rmask8, ident=ident,
    )
    maps = []
    for c in range(NCORES):
        b, seg = c // 4, c % 4
        xc = np.zeros((2560, 1024), np.float32)
        r0 = 32 * seg - 4
        lo, hi = max(r0, 0), min(r0 + 40, 128)
        xc[(lo - r0) * 64:(hi - r0) * 64] = x[b, lo * 64:hi * 64]
        bm, be = _bias_tables(rpb, seg)
        sel = np.zeros((128, 8), np.float32)
        others = [o for o in range(4) if o != seg]
        for u, o in enumerate(others):
            if o < seg:
                sel[0:64, u] = 1.0
            if o > seg:
                sel[64:128, u] = 1.0
        xo = np.ascontiguousarray(np.concatenate([x[b, o * 2048:(o + 1) * 2048] for o in others], axis=0))
        m = dict(common)
        m.update(x=xc, x_oth=xo, b_main=bm, b_edge=be, sel=sel)
        maps.append(m)
    return maps


_NC_CACHE = {}


def kernel(**inputs):
    maps = _prep(inputs)
    if 4 not in _NC_CACHE:
        _NC_CACHE[4] = build(4)
    res = run_bass_kernel_spmd(_NC_CACHE[4], maps, core_ids=list(range(NCORES)))
    out = np.zeros((2, 8192, 1024), np.float32)
    for c in range(NCORES):
        b, seg = c // 4, c % 4
        out[b, seg * 2048:(seg + 1) * 2048] = np.asarray(res.results[c]["out"], np.float32)
    return out
```
